# Optimizing a Trainium2 kernel written in Bass

```python
import jax, jax.numpy as jnp
from jax import lax
import numpy as np

D_MODEL = 1024
BATCH = 8
SEQ = 4096
DEPTH = 4
DEC_BATCH = 32
DEC_SEQ = 32
PAST_LEN = 4096

CHUNK = 64
N_MIXERS = 3
EXPAND = 2
D_INNER = EXPAND * D_MODEL
POOL_WINDOWS = (2, 4, 8, 16)
N_POOL_GROUPS = len(POOL_WINDOWS)
POOL_GROUP = D_INNER // N_POOL_GROUPS
POOL_HIST = max(POOL_WINDOWS) - 1
CONV_WIDTH = 31
CONV_HIST = CONV_WIDTH - 1
SGU_CHUNK = 128
SGU_HEADS = 4
SGU_HEAD_DIM = D_INNER // SGU_HEADS
RMS_EPS = 1e-6
LN_EPS = 1e-5
N_POOL = (DEPTH + 2) // 3
N_CONV = (DEPTH + 1) // 3
N_SGU = DEPTH // 3

kernel_name = 'hybrid_pool_conv_sgu_streaming_step'


def rmsnorm(x, g):
    xf = x.astype(jnp.float32)
    y = xf * lax.rsqrt(jnp.mean(xf * xf, axis=-1, keepdims=True) + RMS_EPS)
    return (y * g.astype(jnp.float32)).astype(x.dtype)


def layernorm(x, g, b):
    xf = x.astype(jnp.float32)
    mu = jnp.mean(xf, axis=-1, keepdims=True)
    var = jnp.mean(jnp.square(xf - mu), axis=-1, keepdims=True)
    y = (xf - mu) * lax.rsqrt(var + LN_EPS) * g.astype(jnp.float32) + b.astype(jnp.float32)
    return y.astype(x.dtype)


def pool_mixer(u, hist, offset, w_grp, scale):
    B, T, E = u.shape
    h = jnp.concatenate([hist.astype(u.dtype), u], axis=1).astype(jnp.float32)
    cs = jnp.concatenate([jnp.zeros_like(h[:, :1]), jnp.cumsum(h, axis=1)], axis=1)
    pos = offset + jnp.arange(T)
    means = []
    for g, w in enumerate(POOL_WINDOWS):
        c0, c1 = g * POOL_GROUP, (g + 1) * POOL_GROUP
        s = cs[:, POOL_HIST + 1:, c0:c1] - cs[:, POOL_HIST + 1 - w:POOL_HIST + 1 - w + T, c0:c1]
        cnt = jnp.minimum(pos + 1, w).astype(jnp.float32)[None, :, None]
        means.append(s / cnt)
    d = (jnp.concatenate(means, axis=-1) - u.astype(jnp.float32)).astype(u.dtype)
    d = d.reshape(B, T, N_POOL_GROUPS, POOL_GROUP)
    y = jnp.einsum('btgc,gcd->btgd', d, w_grp).reshape(B, T, E) * scale
    return y, h[:, -POOL_HIST:].astype(u.dtype)


def conv_mixer(a, hist, w, b, ng, nb):
    h = jnp.concatenate([hist.astype(a.dtype), a], axis=1)
    y = lax.conv_general_dilated(h, w[:, None, :], window_strides=(1,), padding='VALID',
                                 dimension_numbers=('NWC', 'WIO', 'NWC'),
                                 feature_group_count=D_INNER) + b
    y = jax.nn.silu(layernorm(y, ng, nb))
    return y, h[:, -CONV_HIST:]


def sgu_mixer(u, v, ws, bs, ng, nb):
    B, T, E = u.shape
    L = min(T, SGU_CHUNK)
    n = T // L
    vn = layernorm(v, ng, nb)
    mask = jnp.tril(jnp.ones((L, L), dtype=bool))
    wl = ws[:, :L, :L]
    wm = jnp.where(mask[None], wl, jnp.zeros_like(wl))
    vc = vn.reshape(B, n, L, SGU_HEADS, SGU_HEAD_DIM)
    s = jnp.einsum('hij,bnjhc->bnihc', wm, vc) + jnp.transpose(bs[:, :L])[:, :, None]
    y = u * s.reshape(B, T, E)
    return y, vn[:, -L:]


def run_trunk(x, pool_hist, conv_hist, offset, norm_g, pool_in_w, pool_w, pool_scale, pool_out_w,
              conv_in_w, conv_w, conv_b, conv_norm_g, conv_norm_b, conv_out_w,
              sgu_in_w, sgu_norm_g, sgu_norm_b, sgu_w, sgu_b, sgu_out_w, final_g):
    new_pool, new_conv, new_sgu = [], [], []
    for i in range(DEPTH):
        kind, j = i % N_MIXERS, i // N_MIXERS
        h = rmsnorm(x, norm_g[i])
        if kind == 0:
            p = h @ pool_in_w[j]
            u, z = p[..., :D_INNER], p[..., D_INNER:]
            y, st = pool_mixer(u, pool_hist[j], offset, pool_w[j], pool_scale[j])
            out = (y * jax.nn.silu(z)) @ pool_out_w[j]
            new_pool.append(st)
        elif kind == 1:
            p = h @ conv_in_w[j]
            a, gl, z = p[..., :D_INNER], p[..., D_INNER:2 * D_INNER], p[..., 2 * D_INNER:]
            y, st = conv_mixer(a * jax.nn.sigmoid(gl), conv_hist[j], conv_w[j], conv_b[j],
                               conv_norm_g[j], conv_norm_b[j])
            out = (y * jax.nn.silu(z)) @ conv_out_w[j]
            new_conv.append(st)
        else:
            p = h @ sgu_in_w[j]
            u, v, z = p[..., :D_INNER], p[..., D_INNER:2 * D_INNER], p[..., 2 * D_INNER:]
            y, st = sgu_mixer(u, v, sgu_w[j], sgu_b[j], sgu_norm_g[j], sgu_norm_b[j])
            out = (y * jax.nn.silu(z)) @ sgu_out_w[j]
            new_sgu.append(st)
        x = x + out
    return rmsnorm(x, final_g), jnp.stack(new_pool), jnp.stack(new_conv), jnp.stack(new_sgu)


def setup_inputs(seed: int = 0) -> dict:
    key = jax.random.key(seed)
    ks = jax.random.split(key, 24)
    nrm = jax.random.normal
    f32 = jnp.float32
    D, E = D_MODEL, D_INNER
    return {
        'x_prompt': nrm(ks[0], (BATCH, SEQ, D), f32),
        'x_sample': nrm(ks[1], (DEC_BATCH, DEC_SEQ, D), f32),
        'state_pool': nrm(ks[2], (N_POOL, DEC_BATCH, POOL_HIST, E), f32),
        'state_conv': 0.5 * nrm(ks[3], (N_CONV, DEC_BATCH, CONV_HIST, E), f32),
        'norm_g': 1.0 + 0.02 * nrm(ks[4], (DEPTH, D), f32),
        'pool_in_w': nrm(ks[5], (N_POOL, D, 2 * E), f32) * D ** -0.5,
        'pool_w': nrm(ks[6], (N_POOL, N_POOL_GROUPS, POOL_GROUP, POOL_GROUP), f32) * POOL_GROUP ** -0.5,
        'pool_scale': 1.0 + 0.02 * nrm(ks[7], (N_POOL, E), f32),
        'pool_out_w': nrm(ks[8], (N_POOL, E, D), f32) * E ** -0.5,
        'conv_in_w': nrm(ks[9], (N_CONV, D, 3 * E), f32) * D ** -0.5,
        'conv_w': nrm(ks[10], (N_CONV, CONV_WIDTH, E), f32) * CONV_WIDTH ** -0.5,
        'conv_b': 0.02 * nrm(ks[11], (N_CONV, E), f32),
        'conv_norm_g': 1.0 + 0.02 * nrm(ks[12], (N_CONV, E), f32),
        'conv_norm_b': 0.02 * nrm(ks[13], (N_CONV, E), f32),
        'conv_out_w': nrm(ks[14], (N_CONV, E, D), f32) * E ** -0.5,
        'sgu_in_w': nrm(ks[15], (N_SGU, D, 3 * E), f32) * D ** -0.5,
        'sgu_norm_g': 1.0 + 0.02 * nrm(ks[16], (N_SGU, E), f32),
        'sgu_norm_b': 0.02 * nrm(ks[17], (N_SGU, E), f32),
        'sgu_w': nrm(ks[18], (N_SGU, SGU_HEADS, SGU_CHUNK, SGU_CHUNK), f32) * SGU_CHUNK ** -0.5,
        'sgu_b': 1.0 + 0.02 * nrm(ks[19], (N_SGU, SGU_HEADS, SGU_CHUNK), f32),
        'sgu_out_w': nrm(ks[20], (N_SGU, E, D), f32) * E ** -0.5,
        'final_g': 1.0 + 0.02 * nrm(ks[21], (D,), f32),
    }


def reference(x_prompt, x_sample, state_pool, state_conv, norm_g, pool_in_w, pool_w, pool_scale, pool_out_w,
              conv_in_w, conv_w, conv_b, conv_norm_g, conv_norm_b, conv_out_w,
              sgu_in_w, sgu_norm_g, sgu_norm_b, sgu_w, sgu_b, sgu_out_w, final_g):
    b_p = x_prompt.shape[0]
    zero_pool = jnp.zeros((N_POOL, b_p, POOL_HIST, D_INNER), x_prompt.dtype)
    zero_conv = jnp.zeros((N_CONV, b_p, CONV_HIST, D_INNER), x_prompt.dtype)
    y_prompt, new_pool_prompt, new_conv_prompt, new_sgu_prompt = run_trunk(
        x_prompt, zero_pool, zero_conv, 0, norm_g, pool_in_w, pool_w, pool_scale, pool_out_w,
        conv_in_w, conv_w, conv_b, conv_norm_g, conv_norm_b, conv_out_w,
        sgu_in_w, sgu_norm_g, sgu_norm_b, sgu_w, sgu_b, sgu_out_w, final_g)
    y_sample, new_pool_sample, new_conv_sample, new_sgu_sample = run_trunk(
        x_sample, state_pool, state_conv, PAST_LEN, norm_g, pool_in_w, pool_w, pool_scale, pool_out_w,
        conv_in_w, conv_w, conv_b, conv_norm_g, conv_norm_b, conv_out_w,
        sgu_in_w, sgu_norm_g, sgu_norm_b, sgu_w, sgu_b, sgu_out_w, final_g)
    return (y_prompt, y_sample, new_pool_prompt, new_pool_sample, new_conv_prompt, new_conv_sample, new_sgu_prompt, new_sgu_sample)
```

```python
import numpy as np
from contextlib import ExitStack
import concourse.bass as bass
import concourse.mybir as mybir
from concourse.bass_utils import run_bass_kernel_spmd

F32 = mybir.dt.float32
BF16 = mybir.dt.bfloat16
ALU = mybir.AluOpType
AF = mybir.ActivationFunctionType

D = 1024
E = 2048
SEQ = 4096
NCORE = 8
NSAMP = 4
SLEN = 32
TP = 512
NG = SEQ // TP
NCH = 120
NSLOT = 9
PF = 6
RMS_EPS = 1e-6
LN_EPS = 1e-5
POOL_WINDOWS = (2, 4, 8, 16)
BIGW = 512 + 30 + 32
SEQW = 512 + 60 + 32

P_NORMG = 0
P_FINALG = P_NORMG + 32
P_PSCALE = P_FINALG + 8
P_CONVW = P_PSCALE + 32
P_CONVB = P_CONVW + 496
P_CONVNG = P_CONVB + 16
P_CONVNB = P_CONVNG + 16
NPAR = P_CONVNB + 16
B_SGG = 0
B_SGB = B_SGG + 2048
B_BSB = B_SGB + 2048
B_MASK = B_BSB + 512
B_RATIO = B_MASK + 128
B_IDENT = B_RATIO + 64
NBC = B_IDENT + 128
PINGPONG = True
NPE = 19


class Op:
    __slots__ = ("eng", "fn", "deps", "mark", "markno", "dkey", "dval", "idx", "tag")


class Sched:
    def __init__(self):
        self.ops = []
        self.last_w = {}
        self.readers = {}
        self.dcount = {}
        self.tag = None

    def add(self, eng, fn, reads=(), writes=(), dkey=None, ndma=1):
        op = Op()
        op.eng, op.fn, op.mark, op.markno, op.dkey, op.dval = eng, fn, False, 0, dkey, 0
        op.idx = len(self.ops)
        op.tag = self.tag
        deps = set()
        for k in reads:
            w = self.last_w.get(k)
            if w is not None:
                deps.add(w)
        for k in writes:
            w = self.last_w.get(k)
            if w is not None:
                deps.add(w)
            for r in self.readers.get(k, ()):
                deps.add(r)
        wset = set(writes)
        for k in writes:
            self.last_w[k] = op
            self.readers[k] = []
        for k in reads:
            if k not in wset:
                self.readers.setdefault(k, []).append(op)
        deps.discard(op)
        op.deps = deps
        if dkey is not None:
            self.dcount[dkey] = self.dcount.get(dkey, 0) + ndma
            op.dval = 16 * self.dcount[dkey]
        self.ops.append(op)
        return op


def build_program(depth=4, ngroups=NG):
    nc = bass.Bass("TRN2", target_bir_lowering=False)
    S = Sched()

    def dram(name, shape, dt, kind):
        return nc.dram_tensor(name, shape, dt, kind=kind).ap()

    xT = dram("xT", [8, 128, SEQ + NSAMP * SLEN], F32, "ExternalInput")
    wst = dram("wst", [NCH, 128, 2048], F32, "ExternalInput")
    params = dram("params", [128, NPAR], F32, "ExternalInput")
    bcin = dram("bcin", [128, NBC], F32, "ExternalInput")
    sgwT = dram("sgwT", [128, 4, 128], F32, "ExternalInput")
    stpool = dram("stpool", [2, NSAMP, 128, 240], F32, "ExternalInput")
    stconv = dram("stconv", [NSAMP, 128, 480], F32, "ExternalInput")
    yT = dram("yT", [8, 128, SEQ + NSAMP * SLEN], F32, "ExternalOutput")
    o_poolp = dram("o_poolp", [2, 128, 240], F32, "ExternalOutput")
    o_pools = dram("o_pools", [2, NSAMP, 128, 240], F32, "ExternalOutput")
    o_convp = dram("o_convp", [128, 480], F32, "ExternalOutput")
    o_convs = dram("o_convs", [NSAMP, 128, 480], F32, "ExternalOutput")
    o_sgup = dram("o_sgup", [128, 2048], F32, "ExternalOutput")
    o_sgus = dram("o_sgus", [NSAMP, SLEN, 2048], F32, "ExternalOutput")
    wq = dram("wq", [NCH, 128, 2048], BF16, "Internal")
    dgq = dram("dgq", [16, 128, NPE * 128], BF16, "Internal")

    es = ExitStack()

    def sb(name, shape, dt):
        return es.enter_context(nc.sbuf_tensor(name, shape, dt))

    ps = es.enter_context(nc.psum_tensor("ps", [128, 4096], F32))
    x = sb("x", [128, 8, 544], F32)
    h = sb("h", [128, 8, 544], BF16)
    ym = sb("ym", [128, 16, 544], BF16)
    big = sb("big", [128, 16, BIGW], F32)
    slots = sb("slots", [128, NSLOT, 2048], BF16)
    stg = sb("stg", [128, 2, 2048], F32)
    par = sb("par", [128, NPAR], F32)
    bc = sb("bc", [128, NBC], F32)
    sgw32 = sb("sgw32", [128, 4, 128], F32)
    wmT = sb("wmT", [128, 4, 128], BF16)
    ones_d = sb("ones_d", [128, 128], BF16)
    ones_e = sb("ones_e", [128, 128], BF16)
    epsb = sb("epsb", [128, 2], F32)
    pool_hp = sb("pool_hp", [128, 2, 16, 15], F32)
    conv_hp = sb("conv_hp", [128, 16, 30], F32)
    st_pool = sb("st_pool", [128, 2, 240], F32)
    st_conv = sb("st_conv", [128, 480], F32)
    pso = sb("pso", [128, 2, 240], F32)
    cso = sb("cso", [128, 480], F32)
    seqb = sb("seqb", [128, 4, SEQW], F32)
    sA = sb("sA", [128, SEQW], F32)
    sB = sb("sB", [128, SEQW], F32)
    dbuf = sb("dbuf", [128, 4, 544], BF16)
    dbuf2 = dbuf[:, :, :].rearrange("p a b -> p (a b)")
    t32 = sb("t32", [128, 2, 544], F32)
    t16 = sb("t16", [128, 2, 544], BF16)
    rs = sb("rs", [128, BIGW], F32)
    vraw = sb("vraw", [128, 2048], F32)
    bst = sb("bst", [128, 2, 24], F32)
    agb = sb("agb", [128, 2, SEQW], BF16)
    ident_bf = sb("ident_bf", [128, 128], BF16)
    ym2d = ym[:, :, :].rearrange("p a b -> p (a b)")
    DGOFF = [0, 5 * 544]
    dg = ym2d
    stg2d = stg[:, :, :].rearrange("p a b -> p (a b)")
    DGK = [[("ym", c) for c in range(0, 5)], [("ym", c) for c in range(5, 10)]]
    assert NPE * 128 <= 5 * 544
    dgs = vraw[:, :].bitcast(BF16)
    vrawB = ym2d[:, 0:4096].bitcast(F32)
    vnbB = ym2d[:, 4096:6144]
    assert tuple(vrawB.shape) == (128, 2048), vrawB.shape
    mv = sb("mv", [128, 2, 4], F32)

    state = {"tile": 0, "loaded": 0, "par": 0, "fine": 0}

    def next_tile(G):
        n = len(G["tiles"])
        t = state["tile"] % n
        state["tile"] = (t + 1) % n
        return G["tiles"][t]

    def psap(tile, bi, n, p0=0, p1=128, off=0):
        b = tile[bi]
        return ps[p0:p1, b * 512 + off: b * 512 + off + n]

    def PK(tile):
        return [("psb", b) for b in tile]

    def gstruct(g):
        even = (g % 2 == 0)
        if even:
            return dict(T=544, segs=[(0, 512), (512, 32)], blocks=[(0, 272), (272, 272)], s=g // 2,
                        tiles=[(0, 1), (2, 3), (4, 5)], aux=(6, 7), xi=0, g=g)
        return dict(T=512, segs=[(0, 512)], blocks=[(0, 512)], s=None,
                    tiles=[(b,) for b in range(7)], aux=(7,), xi=(1 if PINGPONG else 0), g=g)

    def XAP(G, k, c0, n):
        if G["xi"] == 0:
            return x[:, k, c0:c0 + n]
        return stg2d[:, k * 512 + c0:k * 512 + c0 + n]

    def XK(G, k):
        return ("x", k) if G["xi"] == 0 else ("xo", k)

    def pieces(G, c0, n, H):
        out = []
        for si, (s0, sn) in enumerate(G["segs"]):
            lo, hi = max(c0, s0), min(c0 + n, s0 + sn)
            if lo < hi:
                seqoff = H + (0 if si == 0 else 512 + H) + (lo - s0)
                cloff = (0 if si == 0 else 512 + H) + (lo - s0)
                out.append((lo, hi - lo, seqoff, cloff))
        return out

    def issue_load(gi):
        ci = gi % NCH
        grp = gi // NCH
        sl = gi % NSLOT
        if grp == 0:
            st = gi % 2
            S.add("sp", lambda e, ci=ci, st=st: [e.dma_start(out=stg[:, st, :], in_=wst[ci])],
                  reads=[], writes=[("stg", st)], dkey=("stg", st))
            S.add("act", lambda e, st=st, sl=sl: e.activation(out=slots[:, sl, :], in_=stg[:, st, :], func=AF.Copy),
                  reads=[("stg", st)], writes=[("slot", sl)])
            S.add("act", lambda e, ci=ci, sl=sl: [e.dma_start(out=wq[ci], in_=slots[:, sl, :])],
                  reads=[("slot", sl)], writes=[("wq", ci)], dkey=("slot", sl))
        else:
            S.add("sp", lambda e, ci=ci, sl=sl: [e.dma_start(out=slots[:, sl, :], in_=wq[ci])],
                  reads=[("wq", ci)], writes=[("slot", sl)], dkey=("slot", sl))

    total_chunks = ngroups * NCH

    def use_chunk(gi, hold_from=None):
        upto = min(gi + PF, total_chunks - 1)
        if hold_from is not None:
            upto = min(upto, hold_from + NSLOT - 1)
        while state["loaded"] <= upto:
            if (state["loaded"] % NCH) < chunks_per_pass:
                issue_load(state["loaded"])
            state["loaded"] += 1
        return gi % NSLOT

    L0 = 0
    L1 = 28
    L2 = 60
    L3 = 92
    chunks_per_pass = [0, 28, 60, 92, 120][depth]

    def mm_tile(G, tile, nk, lhs_fn, rhs_fn, reads):
        def fn(e):
            inst = None
            for k in range(nk):
                for bi, (c0, n) in enumerate(G["blocks"]):
                    inst = e.matmul(psap(tile, bi, n), lhsT=lhs_fn(k), rhs=rhs_fn(k, c0, n),
                                    start=(k == 0), stop=(k == nk - 1))
            return inst
        S.add("pe", fn, reads=reads, writes=[*PK(tile)])

    def inproj(G, sl, m2, tile):
        if state["fine"] > 0:
            state["fine"] -= 1
            for k in range(8):
                def fn(e, k=k):
                    inst = None
                    for bi, (c0, n) in enumerate(G["blocks"]):
                        inst = e.matmul(psap(tile, bi, n), lhsT=slots[:, sl, k * 256 + m2 * 128: k * 256 + m2 * 128 + 128],
                                        rhs=h[:, k, c0:c0 + n], start=(k == 0), stop=(k == 7))
                    return inst
                S.add("pe", fn, reads=[("slot", sl), ("h", k)], writes=[*PK(tile)])
            return
        mm_tile(G, tile, 8,
                lambda k: slots[:, sl, k * 256 + m2 * 128: k * 256 + m2 * 128 + 128],
                lambda k, c0, n: h[:, k, c0:c0 + n],
                reads=[("slot", sl)] + [("h", k) for k in range(8)])

    def blockwise(eng, G, fn_block, reads, writes):
        def fn(e):
            inst = None
            for bi, (c0, n) in enumerate(G["blocks"]):
                inst = fn_block(e, bi, c0, n)
            return inst
        S.add(eng, fn, reads=reads, writes=writes)

    def rmsnorm(G, gcol0, out_fn, out_keys, sq=None, sqk="h"):
        T = G["T"]
        if sq is None:
            sq = h
        for k in range(8):
            S.add("act", lambda e, k=k: e.activation(out=sq[:, k, 0:T], in_=XAP(G, k, 0, T), func=AF.Square),
                  reads=[XK(G, k)], writes=[(sqk, k)])
        mm_tile(G, G["aux"], 8, lambda k: ones_d[:, :], lambda k, c0, n: sq[:, k, c0:c0 + n],
                reads=[(sqk, k) for k in range(8)])
        blockwise("act", G, lambda e, bi, c0, n: e.activation(out=rs[:, c0:c0 + n], in_=psap(G["aux"], bi, n),
                                                              func=AF.Sqrt, bias=epsb[:, 0:1], scale=1.0),
                  reads=[*PK(G["aux"])], writes=[("rs",)])
        S.add("dve", lambda e: e.reciprocal(out=rs[:, 0:T], in_=rs[:, 0:T]), reads=[("rs",)], writes=[("rs",)])
        for k in range(8):
            S.add("dve", lambda e, k=k: e.scalar_tensor_tensor(out=out_fn(k), in0=XAP(G, k, 0, T),
                                                               scalar=par[:, gcol0 + k: gcol0 + k + 1],
                                                               in1=rs[:, 0:T], op0=ALU.mult, op1=ALU.mult),
                  reads=[XK(G, k), ("rs",)], writes=[out_keys(k)])

    def outproj(G, gbase, c_first):
        for m in range(8):
            sl = use_chunk(gbase + c_first + m)
            tile = next_tile(G)
            mm_tile(G, tile, 16, lambda k, sl=sl: slots[:, sl, k * 128:(k + 1) * 128],
                    lambda k, c0, n: ym[:, k, c0:c0 + n],
                    reads=[("slot", sl)] + [("ym", k) for k in range(16)])
            blockwise("dve", G, lambda e, bi, c0, n, m=m, tile=tile: e.tensor_tensor(
                out=XAP(G, m, c0, n), in0=psap(tile, bi, n), in1=XAP(G, m, c0, n), op=ALU.add),
                reads=[*PK(tile), XK(G, m)], writes=[XK(G, m)])

    def gen_dg(ct):
        for k in range(NPE):
            wcol = par[:, P_CONVW + ct * 31 + k:P_CONVW + ct * 31 + k + 1]
            if k % 2 == 0:
                S.add("act", lambda e, k=k, wcol=wcol: e.activation(
                    out=dgs[:, k * 128:(k + 1) * 128], in_=ident_bf[:, :], func=AF.Copy, scale=wcol),
                    reads=[("ident",)], writes=[("dgs", k)])
            else:
                S.add("pool", lambda e, k=k, wcol=wcol: e.tensor_scalar(
                    out=dgs[:, k * 128:(k + 1) * 128], in0=ident_bf[:, :], scalar1=wcol, scalar2=1.0,
                    op0=ALU.mult, op1=ALU.mult),
                    reads=[("ident",)], writes=[("dgs", k)])
        S.add("act", lambda e, ct=ct: [e.dma_start(out=dgq[ct], in_=dgs[:, 0:NPE * 128])],
              reads=[("dgs", k) for k in range(NPE)], writes=[("dgq", ct)], dkey=("dgs",))

    def pool_layer(G, g, gbase, cbase, j, l, skip_norm=False, pre_out=None):
        H = 15
        T = G["T"]
        W = H + 512 + ((H + 32) if G["s"] is not None else 0)
        offS = H + 512 + H
        if not skip_norm:
            rmsnorm(G, P_NORMG + 8 * l, lambda k: h[:, k, 0:T], lambda k: ("h", k))
            state["fine"] = 2
        for gq in range(4):
            w = POOL_WINDOWS[gq]
            cu = [cbase + gq * 5 + 0, cbase + gq * 5 + 1]
            cz = [cbase + gq * 5 + 2, cbase + gq * 5 + 3]
            cg = cbase + gq * 5 + 4
            for m in range(4):
                ct = 4 * gq + m
                sl = use_chunk(gbase + cu[m // 2])
                tile = next_tile(G)
                inproj(G, sl, m % 2, tile)
                p = state["par"]
                state["par"] ^= 1
                ub = seqb[:, p, :]
                S.add("pool", lambda e, p=p, ct=ct: e.tensor_copy(out=seqb[:, p, 0:H], in_=pool_hp[:, j, ct, :]),
                      reads=[("pool_hp", j, ct)], writes=[("seqb", p)])
                if G["s"] is not None:
                    S.add("pool", lambda e, p=p, ct=ct: e.tensor_copy(out=seqb[:, p, H + 512:H + 512 + H],
                                                                      in_=st_pool[:, j, ct * 15:(ct + 1) * 15]),
                          reads=[("st_pool",)], writes=[("seqb", p)])

                def evac(e, p=p, tile=tile):
                    inst = None
                    for bi, (c0, n) in enumerate(G["blocks"]):
                        for (lo, ln, so, co) in pieces(G, c0, n, H):
                            inst = e.activation(out=seqb[:, p, so:so + ln], in_=psap(tile, bi, ln, off=lo - c0),
                                                func=AF.Copy)
                    return inst
                S.add("act", evac, reads=[*PK(tile)], writes=[("seqb", p)])
                S.add("pool", lambda e, p=p, ct=ct: e.tensor_copy(out=pool_hp[:, j, ct, :], in_=seqb[:, p, 512:512 + H]),
                      reads=[("seqb", p)], writes=[("pool_hp", j, ct)])
                if G["s"] is not None:
                    S.add("pool", lambda e, p=p, ct=ct: e.tensor_copy(out=pso[:, j, ct * 15:(ct + 1) * 15],
                                                                      in_=seqb[:, p, offS + 17:offS + 32]),
                          reads=[("seqb", p)], writes=[("pso", j)])
                S.add("dve", lambda e, p=p: e.tensor_tensor(out=sA[:, 1:W], in0=seqb[:, p, 1:W], in1=seqb[:, p, 0:W - 1],
                                                            op=ALU.add),
                      reads=[("seqb", p)], writes=[("sA",)])
                res, rk = sA, ("sA",)
                if w >= 4:
                    S.add("dve", lambda e: e.tensor_tensor(out=sB[:, 3:W], in0=sA[:, 3:W], in1=sA[:, 1:W - 2], op=ALU.add),
                          reads=[("sA",)], writes=[("sB",)])
                    res, rk = sB, ("sB",)
                if w >= 8:
                    S.add("dve", lambda e: e.tensor_tensor(out=sA[:, 7:W], in0=sB[:, 7:W], in1=sB[:, 3:W - 4], op=ALU.add),
                          reads=[("sB",)], writes=[("sA",)])
                    res, rk = sA, ("sA",)
                if w >= 16:
                    S.add("dve", lambda e: e.tensor_tensor(out=sB[:, 15:W], in0=sA[:, 15:W], in1=sA[:, 7:W - 8], op=ALU.add),
                          reads=[("sA",)], writes=[("sB",)])
                    res, rk = sB, ("sB",)
                if g == 0:
                    S.add("dve", lambda e, res=res, gq=gq: e.tensor_tensor(
                        out=res[:, H:H + 15], in0=res[:, H:H + 15],
                        in1=bc[:, B_RATIO + gq * 16:B_RATIO + gq * 16 + 15], op=ALU.mult),
                        reads=[rk], writes=[rk])

                def dfn(e, p=p, res=res, m=m, w=w):
                    inst = None
                    for (lo, ln, so, co) in pieces(G, 0, T, H):
                        inst = e.scalar_tensor_tensor(out=dbuf[:, m, lo:lo + ln], in0=res[:, so:so + ln],
                                                      scalar=1.0 / w, in1=seqb[:, p, so:so + ln],
                                                      op0=ALU.mult, op1=ALU.subtract)
                    return inst
                S.add("dve", dfn, reads=[rk, ("seqb", p)], writes=[("d", m)])
            if g == 0 and l == 0:
                for m in range(4):
                    gen_dg(4 * gq + m)
            for m in range(4):
                ct = 4 * gq + m
                sl = use_chunk(gbase + cz[m // 2])
                tile = next_tile(G)
                inproj(G, sl, m % 2, tile)
                blockwise("act", G, lambda e, bi, c0, n, ct=ct, tile=tile: e.activation(
                    out=big[:, ct, c0:c0 + n], in_=psap(tile, bi, n), func=AF.Silu),
                    reads=[*PK(tile)], writes=[("big", ct)])
            slg = use_chunk(gbase + cg)
            for m in range(4):
                ct = 4 * gq + m
                tile = next_tile(G)
                mm_tile(G, tile, 4, lambda k, m=m, slg=slg: slots[:, slg, k * 512 + m * 128:k * 512 + m * 128 + 128],
                        lambda k, c0, n: dbuf[:, k, c0:c0 + n],
                        reads=[("slot", slg)] + [("d", k) for k in range(4)])
                blockwise("dve", G, lambda e, bi, c0, n, ct=ct, tile=tile: e.scalar_tensor_tensor(
                    out=ym[:, ct, c0:c0 + n], in0=psap(tile, bi, n),
                    scalar=par[:, P_PSCALE + j * 16 + ct:P_PSCALE + j * 16 + ct + 1],
                    in1=big[:, ct, c0:c0 + n], op0=ALU.mult, op1=ALU.mult),
                    reads=[*PK(tile), ("big", ct)], writes=[("ym", ct)])
        if G["s"] is not None:
            s = G["s"]
            S.add("sp", lambda e, s=s: [e.dma_start(out=o_pools[j, s], in_=pso[:, j, :])],
                  reads=[("pso", j)], writes=[("o_pools", j, s)], dkey=("pso", j))
        S.tag = (g, S.tag[1][:2] + 'o')
        if pre_out is not None:
            pre_out()
        outproj(G, gbase, cbase + 20)

    def conv_layer(G, g, gbase, cbase, l):
        H = 30
        T = G["T"]
        W = H + 512 + ((H + 32) if G["s"] is not None else 0)
        Wc = W - H
        offS = H + 512 + H
        rmsnorm(G, P_NORMG + 8 * l, lambda k: h[:, k, 0:T], lambda k: ("h", k))
        state["fine"] = 2
        cblocks = [(0, 287), (287, 287)] if G["s"] is not None else [(0, 512)]
        halves = [(0, Wc // 2), (Wc // 2, Wc - Wc // 2)]
        fr = {}
        tile_a = {}

        def front1(ct):
            pr, m2 = ct // 2, ct % 2
            sla = use_chunk(gbase + cbase + 2 * pr)
            slg = use_chunk(gbase + cbase + 2 * pr + 1)
            if len(G["tiles"]) == 3:
                tA = G["tiles"][0 if ct % 2 == 0 else 2]
                tG = G["tiles"][1]
            else:
                tA = next_tile(G)
                tG = next_tile(G)
            inproj(G, sla, m2, tA)
            inproj(G, slg, m2, tG)
            tile_a[ct] = tA
            p = ct % 2
            q = ct % 4
            S.add("sp", lambda e, p=p, ct=ct: [e.dma_start(out=dg[:, DGOFF[p]:DGOFF[p] + NPE * 128], in_=dgq[ct])],
                  reads=[("dgq", ct)], writes=DGK[p], dkey=("dg", p))
            blockwise("act", G, lambda e, bi, c0, n, p=p, tG=tG: e.activation(
                out=t32[:, p, c0:c0 + n], in_=psap(tG, bi, n), func=AF.Sigmoid),
                reads=[*PK(tG)], writes=[("t32", p)])
            S.add("pool", lambda e, q=q, ct=ct: e.tensor_copy(out=seqb[:, q, 0:H], in_=conv_hp[:, ct, :]),
                  reads=[("conv_hp", ct)], writes=[("seqb", q)])
            if G["s"] is not None:
                S.add("pool", lambda e, q=q, ct=ct: e.tensor_copy(out=seqb[:, q, H + 512:H + 512 + H],
                                                                  in_=st_conv[:, ct * 30:(ct + 1) * 30]),
                      reads=[("st_conv",)], writes=[("seqb", q)])

            def glu(e, p=p, q=q, tA=tA):
                inst = None
                for bi, (c0, n) in enumerate(G["blocks"]):
                    for (lo, ln, so, co) in pieces(G, c0, n, H):
                        inst = e.tensor_tensor(out=seqb[:, q, so:so + ln], in0=psap(tA, bi, ln, off=lo - c0),
                                               in1=t32[:, p, lo:lo + ln], op=ALU.mult)
                return inst
            S.add("dve", glu, reads=[*PK(tA), ("t32", p)], writes=[("seqb", q)])

        def front2(ct):
            p = ct % 2
            q = ct % 4
            S.add("act", lambda e, p=p, q=q: e.activation(out=agb[:, p, 0:W], in_=seqb[:, q, 0:W], func=AF.Copy),
                  reads=[("seqb", q)], writes=[("agb", p)])
            S.add("pool", lambda e, q=q, ct=ct: e.tensor_copy(out=conv_hp[:, ct, :], in_=seqb[:, q, 512:512 + H]),
                  reads=[("seqb", q)], writes=[("conv_hp", ct)])
            if G["s"] is not None:
                S.add("pool", lambda e, q=q, ct=ct: e.tensor_copy(out=cso[:, ct * 30:(ct + 1) * 30],
                                                                  in_=seqb[:, q, offS + 2:offS + 32]),
                      reads=[("seqb", q)], writes=[("cso",)])

        def front(ct):
            front1(ct)
            front2(ct)

        acc_tile = {}

        def pe_taps(ct):
            p = ct % 2
            tAcc = tile_a[ct] if len(G["tiles"]) == 3 else next_tile(G)
            acc_tile[ct] = tAcc

            def pfn(e, p=p, tAcc=tAcc):
                inst = None
                for k in range(NPE):
                    for bi, (c0, n) in enumerate(cblocks):
                        inst = e.matmul(psap(tAcc, bi, n), lhsT=dg[:, DGOFF[p] + k * 128:DGOFF[p] + (k + 1) * 128],
                                        rhs=agb[:, p, k + c0:k + c0 + n], start=(k == 0), stop=(k == NPE - 1))
                return inst
            S.add("pe", pfn, reads=[("agb", p)] + DGK[p], writes=[*PK(tAcc)])

        def f0(ct):
            q = ct % 4
            tAcc = acc_tile[ct]
            k = NPE
            wcol = par[:, P_CONVW + ct * 31 + k:P_CONVW + ct * 31 + k + 1]

            def fn(e, q=q, ct=ct, wcol=wcol, k=k, tAcc=tAcc):
                inst = None
                for bi, (c0, n) in enumerate(cblocks):
                    inst = e.scalar_tensor_tensor(out=big[:, ct, c0:c0 + n], in0=seqb[:, q, k + c0:k + c0 + n],
                                                  scalar=wcol, in1=psap(tAcc, bi, n), op0=ALU.mult, op1=ALU.add)
                return inst
            S.add("dve", fn, reads=[("seqb", q), *PK(tAcc)], writes=[("big", ct)])

        def rest(cts, mid):
            ks = list(range(NPE + 1, 31))
            for i, k in enumerate(ks):
                if i == len(ks) // 2 and mid is not None:
                    mid()
                for ct in cts:
                    q = ct % 4
                    wcol = par[:, P_CONVW + ct * 31 + k:P_CONVW + ct * 31 + k + 1]
                    S.add("dve", lambda e, q=q, ct=ct, wcol=wcol, k=k: e.scalar_tensor_tensor(
                        out=big[:, ct, 0:Wc], in0=seqb[:, q, k:k + Wc], scalar=wcol,
                        in1=big[:, ct, 0:Wc], op0=ALU.mult, op1=ALU.add),
                        reads=[("seqb", q), ("big", ct)], writes=[("big", ct)])

        def tail(ct):
            p = ct % 2

            def cbf(e, p=p, ct=ct):
                inst = None
                for (lo, ln, so, co) in pieces(G, 0, T, H):
                    inst = e.activation(out=t16[:, p, lo:lo + ln], in_=big[:, ct, co:co + ln], func=AF.Identity,
                                        bias=par[:, P_CONVB + ct:P_CONVB + ct + 1], scale=1.0)
                return inst
            S.add("act", cbf, reads=[("big", ct)], writes=[("t16", p)])

            def mfn(e, p=p, ct=ct):
                inst = None
                for bi, (c0, n) in enumerate(G["blocks"]):
                    inst = e.matmul(psap(G["aux"], bi, n), lhsT=ones_e[:, :], rhs=t16[:, p, c0:c0 + n],
                                    start=(ct == 0), stop=(ct == 15))
                return inst
            S.add("pe", mfn, reads=[("t16", p)], writes=[*PK(G["aux"])])

        front1(0)
        front1(1)
        front2(0)
        front2(1)
        pe_taps(0)
        pe_taps(1)
        f0(0)
        f0(1)
        for pr in range(8):
            c0, c1 = 2 * pr, 2 * pr + 1
            mid = None
            if pr < 7:
                front1(c0 + 2)
                front1(c1 + 2)
                front2(c0 + 2)
                front2(c1 + 2)
                pe_taps(c0 + 2)
                pe_taps(c1 + 2)
                mid = (lambda a=c0 + 2, b=c1 + 2: (f0(a), f0(b)))
            if pr > 0:
                tail(c0 - 2)
                tail(c1 - 2)
            rest([c0, c1], mid)
        tail(14)
        tail(15)
        S.tag = (g, 'L1b')

        def mevac(e):
            inst = None
            for bi, (c0, n) in enumerate(G["blocks"]):
                for (lo, ln, so, co) in pieces(G, c0, n, H):
                    inst = e.activation(out=sA[:, co:co + ln], in_=psap(G["aux"], bi, ln, off=lo - c0), func=AF.Copy)
            return inst
        S.add("act", mevac, reads=[*PK(G["aux"])], writes=[("sA",)])
        for ct in range(16):
            p = ct % 2
            S.add("dve", lambda e, ct=ct: e.scalar_tensor_tensor(out=big[:, ct, 0:Wc], in0=big[:, ct, 0:Wc],
                                                                 scalar=par[:, P_CONVB + ct:P_CONVB + ct + 1],
                                                                 in1=sA[:, 0:Wc], op0=ALU.add, op1=ALU.subtract),
                  reads=[("big", ct), ("sA",)], writes=[("big", ct)])

            def sqf(e, p=p, ct=ct):
                inst = None
                for (lo, ln, so, co) in pieces(G, 0, T, H):
                    inst = e.activation(out=t16[:, p, lo:lo + ln], in_=big[:, ct, co:co + ln], func=AF.Square)
                return inst
            S.add("act", sqf, reads=[("big", ct)], writes=[("t16", p)])

            def vfn(e, p=p, ct=ct):
                inst = None
                for bi, (c0, n) in enumerate(G["blocks"]):
                    inst = e.matmul(psap(G["aux"], bi, n), lhsT=ones_e[:, :], rhs=t16[:, p, c0:c0 + n],
                                    start=(ct == 0), stop=(ct == 15))
                return inst
            S.add("pe", vfn, reads=[("t16", p)], writes=[*PK(G["aux"])])

        def revac(e):
            inst = None
            for bi, (c0, n) in enumerate(G["blocks"]):
                for (lo, ln, so, co) in pieces(G, c0, n, H):
                    inst = e.activation(out=rs[:, co:co + ln], in_=psap(G["aux"], bi, ln, off=lo - c0), func=AF.Sqrt,
                                        bias=epsb[:, 1:2], scale=1.0)
            return inst
        S.add("act", revac, reads=[*PK(G["aux"])], writes=[("rs",)])
        S.add("dve", lambda e: e.reciprocal(out=rs[:, 0:Wc], in_=rs[:, 0:Wc]), reads=[("rs",)], writes=[("rs",)])
        S.tag = (g, 'L1c')
        for pr in range(8):
            slz = use_chunk(gbase + cbase + 16 + pr)
            for m2 in range(2):
                ct = 2 * pr + m2
                tZ = next_tile(G)
                inproj(G, slz, m2, tZ)
                p = state["par"]
                state["par"] ^= 1
                blockwise("act", G, lambda e, bi, c0, n, p=p, tZ=tZ: e.activation(
                    out=t32[:, p, c0:c0 + n], in_=psap(tZ, bi, n), func=AF.Silu),
                    reads=[*PK(tZ)], writes=[("t32", p)])
                S.add("dve", lambda e, ct=ct: e.tensor_tensor(out=big[:, ct, 0:Wc], in0=big[:, ct, 0:Wc],
                                                              in1=rs[:, 0:Wc], op=ALU.mult),
                      reads=[("big", ct), ("rs",)], writes=[("big", ct)])
                S.add("act", lambda e, ct=ct: e.activation(out=big[:, ct, 0:Wc], in_=big[:, ct, 0:Wc], func=AF.Silu,
                                                           bias=par[:, P_CONVNB + ct:P_CONVNB + ct + 1],
                                                           scale=par[:, P_CONVNG + ct:P_CONVNG + ct + 1]),
                      reads=[("big", ct)], writes=[("big", ct)])

                def ymf(e, p=p, ct=ct):
                    inst = None
                    for (lo, ln, so, co) in pieces(G, 0, T, H):
                        inst = e.tensor_tensor(out=ym[:, ct, lo:lo + ln], in0=big[:, ct, co:co + ln],
                                               in1=t32[:, p, lo:lo + ln], op=ALU.mult)
                    return inst
                S.add("pool", ymf, reads=[("big", ct), ("t32", p)], writes=[("ym", ct)])
        if G["s"] is not None:
            s = G["s"]
            S.add("sp", lambda e, s=s: [e.dma_start(out=o_convs[s], in_=cso[:, :])],
                  reads=[("cso",)], writes=[("o_convs", s)], dkey=("cso",))
        S.tag = (g, S.tag[1][:2] + 'o')
        outproj(G, gbase, cbase + 24)

    def sgu_layer(G, g, gbase, cbase, l):
        T = G["T"]
        rmsnorm(G, P_NORMG + 8 * l, lambda k: h[:, k, 0:T], lambda k: ("h", k))
        state["fine"] = 2
        cv = [gbase + cbase + i for i in range(8)]
        slv = [use_chunk(c, hold_from=cv[0]) for c in cv]
        tchunks = [(128 * i, 128) for i in range(4)] + ([(512, 32)] if G["s"] is not None else [])
        stg_info = {}

        Gv = dict(G)
        Gv["tiles"] = [(b,) for b in range(8)]

        def vstage(tci):
            tok0, nt = tchunks[tci]
            nb = 1
            tl = [next_tile(Gv) for _ in range(4)]
            bs = tci % 2
            if bs == 0:
                VR, VNB = vraw, dbuf2
                VQ = [[("vraw", q)] for q in range(4)]
                DK = [("d", k) for k in range(4)]
            else:
                VR, VNB = vrawB, vnbB
                VQ = [[("ym", c) for c in range(8)] for q in range(4)]
                DK = [("ym", c) for c in range(7, 12)]
            VK = sorted(set(k for q in range(4) for k in VQ[q]))
            for ti, tile in enumerate(tl):
                qs = [ti * nb + b for b in range(nb)]

                def vfn(e, qs=qs, tok0=tok0, nt=nt, tile=tile):
                    inst = None
                    for bi, q in enumerate(qs):
                        for c2 in range(2):
                            cc = 2 * q + c2
                            for k in range(8):
                                inst = e.matmul(psap(tile, bi, 256, p0=0, p1=nt, off=c2 * 256),
                                                lhsT=h[:, k, tok0:tok0 + nt],
                                                rhs=slots[:, slv[cc], k * 256:(k + 1) * 256],
                                                start=(k == 0), stop=(k == 7))
                    return inst
                S.add("pe", vfn, reads=[("slot", slv[2 * q + c2]) for q in qs for c2 in range(2)] + [("h", k) for k in range(8)],
                      writes=[*PK(tile)])

                def vev(e, qs=qs, nt=nt, tile=tile, VR=VR):
                    inst = None
                    for bi, q in enumerate(qs):
                        inst = e.activation(out=VR[0:nt, q * 512:(q + 1) * 512], in_=psap(tile, bi, 512, p0=0, p1=nt),
                                            func=AF.Copy)
                    return inst
                S.add("act", vev, reads=[*PK(tile)], writes=sorted(set(k for q in qs for k in VQ[q])))
            for q in range(4):
                S.add("dve", lambda e, q=q, nt=nt, VR=VR, bs=bs: e.bn_stats(out=bst[0:nt, bs, q * 6:(q + 1) * 6],
                                                                         in_=VR[0:nt, q * 512:(q + 1) * 512]),
                      reads=VQ[q], writes=[("bst", bs, q)])
            S.add("dve", lambda e, nt=nt, bs=bs: e.bn_aggr(out=mv[0:nt, bs, 0:2], in_=bst[0:nt, bs, 0:24]),
                  reads=[("bst", bs, q) for q in range(4)], writes=[("mv", bs)])
            S.add("act", lambda e, nt=nt, bs=bs: e.activation(out=mv[0:nt, bs, 2:3], in_=mv[0:nt, bs, 1:2], func=AF.Sqrt,
                                                              bias=epsb[0:nt, 1:2], scale=1.0),
                  reads=[("mv", bs)], writes=[("mv2", bs)])
            S.add("dve", lambda e, nt=nt, bs=bs: e.reciprocal(out=mv[0:nt, bs, 3:4], in_=mv[0:nt, bs, 2:3]),
                  reads=[("mv2", bs)], writes=[("mv3", bs)])
            S.add("dve", lambda e, nt=nt, VR=VR, bs=bs: e.scalar_tensor_tensor(
                out=VR[0:nt, :], in0=VR[0:nt, :], scalar=mv[0:nt, bs, 0:1], in1=bc[0:nt, B_SGG:B_SGG + 2048],
                op0=ALU.subtract, op1=ALU.mult),
                reads=VK + [("mv", bs)], writes=VK)
            need_f32 = (nt == 32) or (g == ngroups - 1 and tci == 3)
            if need_f32:
                S.add("dve", lambda e, nt=nt, VR=VR, bs=bs: e.scalar_tensor_tensor(
                    out=VR[0:nt, :], in0=VR[0:nt, :], scalar=mv[0:nt, bs, 3:4], in1=bc[0:nt, B_SGB:B_SGB + 2048],
                    op0=ALU.mult, op1=ALU.add),
                    reads=VK + [("mv3", bs)], writes=VK)
                S.add("act", lambda e, nt=nt, VR=VR, VNB=VNB: e.activation(out=VNB[0:nt, 0:2048], in_=VR[0:nt, :], func=AF.Copy),
                      reads=VK, writes=DK)
            else:
                S.add("dve", lambda e, nt=nt, VR=VR, VNB=VNB, bs=bs: e.scalar_tensor_tensor(
                    out=VNB[0:nt, 0:2048], in0=VR[0:nt, :], scalar=mv[0:nt, bs, 3:4], in1=bc[0:nt, B_SGB:B_SGB + 2048],
                    op0=ALU.mult, op1=ALU.add),
                    reads=VK + [("mv3", bs)], writes=DK)
            if nt == 32:
                s = G["s"]
                S.add("sp", lambda e, s=s, VR=VR: [e.dma_start(out=o_sgus[s], in_=VR[0:32, :])],
                      reads=VK, writes=[("o_sgus", s)], dkey=("vraw", bs))
            elif g == ngroups - 1 and tci == 3:
                S.add("sp", lambda e, VR=VR: [e.dma_start(out=o_sgup[:, :], in_=VR[:, :])],
                      reads=VK, writes=[("o_sgup",)], dkey=("vraw", bs))
            stg_info[tci] = (tok0, nt, VNB, DK)

        def sstage(tci):
            tok0, nt, VNB, DK = stg_info[tci]
            for hd in range(4):
                tile = next_tile(Gv)

                def sfn(e, hd=hd, nt=nt, tile=tile, VNB=VNB):
                    inst = None
                    for m in range(4):
                        inst = e.matmul(psap(tile, 0, nt, off=m * 128),
                                        lhsT=VNB[0:nt, (4 * hd + m) * 128:(4 * hd + m + 1) * 128],
                                        rhs=wmT[0:nt, hd, 0:nt], start=True, stop=True)
                    return inst
                S.add("pe", sfn, reads=DK + [("wmT",)], writes=[*PK(tile)])

                def sev(e, hd=hd, nt=nt, tok0=tok0, tile=tile):
                    inst = None
                    for m in range(4):
                        inst = e.tensor_tensor(out=big[:, 4 * hd + m, tok0:tok0 + nt], in0=psap(tile, 0, nt, off=m * 128),
                                               in1=bc[:, B_BSB + hd * 128:B_BSB + hd * 128 + nt], op=ALU.add)
                    return inst
                S.add("dve", sev, reads=[*PK(tile)], writes=[("big", 4 * hd + m) for m in range(4)])
        vstage(0)
        for tci in range(len(tchunks)):
            if tci + 1 < len(tchunks):
                vstage(tci + 1)
            sstage(tci)
        S.tag = (g, 'L2u')
        for pr in range(8):
            slu = use_chunk(gbase + cbase + 8 + 2 * pr)
            slz = use_chunk(gbase + cbase + 8 + 2 * pr + 1)
            for m2 in range(2):
                ct = 2 * pr + m2
                tU = next_tile(G)
                inproj(G, slu, m2, tU)
                tZ = next_tile(G)
                inproj(G, slz, m2, tZ)
                p = state["par"]
                state["par"] ^= 1
                blockwise("act", G, lambda e, bi, c0, n, p=p, tZ=tZ: e.activation(
                    out=t32[:, p, c0:c0 + n], in_=psap(tZ, bi, n), func=AF.Silu),
                    reads=[*PK(tZ)], writes=[("t32", p)])
                blockwise("dve", G, lambda e, bi, c0, n, p=p, tU=tU: e.tensor_tensor(
                    out=t32[:, p, c0:c0 + n], in0=psap(tU, bi, n), in1=t32[:, p, c0:c0 + n], op=ALU.mult),
                    reads=[*PK(tU), ("t32", p)], writes=[("t32", p)])
                S.add("pool", lambda e, p=p, ct=ct: e.tensor_tensor(out=ym[:, ct, 0:T], in0=t32[:, p, 0:T],
                                                                    in1=big[:, ct, 0:T], op=ALU.mult),
                      reads=[("t32", p), ("big", ct)], writes=[("ym", ct)])
        S.tag = (g, S.tag[1][:2] + 'o')
        outproj(G, gbase, cbase + 24)

    S.add("sp", lambda e: [e.dma_start(out=par[:, :], in_=params[:, :])], writes=[("par",)], dkey=("par",))
    S.add("sp", lambda e: [e.dma_start(out=bc[:, :], in_=bcin[:, :])], writes=[("bc",)], dkey=("bc",))
    S.add("sp", lambda e: [e.dma_start(out=sgw32[:, :, :], in_=sgwT[:, :, :])], writes=[("sgw32",)], dkey=("sgw32",))
    S.add("pool", lambda e: e.memset(ones_d[:, :], 1.0 / D), writes=[("ones",)])
    S.add("pool", lambda e: e.memset(ones_e[:, :], 1.0 / E), writes=[("ones",)])
    S.add("pool", lambda e: e.memset(epsb[:, 0:1], RMS_EPS), writes=[("eps",)])
    S.add("pool", lambda e: e.memset(epsb[:, 1:2], LN_EPS), writes=[("eps",)])
    S.add("pool", lambda e: e.memset(pool_hp[:, :, :, :], 0.0), writes=[("pool_hp", j, ct) for j in range(2) for ct in range(16)])
    S.add("pool", lambda e: e.memset(conv_hp[:, :, :], 0.0), writes=[("conv_hp", ct) for ct in range(16)])
    S.add("pool", lambda e: e.memset(seqb[:, :, :], 0.0), writes=[("seqb", q) for q in range(4)])
    S.add("pool", lambda e: e.memset(sA[:, :], 0.0), writes=[("sA",)])
    S.add("pool", lambda e: e.memset(sB[:, :], 0.0), writes=[("sB",)])
    S.add("pool", lambda e: e.memset(rs[:, :], 1.0), writes=[("rs",)])
    S.add("pool", lambda e: e.memset(big[:, :, :], 0.0), writes=[("big", ct) for ct in range(16)])
    for hd in range(4):
        S.add("dve", lambda e, hd=hd: e.tensor_tensor(out=wmT[:, hd, :], in0=sgw32[:, hd, :],
                                                      in1=bc[:, B_MASK:B_MASK + 128], op=ALU.mult),
              reads=[("sgw32",), ("bc",)], writes=[("wmT",)])
    S.add("dve", lambda e: e.tensor_copy(out=ident_bf[:, :], in_=bc[:, B_IDENT:B_IDENT + 128]),
          reads=[("bc",)], writes=[("ident",)])
    S.add("pool", lambda e: e.memset(agb[:, :, :], 0.0), writes=[("agb", 0), ("agb", 1)])
    CONST_KEYS = [("par",), ("bc",), ("ones",), ("eps",), ("ident",)]
    for eng in ("pe", "act", "dve", "pool"):
        S.add(eng, None, reads=CONST_KEYS, writes=[])

    def load_x(g):
        G = gstruct(g)
        for k in range(8):
            if G["xi"] == 0:
                S.add("sp", lambda e, g=g, k=k: [e.dma_start(out=x[:, k, 0:512], in_=xT[k, :, g * 512:(g + 1) * 512])],
                      writes=[("x", k)], dkey=("x", k))
            else:
                S.add("sp", lambda e, g=g, k=k: [e.dma_start(out=stg2d[:, k * 512:(k + 1) * 512], in_=xT[k, :, g * 512:(g + 1) * 512])],
                      reads=[("stg", 0), ("stg", 1)], writes=[("xo", k), ("stg", 0), ("stg", 1)], dkey=("x", k))
        if G["s"] is not None:
            s = G["s"]
            S.add("sp", lambda e, s=s: [e.dma_start(out=x[:, :, 512:544],
                                                      in_=xT[:, :, SEQ + s * 32:SEQ + (s + 1) * 32].rearrange("k p t -> p k t"))],
                  writes=[("x", k) for k in range(8)], dkey=("xs",))
            S.add("sp", lambda e, s=s: [e.dma_start(out=st_pool[:, :, :], in_=stpool[:, s, :, :].rearrange("j p f -> p j f"))],
                  writes=[("st_pool",)], dkey=("st_pool",))
            S.add("sp", lambda e, s=s: [e.dma_start(out=st_conv[:, :], in_=stconv[s])],
                  writes=[("st_conv",)], dkey=("st_conv",))

    loaded_x = set()
    early_norm = set()
    for g in range(ngroups):
        G = gstruct(g)
        T = G["T"]
        gbase = g * NCH
        if g not in loaded_x:
            load_x(g)
            loaded_x.add(g)
        S.tag = (g, 'L0')
        if depth >= 1:
            pool_layer(G, g, gbase, L0, 0, 0, skip_norm=(g in early_norm))
        S.tag = (g, 'L1')
        if depth >= 2:
            conv_layer(G, g, gbase, L1, 1)
        S.tag = (g, 'L2')
        if depth >= 3:
            sgu_layer(G, g, gbase, L2, 2)
        S.tag = (g, 'L3')
        if depth >= 4:
            hook = None
            if PINGPONG and g >= 1 and g + 1 < ngroups:
                load_x(g + 1)
                loaded_x.add(g + 1)

                def hook(gn=g + 1):
                    Gn = gstruct(gn)
                    Tn = Gn["T"]
                    rmsnorm(Gn, P_NORMG, lambda k, Tn=Tn: h[:, k, 0:Tn], lambda k: ("h", k))
                    state["fine"] = 2
                    early_norm.add(gn)
            pool_layer(G, g, gbase, L3, 1, 3, pre_out=hook)
        S.tag = (g, 'FIN')
        rmsnorm(G, P_FINALG, lambda k, T=T: big[:, k, 0:T], lambda k: ("big", k), sq=ym, sqk="ym")
        for k in range(8):
            S.add("sp", lambda e, g=g, k=k: [e.dma_start(out=yT[k, :, g * 512:(g + 1) * 512], in_=big[:, k, 0:512])],
                  reads=[("big", k)], writes=[("yT", g, k)], dkey=("yout", k))
        if G["s"] is not None:
            s = G["s"]
            S.add("sp", lambda e, s=s: [e.dma_start(out=yT[:, :, SEQ + s * 32:SEQ + (s + 1) * 32].rearrange("k p t -> p k t"),
                                                      in_=big[:, 0:8, 512:544])],
                  reads=[("big", k) for k in range(8)], writes=[("yTs", s)], dkey=("youts",))
    S.add("sp", lambda e: [e.dma_start(out=o_poolp[:, :, :].rearrange("j p f -> p j f"),
                                       in_=pool_hp[:, :, :, :].rearrange("p j c r -> p j (c r)"))],
          reads=[("pool_hp", j, ct) for j in range(2) for ct in range(16)], writes=[("o_poolp",)], dkey=("ohp",))
    S.add("sp", lambda e: [e.dma_start(out=o_convp[:, :], in_=conv_hp[:, :, :].rearrange("p c r -> p (c r)"))],
          reads=[("conv_hp", ct) for ct in range(16)], writes=[("o_convp",)], dkey=("ohp",))

    ops = S.ops
    for op in ops:
        for d in op.deps:
            if d.dkey is None and not (d.eng == "pe" and op.eng == "pe"):
                d.mark = True
    cnt = {}
    for op in ops:
        if op.mark:
            cnt[op.eng] = cnt.get(op.eng, 0) + 1
            op.markno = cnt[op.eng]

    engs = ["pe", "act", "dve", "pool", "sp"]
    esem = {en: es.enter_context(nc.semaphore("sem_" + en)) for en in engs}
    dsem = {}
    for k in S.dcount:
        dsem[k] = es.enter_context(nc.semaphore("dsem%d" % len(dsem)))

    blk = es.enter_context(nc.Block())

    def emit(engname, handle):
        waited = {}
        for op in ops:
            if op.eng != engname:
                continue
            for d in sorted(op.deps, key=lambda o: o.idx):
                if d.dkey is not None:
                    sem, val = dsem[d.dkey], d.dval
                else:
                    if d.eng == "pe" and engname == "pe":
                        continue
                    if d.fn is None:
                        continue
                    sem, val = esem[d.eng], d.markno
                key = id(sem)
                if waited.get(key, 0) >= val:
                    continue
                waited[key] = val
                handle.wait_ge(sem, val)
            if op.fn is None:
                continue
            n0 = nc.n_instructions()
            r = op.fn(handle)
            DBG_TAGS.setdefault(engname, []).append((op.tag, nc.n_instructions() - n0))
            if op.dkey is not None:
                for inst in r:
                    inst.then_inc(dsem[op.dkey], 16)
            elif op.mark:
                r.then_inc(esem[engname], 1)
        if engname == "sp":
            for k, n in S.dcount.items():
                handle.wait_ge(dsem[k], 16 * n)

    @blk.tensor
    def _(e):
        emit("pe", e)

    @blk.scalar
    def _(e):
        emit("act", e)

    @blk.vector
    def _(e):
        emit("dve", e)

    @blk.gpsimd
    def _(e):
        emit("pool", e)

    @blk.sync
    def _(e):
        emit("sp", e)

    es.close()
    return nc


def _in_chunk(Wm, col0):
    blk = Wm[:, col0:col0 + 256].reshape(8, 128, 256).transpose(1, 0, 2)
    return np.ascontiguousarray(blk).reshape(128, 2048)


def _out_chunk(Wm, m):
    blk = Wm[:, m * 128:(m + 1) * 128].reshape(16, 128, 128).transpose(1, 0, 2)
    return np.ascontiguousarray(blk).reshape(128, 2048)


def _grp_chunk(Wg):
    blk = Wg.reshape(4, 128, 512).transpose(1, 0, 2)
    return np.ascontiguousarray(blk).reshape(128, 2048)


def _pack_weights(pool_in_w, pool_w, pool_out_w, conv_in_w, conv_out_w, sgu_in_w, sgu_out_w):
    ch = []

    def pool(j):
        for gq in range(4):
            ch.append(_in_chunk(pool_in_w[j], 512 * gq))
            ch.append(_in_chunk(pool_in_w[j], 512 * gq + 256))
            ch.append(_in_chunk(pool_in_w[j], E + 512 * gq))
            ch.append(_in_chunk(pool_in_w[j], E + 512 * gq + 256))
            ch.append(_grp_chunk(pool_w[j, gq]))
        for m in range(8):
            ch.append(_out_chunk(pool_out_w[j], m))
    pool(0)
    for pr in range(8):
        ch.append(_in_chunk(conv_in_w[0], 256 * pr))
        ch.append(_in_chunk(conv_in_w[0], E + 256 * pr))
    for pr in range(8):
        ch.append(_in_chunk(conv_in_w[0], 2 * E + 256 * pr))
    for m in range(8):
        ch.append(_out_chunk(conv_out_w[0], m))
    for i in range(8):
        ch.append(_in_chunk(sgu_in_w[0], E + 256 * i))
    for pr in range(8):
        ch.append(_in_chunk(sgu_in_w[0], 256 * pr))
        ch.append(_in_chunk(sgu_in_w[0], 2 * E + 256 * pr))
    for m in range(8):
        ch.append(_out_chunk(sgu_out_w[0], m))
    pool(1)
    assert len(ch) == NCH
    return np.stack(ch).astype(np.float32)


def _colvec(v, nt):
    return np.asarray(v, np.float32).reshape(nt, 128).T


_PROG = {}
DBG_TAGS = {}


def kernel(x_prompt, x_sample, state_pool, state_conv, norm_g, pool_in_w, pool_w, pool_scale, pool_out_w,
           conv_in_w, conv_w, conv_b, conv_norm_g, conv_norm_b, conv_out_w,
           sgu_in_w, sgu_norm_g, sgu_norm_b, sgu_w, sgu_b, sgu_out_w, final_g, _depth=4, _ngroups=NG):
    f = lambda a: np.asarray(a, np.float32)
    x_prompt, x_sample, state_pool, state_conv = f(x_prompt), f(x_sample), f(state_pool), f(state_conv)
    wst = _pack_weights(f(pool_in_w), f(pool_w), f(pool_out_w), f(conv_in_w), f(conv_out_w), f(sgu_in_w), f(sgu_out_w))
    par = np.zeros((128, NPAR), np.float32)
    for l in range(4):
        par[:, P_NORMG + 8 * l:P_NORMG + 8 * l + 8] = _colvec(f(norm_g)[l], 8)
    par[:, P_FINALG:P_FINALG + 8] = _colvec(f(final_g), 8)
    for j in range(2):
        par[:, P_PSCALE + 16 * j:P_PSCALE + 16 * j + 16] = _colvec(f(pool_scale)[j], 16)
    cw = f(conv_w)[0]
    par[:, P_CONVW:P_CONVW + 496] = cw.T.reshape(16, 128, 31).transpose(1, 0, 2).reshape(128, 496)
    par[:, P_CONVB:P_CONVB + 16] = _colvec(f(conv_b)[0], 16)
    par[:, P_CONVNG:P_CONVNG + 16] = _colvec(f(conv_norm_g)[0], 16)
    par[:, P_CONVNB:P_CONVNB + 16] = _colvec(f(conv_norm_b)[0], 16)
    bcin = np.zeros((128, NBC), np.float32)
    bcin[:, B_SGG:B_SGG + 2048] = f(sgu_norm_g)[0][None, :]
    bcin[:, B_SGB:B_SGB + 2048] = f(sgu_norm_b)[0][None, :]
    bcin[:, B_BSB:B_BSB + 512] = f(sgu_b)[0].reshape(1, 512)
    jj = np.arange(128)
    bcin[:, B_MASK:B_MASK + 128] = (jj[:, None] <= jj[None, :]).astype(np.float32)
    for gq, w in enumerate(POOL_WINDOWS):
        t = np.arange(16)
        bcin[:, B_RATIO + 16 * gq:B_RATIO + 16 * gq + 16] = (w / np.minimum(t + 1, w)).astype(np.float32)[None, :]
    bcin[:, B_IDENT:B_IDENT + 128] = np.eye(128, dtype=np.float32)
    sgwT = np.ascontiguousarray(f(sgu_w)[0].transpose(2, 0, 1))

    in_maps = []
    for c in range(NCORE):
        xt = np.empty((8, 128, SEQ + NSAMP * SLEN), np.float32)
        xt[:, :, :SEQ] = x_prompt[c].T.reshape(8, 128, SEQ)
        xs = x_sample[NSAMP * c:NSAMP * (c + 1)].reshape(NSAMP * SLEN, D)
        xt[:, :, SEQ:] = xs.T.reshape(8, 128, NSAMP * SLEN)
        sp = state_pool[:, NSAMP * c:NSAMP * (c + 1)]
        sp = sp.transpose(0, 1, 3, 2).reshape(2, NSAMP, 16, 128, 15).transpose(0, 1, 3, 2, 4).reshape(2, NSAMP, 128, 240)
        sc = state_conv[0, NSAMP * c:NSAMP * (c + 1)]
        sc = sc.transpose(0, 2, 1).reshape(NSAMP, 16, 128, 30).transpose(0, 2, 1, 3).reshape(NSAMP, 128, 480)
        in_maps.append({"xT": xt, "wst": wst, "params": par, "bcin": bcin, "sgwT": sgwT,
                        "stpool": np.ascontiguousarray(sp), "stconv": np.ascontiguousarray(sc)})
    key = (_depth, _ngroups)
    if key not in _PROG:
        _PROG[key] = build_program(_depth, _ngroups)
    res = run_bass_kernel_spmd(_PROG[key], in_maps, core_ids=list(range(NCORE)))
    R = res.results

    y_prompt = np.empty((8, SEQ, D), np.float32)
    y_sample = np.empty((32, SLEN, D), np.float32)
    npp = np.empty((2, 8, 15, E), np.float32)
    nps = np.empty((2, 32, 15, E), np.float32)
    ncp = np.empty((1, 8, 30, E), np.float32)
    ncs = np.empty((1, 32, 30, E), np.float32)
    nsp = np.empty((1, 8, 128, E), np.float32)
    nss = np.empty((1, 32, SLEN, E), np.float32)
    for c in range(NCORE):
        r = R[c]
        yt = np.asarray(r["yT"]).reshape(D, SEQ + NSAMP * SLEN)
        y_prompt[c] = yt[:, :SEQ].T
        y_sample[NSAMP * c:NSAMP * (c + 1)] = yt[:, SEQ:].T.reshape(NSAMP, SLEN, D)
        pp = np.asarray(r["o_poolp"]).reshape(2, 128, 16, 15)
        npp[:, c] = pp.transpose(0, 3, 2, 1).reshape(2, 15, E)
        pq = np.asarray(r["o_pools"]).reshape(2, NSAMP, 128, 16, 15)
        nps[:, NSAMP * c:NSAMP * (c + 1)] = pq.transpose(0, 1, 4, 3, 2).reshape(2, NSAMP, 15, E)
        cp = np.asarray(r["o_convp"]).reshape(128, 16, 30)
        ncp[0, c] = cp.transpose(2, 1, 0).reshape(30, E)
        cq = np.asarray(r["o_convs"]).reshape(NSAMP, 128, 16, 30)
        ncs[0, NSAMP * c:NSAMP * (c + 1)] = cq.transpose(0, 3, 2, 1).reshape(NSAMP, 30, E)
        nsp[0, c] = np.asarray(r["o_sgup"]).reshape(128, E)
        nss[0, NSAMP * c:NSAMP * (c + 1)] = np.asarray(r["o_sgus"]).reshape(NSAMP, SLEN, E)
    return (y_prompt, y_sample, npp, nps, ncp, ncs, nsp, nss)
```

```python
import numpy as np
from contextlib import ExitStack
import concourse.bass as bass
import concourse.mybir as mybir
from concourse.bass_utils import run_bass_kernel_spmd

F32 = mybir.dt.float32
BF16 = mybir.dt.bfloat16
ALU = mybir.AluOpType
AF = mybir.ActivationFunctionType

D = 1024
E = 2048
SEQ = 4096
NCORE = 8
NSAMP = 4
SLEN = 32
TP = 512
NG = SEQ // TP
NCH = 120
NSLOT = 9
PF = 6
RMS_EPS = 1e-6
LN_EPS = 1e-5
POOL_WINDOWS = (2, 4, 8, 16)
BIGW = 512 + 30 + 32
SEQW = 512 + 60 + 32

P_NORMG = 0
P_FINALG = P_NORMG + 32
P_PSCALE = P_FINALG + 8
P_CONVW = P_PSCALE + 32
P_CONVB = P_CONVW + 496
P_CONVNG = P_CONVB + 16
P_CONVNB = P_CONVNG + 16
NPAR = P_CONVNB + 16
B_SGG = 0
B_SGB = B_SGG + 2048
B_BSB = B_SGB + 2048
B_MASK = B_BSB + 512
B_RATIO = B_MASK + 128
B_IDENT = B_RATIO + 64
NBC = B_IDENT + 128
PINGPONG = True
NPE = 19


class Op:
    __slots__ = ("eng", "fn", "deps", "mark", "markno", "dkey", "dval", "idx", "tag")


class Sched:
    def __init__(self):
        self.ops = []
        self.last_w = {}
        self.readers = {}
        self.dcount = {}
        self.tag = None

    def add(self, eng, fn, reads=(), writes=(), dkey=None, ndma=1):
        op = Op()
        op.eng, op.fn, op.mark, op.markno, op.dkey, op.dval = eng, fn, False, 0, dkey, 0
        op.idx = len(self.ops)
        op.tag = self.tag
        deps = set()
        for k in reads:
            w = self.last_w.get(k)
            if w is not None:
                deps.add(w)
        for k in writes:
            w = self.last_w.get(k)
            if w is not None:
                deps.add(w)
            for r in self.readers.get(k, ()):
                deps.add(r)
        wset = set(writes)
        for k in writes:
            self.last_w[k] = op
            self.readers[k] = []
        for k in reads:
            if k not in wset:
                self.readers.setdefault(k, []).append(op)
        deps.discard(op)
        op.deps = deps
        if dkey is not None:
            self.dcount[dkey] = self.dcount.get(dkey, 0) + ndma
            op.dval = 16 * self.dcount[dkey]
        self.ops.append(op)
        return op


def build_program(depth=4, ngroups=NG):
    nc = bass.Bass("TRN2", target_bir_lowering=False)
    S = Sched()

    def dram(name, shape, dt, kind):
        return nc.dram_tensor(name, shape, dt, kind=kind).ap()

    xT = dram("xT", [8, 128, SEQ + NSAMP * SLEN], F32, "ExternalInput")
    wst = dram("wst", [NCH, 128, 2048], F32, "ExternalInput")
    params = dram("params", [128, NPAR], F32, "ExternalInput")
    bcin = dram("bcin", [128, NBC], F32, "ExternalInput")
    sgwT = dram("sgwT", [128, 4, 128], F32, "ExternalInput")
    stpool = dram("stpool", [2, NSAMP, 128, 240], F32, "ExternalInput")
    stconv = dram("stconv", [NSAMP, 128, 480], F32, "ExternalInput")
    yT = dram("yT", [8, 128, SEQ + NSAMP * SLEN], F32, "ExternalOutput")
    o_poolp = dram("o_poolp", [2, 128, 240], F32, "ExternalOutput")
    o_pools = dram("o_pools", [2, NSAMP, 128, 240], F32, "ExternalOutput")
    o_convp = dram("o_convp", [128, 480], F32, "ExternalOutput")
    o_convs = dram("o_convs", [NSAMP, 128, 480], F32, "ExternalOutput")
    o_sgup = dram("o_sgup", [128, 2048], F32, "ExternalOutput")
    o_sgus = dram("o_sgus", [NSAMP, SLEN, 2048], F32, "ExternalOutput")
    wq = dram("wq", [NCH, 128, 2048], BF16, "Internal")
    dgq = dram("dgq", [16, 128, NPE * 128], BF16, "Internal")

    es = ExitStack()

    def sb(name, shape, dt):
        return es.enter_context(nc.sbuf_tensor(name, shape, dt))

    ps = es.enter_context(nc.psum_tensor("ps", [128, 4096], F32))
    x = sb("x", [128, 8, 544], F32)
    h = sb("h", [128, 8, 544], BF16)
    ym = sb("ym", [128, 16, 544], BF16)
    big = sb("big", [128, 16, BIGW], F32)
    slots = sb("slots", [128, NSLOT, 2048], BF16)
    stg = sb("stg", [128, 2, 2048], F32)
    par = sb("par", [128, NPAR], F32)
    bc = sb("bc", [128, NBC], F32)
    sgw32 = sb("sgw32", [128, 4, 128], F32)
    wmT = sb("wmT", [128, 4, 128], BF16)
    ones_d = sb("ones_d", [128, 128], BF16)
    ones_e = sb("ones_e", [128, 128], BF16)
    epsb = sb("epsb", [128, 2], F32)
    pool_hp = sb("pool_hp", [128, 2, 16, 15], F32)
    conv_hp = sb("conv_hp", [128, 16, 30], F32)
    st_pool = sb("st_pool", [128, 2, 240], F32)
    st_conv = sb("st_conv", [128, 480], F32)
    pso = sb("pso", [128, 2, 240], F32)
    cso = sb("cso", [128, 480], F32)
    seqb = sb("seqb", [128, 4, SEQW], F32)
    sA = sb("sA", [128, SEQW], F32)
    sB = sb("sB", [128, SEQW], F32)
    dbuf = sb("dbuf", [128, 4, 544], BF16)
    dbuf2 = dbuf[:, :, :].rearrange("p a b -> p (a b)")
    t32 = sb("t32", [128, 2, 544], F32)
    t16 = sb("t16", [128, 2, 544], BF16)
    rs = sb("rs", [128, BIGW], F32)
    vraw = sb("vraw", [128, 2048], F32)
    bst = sb("bst", [128, 2, 24], F32)
    agb = sb("agb", [128, 2, SEQW], BF16)
    ident_bf = sb("ident_bf", [128, 128], BF16)
    ym2d = ym[:, :, :].rearrange("p a b -> p (a b)")
    DGOFF = [0, 5 * 544]
    dg = ym2d
    stg2d = stg[:, :, :].rearrange("p a b -> p (a b)")
    DGK = [[("ym", c) for c in range(0, 5)], [("ym", c) for c in range(5, 10)]]
    assert NPE * 128 <= 5 * 544
    dgs = vraw[:, :].bitcast(BF16)
    vrawB = ym2d[:, 0:4096].bitcast(F32)
    vnbB = ym2d[:, 4096:6144]
    assert tuple(vrawB.shape) == (128, 2048), vrawB.shape
    mv = sb("mv", [128, 2, 4], F32)

    state = {"tile": 0, "loaded": 0, "par": 0, "fine": 0}

    def next_tile(G):
        n = len(G["tiles"])
        t = state["tile"] % n
        state["tile"] = (t + 1) % n
        return G["tiles"][t]

    def psap(tile, bi, n, p0=0, p1=128, off=0):
        b = tile[bi]
        return ps[p0:p1, b * 512 + off: b * 512 + off + n]

    def PK(tile):
        return [("psb", b) for b in tile]

    def gstruct(g):
        even = (g % 2 == 0)
        if even:
            return dict(T=544, segs=[(0, 512), (512, 32)], blocks=[(0, 272), (272, 272)], s=g // 2,
                        tiles=[(0, 1), (2, 3), (4, 5)], aux=(6, 7), xi=0, g=g)
        return dict(T=512, segs=[(0, 512)], blocks=[(0, 512)], s=None,
                    tiles=[(b,) for b in range(7)], aux=(7,), xi=(1 if PINGPONG else 0), g=g)

    def XAP(G, k, c0, n):
        if G["xi"] == 0:
            return x[:, k, c0:c0 + n]
        return stg2d[:, k * 512 + c0:k * 512 + c0 + n]

    def XK(G, k):
        return ("x", k) if G["xi"] == 0 else ("xo", k)

    def pieces(G, c0, n, H):
        out = []
        for si, (s0, sn) in enumerate(G["segs"]):
            lo, hi = max(c0, s0), min(c0 + n, s0 + sn)
            if lo < hi:
                seqoff = H + (0 if si == 0 else 512 + H) + (lo - s0)
                cloff = (0 if si == 0 else 512 + H) + (lo - s0)
                out.append((lo, hi - lo, seqoff, cloff))
        return out

    def issue_load(gi):
        ci = gi % NCH
        grp = gi // NCH
        sl = gi % NSLOT
        if grp == 0:
            st = gi % 2
            S.add("sp", lambda e, ci=ci, st=st: [e.dma_start(out=stg[:, st, :], in_=wst[ci])],
                  reads=[], writes=[("stg", st)], dkey=("stg", st))
            S.add("act", lambda e, st=st, sl=sl: e.activation(out=slots[:, sl, :], in_=stg[:, st, :], func=AF.Copy),
                  reads=[("stg", st)], writes=[("slot", sl)])
            S.add("act", lambda e, ci=ci, sl=sl: [e.dma_start(out=wq[ci], in_=slots[:, sl, :])],
                  reads=[("slot", sl)], writes=[("wq", ci)], dkey=("slot", sl))
        else:
            S.add("sp", lambda e, ci=ci, sl=sl: [e.dma_start(out=slots[:, sl, :], in_=wq[ci])],
                  reads=[("wq", ci)], writes=[("slot", sl)], dkey=("slot", sl))

    total_chunks = ngroups * NCH

    def use_chunk(gi, hold_from=None):
        upto = min(gi + PF, total_chunks - 1)
        if hold_from is not None:
            upto = min(upto, hold_from + NSLOT - 1)
        while state["loaded"] <= upto:
            if (state["loaded"] % NCH) < chunks_per_pass:
                issue_load(state["loaded"])
            state["loaded"] += 1
        return gi % NSLOT

    L0 = 0
    L1 = 28
    L2 = 60
    L3 = 92
    chunks_per_pass = [0, 28, 60, 92, 120][depth]

    def mm_tile(G, tile, nk, lhs_fn, rhs_fn, reads):
        def fn(e):
            inst = None
            for k in range(nk):
                for bi, (c0, n) in enumerate(G["blocks"]):
                    inst = e.matmul(psap(tile, bi, n), lhsT=lhs_fn(k), rhs=rhs_fn(k, c0, n),
                                    start=(k == 0), stop=(k == nk - 1))
            return inst
        S.add("pe", fn, reads=reads, writes=[*PK(tile)])

    def inproj(G, sl, m2, tile):
        if state["fine"] > 0:
            state["fine"] -= 1
            for k in range(8):
                def fn(e, k=k):
                    inst = None
                    for bi, (c0, n) in enumerate(G["blocks"]):
                        inst = e.matmul(psap(tile, bi, n), lhsT=slots[:, sl, k * 256 + m2 * 128: k * 256 + m2 * 128 + 128],
                                        rhs=h[:, k, c0:c0 + n], start=(k == 0), stop=(k == 7))
                    return inst
                S.add("pe", fn, reads=[("slot", sl), ("h", k)], writes=[*PK(tile)])
            return
        mm_tile(G, tile, 8,
                lambda k: slots[:, sl, k * 256 + m2 * 128: k * 256 + m2 * 128 + 128],
                lambda k, c0, n: h[:, k, c0:c0 + n],
                reads=[("slot", sl)] + [("h", k) for k in range(8)])

    def blockwise(eng, G, fn_block, reads, writes):
        def fn(e):
            inst = None
            for bi, (c0, n) in enumerate(G["blocks"]):
                inst = fn_block(e, bi, c0, n)
            return inst
        S.add(eng, fn, reads=reads, writes=writes)

    def rmsnorm(G, gcol0, out_fn, out_keys, sq=None, sqk="h"):
        T = G["T"]
        if sq is None:
            sq = h
        for k in range(8):
            S.add("act", lambda e, k=k: e.activation(out=sq[:, k, 0:T], in_=XAP(G, k, 0, T), func=AF.Square),
                  reads=[XK(G, k)], writes=[(sqk, k)])
        mm_tile(G, G["aux"], 8, lambda k: ones_d[:, :], lambda k, c0, n: sq[:, k, c0:c0 + n],
                reads=[(sqk, k) for k in range(8)])
        blockwise("act", G, lambda e, bi, c0, n: e.activation(out=rs[:, c0:c0 + n], in_=psap(G["aux"], bi, n),
                                                              func=AF.Sqrt, bias=epsb[:, 0:1], scale=1.0),
                  reads=[*PK(G["aux"])], writes=[("rs",)])
        S.add("dve", lambda e: e.reciprocal(out=rs[:, 0:T], in_=rs[:, 0:T]), reads=[("rs",)], writes=[("rs",)])
        for k in range(8):
            S.add("dve", lambda e, k=k: e.scalar_tensor_tensor(out=out_fn(k), in0=XAP(G, k, 0, T),
                                                               scalar=par[:, gcol0 + k: gcol0 + k + 1],
                                                               in1=rs[:, 0:T], op0=ALU.mult, op1=ALU.mult),
                  reads=[XK(G, k), ("rs",)], writes=[out_keys(k)])

    def outproj(G, gbase, c_first):
        for m in range(8):
            sl = use_chunk(gbase + c_first + m)
            tile = next_tile(G)
            mm_tile(G, tile, 16, lambda k, sl=sl: slots[:, sl, k * 128:(k + 1) * 128],
                    lambda k, c0, n: ym[:, k, c0:c0 + n],
                    reads=[("slot", sl)] + [("ym", k) for k in range(16)])
            blockwise("dve", G, lambda e, bi, c0, n, m=m, tile=tile: e.tensor_tensor(
                out=XAP(G, m, c0, n), in0=psap(tile, bi, n), in1=XAP(G, m, c0, n), op=ALU.add),
                reads=[*PK(tile), XK(G, m)], writes=[XK(G, m)])

    def gen_dg(ct):
        for k in range(NPE):
            wcol = par[:, P_CONVW + ct * 31 + k:P_CONVW + ct * 31 + k + 1]
            if k % 2 == 0:
                S.add("act", lambda e, k=k, wcol=wcol: e.activation(
                    out=dgs[:, k * 128:(k + 1) * 128], in_=ident_bf[:, :], func=AF.Copy, scale=wcol),
                    reads=[("ident",)], writes=[("dgs", k)])
            else:
                S.add("pool", lambda e, k=k, wcol=wcol: e.tensor_scalar(
                    out=dgs[:, k * 128:(k + 1) * 128], in0=ident_bf[:, :], scalar1=wcol, scalar2=1.0,
                    op0=ALU.mult, op1=ALU.mult),
                    reads=[("ident",)], writes=[("dgs", k)])
        S.add("act", lambda e, ct=ct: [e.dma_start(out=dgq[ct], in_=dgs[:, 0:NPE * 128])],
              reads=[("dgs", k) for k in range(NPE)], writes=[("dgq", ct)], dkey=("dgs",))

    def pool_layer(G, g, gbase, cbase, j, l, skip_norm=False, pre_out=None):
        H = 15
        T = G["T"]
        W = H + 512 + ((H + 32) if G["s"] is not None else 0)
        offS = H + 512 + H
        if not skip_norm:
            rmsnorm(G, P_NORMG + 8 * l, lambda k: h[:, k, 0:T], lambda k: ("h", k))
            state["fine"] = 2
        for gq in range(4):
            w = POOL_WINDOWS[gq]
            cu = [cbase + gq * 5 + 0, cbase + gq * 5 + 1]
            cz = [cbase + gq * 5 + 2, cbase + gq * 5 + 3]
            cg = cbase + gq * 5 + 4
            for m in range(4):
                ct = 4 * gq + m
                sl = use_chunk(gbase + cu[m // 2])
                tile = next_tile(G)
                inproj(G, sl, m % 2, tile)
                p = state["par"]
                state["par"] ^= 1
                ub = seqb[:, p, :]
                S.add("pool", lambda e, p=p, ct=ct: e.tensor_copy(out=seqb[:, p, 0:H], in_=pool_hp[:, j, ct, :]),
                      reads=[("pool_hp", j, ct)], writes=[("seqb", p)])
                if G["s"] is not None:
                    S.add("pool", lambda e, p=p, ct=ct: e.tensor_copy(out=seqb[:, p, H + 512:H + 512 + H],
                                                                      in_=st_pool[:, j, ct * 15:(ct + 1) * 15]),
                          reads=[("st_pool",)], writes=[("seqb", p)])

                def evac(e, p=p, tile=tile):
                    inst = None
                    for bi, (c0, n) in enumerate(G["blocks"]):
                        for (lo, ln, so, co) in pieces(G, c0, n, H):
                            inst = e.activation(out=seqb[:, p, so:so + ln], in_=psap(tile, bi, ln, off=lo - c0),
                                                func=AF.Copy)
                    return inst
                S.add("act", evac, reads=[*PK(tile)], writes=[("seqb", p)])
                S.add("pool", lambda e, p=p, ct=ct: e.tensor_copy(out=pool_hp[:, j, ct, :], in_=seqb[:, p, 512:512 + H]),
                      reads=[("seqb", p)], writes=[("pool_hp", j, ct)])
                if G["s"] is not None:
                    S.add("pool", lambda e, p=p, ct=ct: e.tensor_copy(out=pso[:, j, ct * 15:(ct + 1) * 15],
                                                                      in_=seqb[:, p, offS + 17:offS + 32]),
                          reads=[("seqb", p)], writes=[("pso", j)])
                S.add("dve", lambda e, p=p: e.tensor_tensor(out=sA[:, 1:W], in0=seqb[:, p, 1:W], in1=seqb[:, p, 0:W - 1],
                                                            op=ALU.add),
                      reads=[("seqb", p)], writes=[("sA",)])
                res, rk = sA, ("sA",)
                if w >= 4:
                    S.add("dve", lambda e: e.tensor_tensor(out=sB[:, 3:W], in0=sA[:, 3:W], in1=sA[:, 1:W - 2], op=ALU.add),
                          reads=[("sA",)], writes=[("sB",)])
                    res, rk = sB, ("sB",)
                if w >= 8:
                    S.add("dve", lambda e: e.tensor_tensor(out=sA[:, 7:W], in0=sB[:, 7:W], in1=sB[:, 3:W - 4], op=ALU.add),
                          reads=[("sB",)], writes=[("sA",)])
                    res, rk = sA, ("sA",)
                if w >= 16:
                    S.add("dve", lambda e: e.tensor_tensor(out=sB[:, 15:W], in0=sA[:, 15:W], in1=sA[:, 7:W - 8], op=ALU.add),
                          reads=[("sA",)], writes=[("sB",)])
                    res, rk = sB, ("sB",)
                if g == 0:
                    S.add("dve", lambda e, res=res, gq=gq: e.tensor_tensor(
                        out=res[:, H:H + 15], in0=res[:, H:H + 15],
                        in1=bc[:, B_RATIO + gq * 16:B_RATIO + gq * 16 + 15], op=ALU.mult),
                        reads=[rk], writes=[rk])

                def dfn(e, p=p, res=res, m=m, w=w):
                    inst = None
                    for (lo, ln, so, co) in pieces(G, 0, T, H):
                        inst = e.scalar_tensor_tensor(out=dbuf[:, m, lo:lo + ln], in0=res[:, so:so + ln],
                                                      scalar=1.0 / w, in1=seqb[:, p, so:so + ln],
                                                      op0=ALU.mult, op1=ALU.subtract)
                    return inst
                S.add("dve", dfn, reads=[rk, ("seqb", p)], writes=[("d", m)])
            if g == 0 and l == 0:
                for m in range(4):
                    gen_dg(4 * gq + m)
            for m in range(4):
                ct = 4 * gq + m
                sl = use_chunk(gbase + cz[m // 2])
                tile = next_tile(G)
                inproj(G, sl, m % 2, tile)
                blockwise("act", G, lambda e, bi, c0, n, ct=ct, tile=tile: e.activation(
                    out=big[:, ct, c0:c0 + n], in_=psap(tile, bi, n), func=AF.Silu),
                    reads=[*PK(tile)], writes=[("big", ct)])
            slg = use_chunk(gbase + cg)
            for m in range(4):
                ct = 4 * gq + m
                tile = next_tile(G)
                mm_tile(G, tile, 4, lambda k, m=m, slg=slg: slots[:, slg, k * 512 + m * 128:k * 512 + m * 128 + 128],
                        lambda k, c0, n: dbuf[:, k, c0:c0 + n],
                        reads=[("slot", slg)] + [("d", k) for k in range(4)])
                blockwise("dve", G, lambda e, bi, c0, n, ct=ct, tile=tile: e.scalar_tensor_tensor(
                    out=ym[:, ct, c0:c0 + n], in0=psap(tile, bi, n),
                    scalar=par[:, P_PSCALE + j * 16 + ct:P_PSCALE + j * 16 + ct + 1],
                    in1=big[:, ct, c0:c0 + n], op0=ALU.mult, op1=ALU.mult),
                    reads=[*PK(tile), ("big", ct)], writes=[("ym", ct)])
        if G["s"] is not None:
            s = G["s"]
            S.add("sp", lambda e, s=s: [e.dma_start(out=o_pools[j, s], in_=pso[:, j, :])],
                  reads=[("pso", j)], writes=[("o_pools", j, s)], dkey=("pso", j))
        S.tag = (g, S.tag[1][:2] + 'o')
        if pre_out is not None:
            pre_out()
        outproj(G, gbase, cbase + 20)

    def conv_layer(G, g, gbase, cbase, l):
        H = 30
        T = G["T"]
        W = H + 512 + ((H + 32) if G["s"] is not None else 0)
        Wc = W - H
        offS = H + 512 + H
        rmsnorm(G, P_NORMG + 8 * l, lambda k: h[:, k, 0:T], lambda k: ("h", k))
        state["fine"] = 2
        onepass = (len(G["tiles"][0]) == 1)
        Gc = dict(G)
        aux2 = None
        if onepass:
            Gc["tiles"] = G["tiles"][:6]
            aux2 = G["tiles"][6]
        cblocks = [(0, 287), (287, 287)] if G["s"] is not None else [(0, 512)]
        halves = [(0, Wc // 2), (Wc // 2, Wc - Wc // 2)]
        fr = {}
        tile_a = {}

        def front1(ct):
            pr, m2 = ct // 2, ct % 2
            sla = use_chunk(gbase + cbase + 2 * pr)
            slg = use_chunk(gbase + cbase + 2 * pr + 1)
            if len(G["tiles"]) == 3:
                tA = G["tiles"][0 if ct % 2 == 0 else 2]
                tG = G["tiles"][1]
            else:
                tA = next_tile(Gc)
                tG = next_tile(Gc)
            inproj(G, sla, m2, tA)
            inproj(G, slg, m2, tG)
            tile_a[ct] = tA
            p = ct % 2
            q = ct % 4
            S.add("sp", lambda e, p=p, ct=ct: [e.dma_start(out=dg[:, DGOFF[p]:DGOFF[p] + NPE * 128], in_=dgq[ct])],
                  reads=[("dgq", ct)], writes=DGK[p], dkey=("dg", p))
            blockwise("act", G, lambda e, bi, c0, n, p=p, tG=tG: e.activation(
                out=t32[:, p, c0:c0 + n], in_=psap(tG, bi, n), func=AF.Sigmoid),
                reads=[*PK(tG)], writes=[("t32", p)])
            S.add("pool", lambda e, q=q, ct=ct: e.tensor_copy(out=seqb[:, q, 0:H], in_=conv_hp[:, ct, :]),
                  reads=[("conv_hp", ct)], writes=[("seqb", q)])
            if G["s"] is not None:
                S.add("pool", lambda e, q=q, ct=ct: e.tensor_copy(out=seqb[:, q, H + 512:H + 512 + H],
                                                                  in_=st_conv[:, ct * 30:(ct + 1) * 30]),
                      reads=[("st_conv",)], writes=[("seqb", q)])

            def glu(e, p=p, q=q, tA=tA):
                inst = None
                for bi, (c0, n) in enumerate(G["blocks"]):
                    for (lo, ln, so, co) in pieces(G, c0, n, H):
                        inst = e.tensor_tensor(out=seqb[:, q, so:so + ln], in0=psap(tA, bi, ln, off=lo - c0),
                                               in1=t32[:, p, lo:lo + ln], op=ALU.mult)
                return inst
            S.add("dve", glu, reads=[*PK(tA), ("t32", p)], writes=[("seqb", q)])

        def front2(ct):
            p = ct % 2
            q = ct % 4
            S.add("act", lambda e, p=p, q=q: e.activation(out=agb[:, p, 0:W], in_=seqb[:, q, 0:W], func=AF.Copy),
                  reads=[("seqb", q)], writes=[("agb", p)])
            S.add("pool", lambda e, q=q, ct=ct: e.tensor_copy(out=conv_hp[:, ct, :], in_=seqb[:, q, 512:512 + H]),
                  reads=[("seqb", q)], writes=[("conv_hp", ct)])
            if G["s"] is not None:
                S.add("pool", lambda e, q=q, ct=ct: e.tensor_copy(out=cso[:, ct * 30:(ct + 1) * 30],
                                                                  in_=seqb[:, q, offS + 2:offS + 32]),
                      reads=[("seqb", q)], writes=[("cso",)])

        def front(ct):
            front1(ct)
            front2(ct)

        acc_tile = {}

        def pe_taps(ct):
            p = ct % 2
            tAcc = tile_a[ct] if len(G["tiles"]) == 3 else next_tile(Gc)
            acc_tile[ct] = tAcc

            def pfn(e, p=p, tAcc=tAcc):
                inst = None
                for k in range(NPE):
                    for bi, (c0, n) in enumerate(cblocks):
                        inst = e.matmul(psap(tAcc, bi, n), lhsT=dg[:, DGOFF[p] + k * 128:DGOFF[p] + (k + 1) * 128],
                                        rhs=agb[:, p, k + c0:k + c0 + n], start=(k == 0), stop=(k == NPE - 1))
                return inst
            S.add("pe", pfn, reads=[("agb", p)] + DGK[p], writes=[*PK(tAcc)])

        def f0(ct):
            q = ct % 4
            tAcc = acc_tile[ct]
            k = NPE
            wcol = par[:, P_CONVW + ct * 31 + k:P_CONVW + ct * 31 + k + 1]

            def fn(e, q=q, ct=ct, wcol=wcol, k=k, tAcc=tAcc):
                inst = None
                for bi, (c0, n) in enumerate(cblocks):
                    inst = e.scalar_tensor_tensor(out=big[:, ct, c0:c0 + n], in0=seqb[:, q, k + c0:k + c0 + n],
                                                  scalar=wcol, in1=psap(tAcc, bi, n), op0=ALU.mult, op1=ALU.add)
                return inst
            S.add("dve", fn, reads=[("seqb", q), *PK(tAcc)], writes=[("big", ct)])

        def rest(cts, mid):
            ks = list(range(NPE + 1, 31))
            for i, k in enumerate(ks):
                if i == len(ks) // 2 and mid is not None:
                    mid()
                for ct in cts:
                    q = ct % 4
                    wcol = par[:, P_CONVW + ct * 31 + k:P_CONVW + ct * 31 + k + 1]
                    S.add("dve", lambda e, q=q, ct=ct, wcol=wcol, k=k: e.scalar_tensor_tensor(
                        out=big[:, ct, 0:Wc], in0=seqb[:, q, k:k + Wc], scalar=wcol,
                        in1=big[:, ct, 0:Wc], op0=ALU.mult, op1=ALU.add),
                        reads=[("seqb", q), ("big", ct)], writes=[("big", ct)])

        def tail(ct):
            p = ct % 2

            def cbf(e, p=p, ct=ct):
                inst = None
                for (lo, ln, so, co) in pieces(G, 0, T, H):
                    inst = e.activation(out=t16[:, p, lo:lo + ln], in_=big[:, ct, co:co + ln], func=AF.Identity,
                                        bias=par[:, P_CONVB + ct:P_CONVB + ct + 1], scale=1.0)
                return inst
            S.add("act", cbf, reads=[("big", ct)], writes=[("t16", p)])

            def mfn(e, p=p, ct=ct):
                inst = None
                for bi, (c0, n) in enumerate(G["blocks"]):
                    inst = e.matmul(psap(G["aux"], bi, n), lhsT=ones_e[:, :], rhs=t16[:, p, c0:c0 + n],
                                    start=(ct == 0), stop=(ct == 15))
                return inst
            S.add("pe", mfn, reads=[("t16", p)], writes=[*PK(G["aux"])])
            if onepass:
                S.add("act", lambda e, p=p, ct=ct: e.activation(out=dbuf[:, p, 0:T], in_=big[:, ct, 0:T], func=AF.Square,
                                                                bias=par[:, P_CONVB + ct:P_CONVB + ct + 1], scale=1.0),
                      reads=[("big", ct)], writes=[("d", p)])
                S.add("pe", lambda e, p=p, ct=ct: e.matmul(psap(aux2, 0, T), lhsT=ones_e[:, :], rhs=dbuf[:, p, 0:T],
                                                           start=(ct == 0), stop=(ct == 15)),
                      reads=[("d", p)], writes=[*PK(aux2)])

        front1(0)
        front1(1)
        front2(0)
        front2(1)
        pe_taps(0)
        pe_taps(1)
        f0(0)
        f0(1)
        for pr in range(8):
            c0, c1 = 2 * pr, 2 * pr + 1
            mid = None
            if pr < 7:
                front1(c0 + 2)
                front1(c1 + 2)
                front2(c0 + 2)
                front2(c1 + 2)
                pe_taps(c0 + 2)
                pe_taps(c1 + 2)
                mid = (lambda a=c0 + 2, b=c1 + 2: (f0(a), f0(b)))
            if pr > 0:
                tail(c0 - 2)
                tail(c1 - 2)
            rest([c0, c1], mid)
        tail(14)
        tail(15)
        S.tag = (g, 'L1b')

        def mevac(e):
            inst = None
            for bi, (c0, n) in enumerate(G["blocks"]):
                for (lo, ln, so, co) in pieces(G, c0, n, H):
                    inst = e.activation(out=sA[:, co:co + ln], in_=psap(G["aux"], bi, ln, off=lo - c0), func=AF.Copy)
            return inst
        S.add("act", mevac, reads=[*PK(G["aux"])], writes=[("sA",)])
        if onepass:
            S.add("act", lambda e: e.activation(out=sB[:, 0:T], in_=psap(aux2, 0, T), func=AF.Copy),
                  reads=[*PK(aux2)], writes=[("sB",)])
            S.add("dve", lambda e: e.tensor_tensor(out=rs[:, 0:T], in0=sA[:, 0:T], in1=sA[:, 0:T], op=ALU.mult),
                  reads=[("sA",)], writes=[("rs",)])
            S.add("dve", lambda e: e.tensor_tensor(out=rs[:, 0:T], in0=sB[:, 0:T], in1=rs[:, 0:T], op=ALU.subtract),
                  reads=[("sB",), ("rs",)], writes=[("rs",)])
            S.add("dve", lambda e: e.tensor_scalar(out=rs[:, 0:T], in0=rs[:, 0:T], scalar1=0.0, scalar2=None, op0=ALU.max),
                  reads=[("rs",)], writes=[("rs",)])
            S.add("act", lambda e: e.activation(out=rs[:, 0:T], in_=rs[:, 0:T], func=AF.Sqrt, bias=epsb[:, 1:2], scale=1.0),
                  reads=[("rs",)], writes=[("rs",)])
        for ct in (range(16) if not onepass else []):
            p = ct % 2
            S.add("dve", lambda e, ct=ct: e.scalar_tensor_tensor(out=big[:, ct, 0:Wc], in0=big[:, ct, 0:Wc],
                                                                 scalar=par[:, P_CONVB + ct:P_CONVB + ct + 1],
                                                                 in1=sA[:, 0:Wc], op0=ALU.add, op1=ALU.subtract),
                  reads=[("big", ct), ("sA",)], writes=[("big", ct)])

            def sqf(e, p=p, ct=ct):
                inst = None
                for (lo, ln, so, co) in pieces(G, 0, T, H):
                    inst = e.activation(out=t16[:, p, lo:lo + ln], in_=big[:, ct, co:co + ln], func=AF.Square)
                return inst
            S.add("act", sqf, reads=[("big", ct)], writes=[("t16", p)])

            def vfn(e, p=p, ct=ct):
                inst = None
                for bi, (c0, n) in enumerate(G["blocks"]):
                    inst = e.matmul(psap(G["aux"], bi, n), lhsT=ones_e[:, :], rhs=t16[:, p, c0:c0 + n],
                                    start=(ct == 0), stop=(ct == 15))
                return inst
            S.add("pe", vfn, reads=[("t16", p)], writes=[*PK(G["aux"])])

        def revac(e):
            inst = None
            for bi, (c0, n) in enumerate(G["blocks"]):
                for (lo, ln, so, co) in pieces(G, c0, n, H):
                    inst = e.activation(out=rs[:, co:co + ln], in_=psap(G["aux"], bi, ln, off=lo - c0), func=AF.Sqrt,
                                        bias=epsb[:, 1:2], scale=1.0)
            return inst
        if not onepass:
            S.add("act", revac, reads=[*PK(G["aux"])], writes=[("rs",)])
        S.add("dve", lambda e: e.reciprocal(out=rs[:, 0:Wc], in_=rs[:, 0:Wc]), reads=[("rs",)], writes=[("rs",)])
        S.tag = (g, 'L1c')
        for pr in range(8):
            slz = use_chunk(gbase + cbase + 16 + pr)
            for m2 in range(2):
                ct = 2 * pr + m2
                tZ = next_tile(Gc)
                inproj(G, slz, m2, tZ)
                p = state["par"]
                state["par"] ^= 1
                blockwise("act", G, lambda e, bi, c0, n, p=p, tZ=tZ: e.activation(
                    out=t32[:, p, c0:c0 + n], in_=psap(tZ, bi, n), func=AF.Silu),
                    reads=[*PK(tZ)], writes=[("t32", p)])
                if onepass:
                    S.add("dve", lambda e, ct=ct: e.scalar_tensor_tensor(out=big[:, ct, 0:Wc], in0=big[:, ct, 0:Wc],
                                                                         scalar=par[:, P_CONVB + ct:P_CONVB + ct + 1],
                                                                         in1=sA[:, 0:Wc], op0=ALU.add, op1=ALU.subtract),
                          reads=[("big", ct), ("sA",)], writes=[("big", ct)])
                S.add("dve", lambda e, ct=ct: e.tensor_tensor(out=big[:, ct, 0:Wc], in0=big[:, ct, 0:Wc],
                                                              in1=rs[:, 0:Wc], op=ALU.mult),
                      reads=[("big", ct), ("rs",)], writes=[("big", ct)])
                S.add("act", lambda e, ct=ct: e.activation(out=big[:, ct, 0:Wc], in_=big[:, ct, 0:Wc], func=AF.Silu,
                                                           bias=par[:, P_CONVNB + ct:P_CONVNB + ct + 1],
                                                           scale=par[:, P_CONVNG + ct:P_CONVNG + ct + 1]),
                      reads=[("big", ct)], writes=[("big", ct)])

                def ymf(e, p=p, ct=ct):
                    inst = None
                    for (lo, ln, so, co) in pieces(G, 0, T, H):
                        inst = e.tensor_tensor(out=ym[:, ct, lo:lo + ln], in0=big[:, ct, co:co + ln],
                                               in1=t32[:, p, lo:lo + ln], op=ALU.mult)
                    return inst
                S.add("pool", ymf, reads=[("big", ct), ("t32", p)], writes=[("ym", ct)])
        if G["s"] is not None:
            s = G["s"]
            S.add("sp", lambda e, s=s: [e.dma_start(out=o_convs[s], in_=cso[:, :])],
                  reads=[("cso",)], writes=[("o_convs", s)], dkey=("cso",))
        S.tag = (g, S.tag[1][:2] + 'o')
        outproj(G, gbase, cbase + 24)

    def sgu_layer(G, g, gbase, cbase, l):
        T = G["T"]
        rmsnorm(G, P_NORMG + 8 * l, lambda k: h[:, k, 0:T], lambda k: ("h", k))
        state["fine"] = 2
        cv = [gbase + cbase + i for i in range(8)]
        slv = [use_chunk(c, hold_from=cv[0]) for c in cv]
        tchunks = [(128 * i, 128) for i in range(4)] + ([(512, 32)] if G["s"] is not None else [])
        stg_info = {}

        Gv = dict(G)
        Gv["tiles"] = [(b,) for b in range(8)]

        def vstage(tci):
            tok0, nt = tchunks[tci]
            nb = 1
            tl = [next_tile(Gv) for _ in range(4)]
            bs = tci % 2
            if bs == 0:
                VR, VNB = vraw, dbuf2
                VQ = [[("vraw", q)] for q in range(4)]
                DK = [("d", k) for k in range(4)]
            else:
                VR, VNB = vrawB, vnbB
                VQ = [[("ym", c) for c in range(8)] for q in range(4)]
                DK = [("ym", c) for c in range(7, 12)]
            VK = sorted(set(k for q in range(4) for k in VQ[q]))
            for ti, tile in enumerate(tl):
                qs = [ti * nb + b for b in range(nb)]

                def vfn(e, qs=qs, tok0=tok0, nt=nt, tile=tile):
                    inst = None
                    for bi, q in enumerate(qs):
                        for c2 in range(2):
                            cc = 2 * q + c2
                            for k in range(8):
                                inst = e.matmul(psap(tile, bi, 256, p0=0, p1=nt, off=c2 * 256),
                                                lhsT=h[:, k, tok0:tok0 + nt],
                                                rhs=slots[:, slv[cc], k * 256:(k + 1) * 256],
                                                start=(k == 0), stop=(k == 7))
                    return inst
                S.add("pe", vfn, reads=[("slot", slv[2 * q + c2]) for q in qs for c2 in range(2)] + [("h", k) for k in range(8)],
                      writes=[*PK(tile)])

                def vev(e, qs=qs, nt=nt, tile=tile, VR=VR):
                    inst = None
                    for bi, q in enumerate(qs):
                        inst = e.activation(out=VR[0:nt, q * 512:(q + 1) * 512], in_=psap(tile, bi, 512, p0=0, p1=nt),
                                            func=AF.Copy)
                    return inst
                S.add("act", vev, reads=[*PK(tile)], writes=sorted(set(k for q in qs for k in VQ[q])))
            for q in range(4):
                S.add("dve", lambda e, q=q, nt=nt, VR=VR, bs=bs: e.bn_stats(out=bst[0:nt, bs, q * 6:(q + 1) * 6],
                                                                         in_=VR[0:nt, q * 512:(q + 1) * 512]),
                      reads=VQ[q], writes=[("bst", bs, q)])
            S.add("dve", lambda e, nt=nt, bs=bs: e.bn_aggr(out=mv[0:nt, bs, 0:2], in_=bst[0:nt, bs, 0:24]),
                  reads=[("bst", bs, q) for q in range(4)], writes=[("mv", bs)])
            S.add("act", lambda e, nt=nt, bs=bs: e.activation(out=mv[0:nt, bs, 2:3], in_=mv[0:nt, bs, 1:2], func=AF.Sqrt,
                                                              bias=epsb[0:nt, 1:2], scale=1.0),
                  reads=[("mv", bs)], writes=[("mv2", bs)])
            S.add("dve", lambda e, nt=nt, bs=bs: e.reciprocal(out=mv[0:nt, bs, 3:4], in_=mv[0:nt, bs, 2:3]),
                  reads=[("mv2", bs)], writes=[("mv3", bs)])
            S.add("dve", lambda e, nt=nt, VR=VR, bs=bs: e.scalar_tensor_tensor(
                out=VR[0:nt, :], in0=VR[0:nt, :], scalar=mv[0:nt, bs, 0:1], in1=bc[0:nt, B_SGG:B_SGG + 2048],
                op0=ALU.subtract, op1=ALU.mult),
                reads=VK + [("mv", bs)], writes=VK)
            need_f32 = (nt == 32) or (g == ngroups - 1 and tci == 3)
            if need_f32:
                S.add("dve", lambda e, nt=nt, VR=VR, bs=bs: e.scalar_tensor_tensor(
                    out=VR[0:nt, :], in0=VR[0:nt, :], scalar=mv[0:nt, bs, 3:4], in1=bc[0:nt, B_SGB:B_SGB + 2048],
                    op0=ALU.mult, op1=ALU.add),
                    reads=VK + [("mv3", bs)], writes=VK)
                S.add("act", lambda e, nt=nt, VR=VR, VNB=VNB: e.activation(out=VNB[0:nt, 0:2048], in_=VR[0:nt, :], func=AF.Copy),
                      reads=VK, writes=DK)
            else:
                S.add("dve", lambda e, nt=nt, VR=VR, VNB=VNB, bs=bs: e.scalar_tensor_tensor(
                    out=VNB[0:nt, 0:2048], in0=VR[0:nt, :], scalar=mv[0:nt, bs, 3:4], in1=bc[0:nt, B_SGB:B_SGB + 2048],
                    op0=ALU.mult, op1=ALU.add),
                    reads=VK + [("mv3", bs)], writes=DK)
            if nt == 32:
                s = G["s"]
                S.add("sp", lambda e, s=s, VR=VR: [e.dma_start(out=o_sgus[s], in_=VR[0:32, :])],
                      reads=VK, writes=[("o_sgus", s)], dkey=("vraw", bs))
            elif g == ngroups - 1 and tci == 3:
                S.add("sp", lambda e, VR=VR: [e.dma_start(out=o_sgup[:, :], in_=VR[:, :])],
                      reads=VK, writes=[("o_sgup",)], dkey=("vraw", bs))
            stg_info[tci] = (tok0, nt, VNB, DK)

        def sstage(tci):
            tok0, nt, VNB, DK = stg_info[tci]
            for hd in range(4):
                tile = next_tile(Gv)

                def sfn(e, hd=hd, nt=nt, tile=tile, VNB=VNB):
                    inst = None
                    for m in range(4):
                        inst = e.matmul(psap(tile, 0, nt, off=m * 128),
                                        lhsT=VNB[0:nt, (4 * hd + m) * 128:(4 * hd + m + 1) * 128],
                                        rhs=wmT[0:nt, hd, 0:nt], start=True, stop=True)
                    return inst
                S.add("pe", sfn, reads=DK + [("wmT",)], writes=[*PK(tile)])

                def sev(e, hd=hd, nt=nt, tok0=tok0, tile=tile):
                    inst = None
                    for m in range(4):
                        inst = e.tensor_tensor(out=big[:, 4 * hd + m, tok0:tok0 + nt], in0=psap(tile, 0, nt, off=m * 128),
                                               in1=bc[:, B_BSB + hd * 128:B_BSB + hd * 128 + nt], op=ALU.add)
                    return inst
                S.add("dve", sev, reads=[*PK(tile)], writes=[("big", 4 * hd + m) for m in range(4)])
        vstage(0)
        for tci in range(len(tchunks)):
            if tci + 1 < len(tchunks):
                vstage(tci + 1)
            sstage(tci)
        S.tag = (g, 'L2u')
        for pr in range(8):
            slu = use_chunk(gbase + cbase + 8 + 2 * pr)
            slz = use_chunk(gbase + cbase + 8 + 2 * pr + 1)
            for m2 in range(2):
                ct = 2 * pr + m2
                tU = next_tile(G)
                inproj(G, slu, m2, tU)
                tZ = next_tile(G)
                inproj(G, slz, m2, tZ)
                p = state["par"]
                state["par"] ^= 1
                blockwise("act", G, lambda e, bi, c0, n, p=p, tZ=tZ: e.activation(
                    out=t32[:, p, c0:c0 + n], in_=psap(tZ, bi, n), func=AF.Silu),
                    reads=[*PK(tZ)], writes=[("t32", p)])
                blockwise("dve", G, lambda e, bi, c0, n, p=p, tU=tU: e.tensor_tensor(
                    out=t32[:, p, c0:c0 + n], in0=psap(tU, bi, n), in1=t32[:, p, c0:c0 + n], op=ALU.mult),
                    reads=[*PK(tU), ("t32", p)], writes=[("t32", p)])
                S.add("pool", lambda e, p=p, ct=ct: e.tensor_tensor(out=ym[:, ct, 0:T], in0=t32[:, p, 0:T],
                                                                    in1=big[:, ct, 0:T], op=ALU.mult),
                      reads=[("t32", p), ("big", ct)], writes=[("ym", ct)])
        S.tag = (g, S.tag[1][:2] + 'o')
        outproj(G, gbase, cbase + 24)

    S.add("sp", lambda e: [e.dma_start(out=par[:, :], in_=params[:, :])], writes=[("par",)], dkey=("par",))
    S.add("sp", lambda e: [e.dma_start(out=bc[:, :], in_=bcin[:, :])], writes=[("bc",)], dkey=("bc",))
    S.add("sp", lambda e: [e.dma_start(out=sgw32[:, :, :], in_=sgwT[:, :, :])], writes=[("sgw32",)], dkey=("sgw32",))
    S.add("pool", lambda e: e.memset(ones_d[:, :], 1.0 / D), writes=[("ones",)])
    S.add("pool", lambda e: e.memset(ones_e[:, :], 1.0 / E), writes=[("ones",)])
    S.add("pool", lambda e: e.memset(epsb[:, 0:1], RMS_EPS), writes=[("eps",)])
    S.add("pool", lambda e: e.memset(epsb[:, 1:2], LN_EPS), writes=[("eps",)])
    S.add("pool", lambda e: e.memset(pool_hp[:, :, :, :], 0.0), writes=[("pool_hp", j, ct) for j in range(2) for ct in range(16)])
    S.add("pool", lambda e: e.memset(conv_hp[:, :, :], 0.0), writes=[("conv_hp", ct) for ct in range(16)])
    S.add("pool", lambda e: e.memset(seqb[:, :, :], 0.0), writes=[("seqb", q) for q in range(4)])
    S.add("pool", lambda e: e.memset(sA[:, :], 0.0), writes=[("sA",)])
    S.add("pool", lambda e: e.memset(sB[:, :], 0.0), writes=[("sB",)])
    S.add("pool", lambda e: e.memset(rs[:, :], 1.0), writes=[("rs",)])
    S.add("pool", lambda e: e.memset(big[:, :, :], 0.0), writes=[("big", ct) for ct in range(16)])
    for hd in range(4):
        S.add("dve", lambda e, hd=hd: e.tensor_tensor(out=wmT[:, hd, :], in0=sgw32[:, hd, :],
                                                      in1=bc[:, B_MASK:B_MASK + 128], op=ALU.mult),
              reads=[("sgw32",), ("bc",)], writes=[("wmT",)])
    S.add("dve", lambda e: e.tensor_copy(out=ident_bf[:, :], in_=bc[:, B_IDENT:B_IDENT + 128]),
          reads=[("bc",)], writes=[("ident",)])
    S.add("pool", lambda e: e.memset(agb[:, :, :], 0.0), writes=[("agb", 0), ("agb", 1)])
    CONST_KEYS = [("par",), ("bc",), ("ones",), ("eps",), ("ident",)]
    for eng in ("pe", "act", "dve", "pool"):
        S.add(eng, None, reads=CONST_KEYS, writes=[])

    def load_x(g):
        G = gstruct(g)
        for k in range(8):
            if G["xi"] == 0:
                S.add("sp", lambda e, g=g, k=k: [e.dma_start(out=x[:, k, 0:512], in_=xT[k, :, g * 512:(g + 1) * 512])],
                      writes=[("x", k)], dkey=("x", k))
            else:
                S.add("sp", lambda e, g=g, k=k: [e.dma_start(out=stg2d[:, k * 512:(k + 1) * 512], in_=xT[k, :, g * 512:(g + 1) * 512])],
                      reads=[("stg", 0), ("stg", 1)], writes=[("xo", k), ("stg", 0), ("stg", 1)], dkey=("x", k))
        if G["s"] is not None:
            s = G["s"]
            S.add("sp", lambda e, s=s: [e.dma_start(out=x[:, :, 512:544],
                                                      in_=xT[:, :, SEQ + s * 32:SEQ + (s + 1) * 32].rearrange("k p t -> p k t"))],
                  writes=[("x", k) for k in range(8)], dkey=("xs",))
            S.add("sp", lambda e, s=s: [e.dma_start(out=st_pool[:, :, :], in_=stpool[:, s, :, :].rearrange("j p f -> p j f"))],
                  writes=[("st_pool",)], dkey=("st_pool",))
            S.add("sp", lambda e, s=s: [e.dma_start(out=st_conv[:, :], in_=stconv[s])],
                  writes=[("st_conv",)], dkey=("st_conv",))

    loaded_x = set()
    early_norm = set()
    for g in range(ngroups):
        G = gstruct(g)
        T = G["T"]
        gbase = g * NCH
        if g not in loaded_x:
            load_x(g)
            loaded_x.add(g)
        S.tag = (g, 'L0')
        if depth >= 1:
            pool_layer(G, g, gbase, L0, 0, 0, skip_norm=(g in early_norm))
        S.tag = (g, 'L1')
        if depth >= 2:
            conv_layer(G, g, gbase, L1, 1)
        S.tag = (g, 'L2')
        if depth >= 3:
            sgu_layer(G, g, gbase, L2, 2)
        S.tag = (g, 'L3')
        if depth >= 4:
            hook = None
            if PINGPONG and g >= 1 and g + 1 < ngroups:
                load_x(g + 1)
                loaded_x.add(g + 1)

                def hook(gn=g + 1):
                    Gn = gstruct(gn)
                    Tn = Gn["T"]
                    rmsnorm(Gn, P_NORMG, lambda k, Tn=Tn: h[:, k, 0:Tn], lambda k: ("h", k))
                    state["fine"] = 2
                    early_norm.add(gn)
            pool_layer(G, g, gbase, L3, 1, 3, pre_out=hook)
        S.tag = (g, 'FIN')
        rmsnorm(G, P_FINALG, lambda k, T=T: big[:, k, 0:T], lambda k: ("big", k), sq=ym, sqk="ym")
        for k in range(8):
            S.add("sp", lambda e, g=g, k=k: [e.dma_start(out=yT[k, :, g * 512:(g + 1) * 512], in_=big[:, k, 0:512])],
                  reads=[("big", k)], writes=[("yT", g, k)], dkey=("yout", k))
        if G["s"] is not None:
            s = G["s"]
            S.add("sp", lambda e, s=s: [e.dma_start(out=yT[:, :, SEQ + s * 32:SEQ + (s + 1) * 32].rearrange("k p t -> p k t"),
                                                      in_=big[:, 0:8, 512:544])],
                  reads=[("big", k) for k in range(8)], writes=[("yTs", s)], dkey=("youts",))
    S.add("sp", lambda e: [e.dma_start(out=o_poolp[:, :, :].rearrange("j p f -> p j f"),
                                       in_=pool_hp[:, :, :, :].rearrange("p j c r -> p j (c r)"))],
          reads=[("pool_hp", j, ct) for j in range(2) for ct in range(16)], writes=[("o_poolp",)], dkey=("ohp",))
    S.add("sp", lambda e: [e.dma_start(out=o_convp[:, :], in_=conv_hp[:, :, :].rearrange("p c r -> p (c r)"))],
          reads=[("conv_hp", ct) for ct in range(16)], writes=[("o_convp",)], dkey=("ohp",))

    ops = S.ops
    for op in ops:
        for d in op.deps:
            if d.dkey is None and not (d.eng == "pe" and op.eng == "pe"):
                d.mark = True
    cnt = {}
    for op in ops:
        if op.mark:
            cnt[op.eng] = cnt.get(op.eng, 0) + 1
            op.markno = cnt[op.eng]

    engs = ["pe", "act", "dve", "pool", "sp"]
    esem = {en: es.enter_context(nc.semaphore("sem_" + en)) for en in engs}
    dsem = {}
    for k in S.dcount:
        dsem[k] = es.enter_context(nc.semaphore("dsem%d" % len(dsem)))

    blk = es.enter_context(nc.Block())

    def emit(engname, handle):
        waited = {}
        for op in ops:
            if op.eng != engname:
                continue
            for d in sorted(op.deps, key=lambda o: o.idx):
                if d.dkey is not None:
                    sem, val = dsem[d.dkey], d.dval
                else:
                    if d.eng == "pe" and engname == "pe":
                        continue
                    if d.fn is None:
                        continue
                    sem, val = esem[d.eng], d.markno
                key = id(sem)
                if waited.get(key, 0) >= val:
                    continue
                waited[key] = val
                handle.wait_ge(sem, val)
            if op.fn is None:
                continue
            n0 = nc.n_instructions()
            r = op.fn(handle)
            DBG_TAGS.setdefault(engname, []).append((op.tag, nc.n_instructions() - n0))
            if op.dkey is not None:
                for inst in r:
                    inst.then_inc(dsem[op.dkey], 16)
            elif op.mark:
                r.then_inc(esem[engname], 1)
        if engname == "sp":
            for k, n in S.dcount.items():
                handle.wait_ge(dsem[k], 16 * n)

    @blk.tensor
    def _(e):
        emit("pe", e)

    @blk.scalar
    def _(e):
        emit("act", e)

    @blk.vector
    def _(e):
        emit("dve", e)

    @blk.gpsimd
    def _(e):
        emit("pool", e)

    @blk.sync
    def _(e):
        emit("sp", e)

    es.close()
    return nc


def _in_chunk(Wm, col0):
    blk = Wm[:, col0:col0 + 256].reshape(8, 128, 256).transpose(1, 0, 2)
    return np.ascontiguousarray(blk).reshape(128, 2048)


def _out_chunk(Wm, m):
    blk = Wm[:, m * 128:(m + 1) * 128].reshape(16, 128, 128).transpose(1, 0, 2)
    return np.ascontiguousarray(blk).reshape(128, 2048)


def _grp_chunk(Wg):
    blk = Wg.reshape(4, 128, 512).transpose(1, 0, 2)
    return np.ascontiguousarray(blk).reshape(128, 2048)


def _pack_weights(pool_in_w, pool_w, pool_out_w, conv_in_w, conv_out_w, sgu_in_w, sgu_out_w):
    ch = []

    def pool(j):
        for gq in range(4):
            ch.append(_in_chunk(pool_in_w[j], 512 * gq))
            ch.append(_in_chunk(pool_in_w[j], 512 * gq + 256))
            ch.append(_in_chunk(pool_in_w[j], E + 512 * gq))
            ch.append(_in_chunk(pool_in_w[j], E + 512 * gq + 256))
            ch.append(_grp_chunk(pool_w[j, gq]))
        for m in range(8):
            ch.append(_out_chunk(pool_out_w[j], m))
    pool(0)
    for pr in range(8):
        ch.append(_in_chunk(conv_in_w[0], 256 * pr))
        ch.append(_in_chunk(conv_in_w[0], E + 256 * pr))
    for pr in range(8):
        ch.append(_in_chunk(conv_in_w[0], 2 * E + 256 * pr))
    for m in range(8):
        ch.append(_out_chunk(conv_out_w[0], m))
    for i in range(8):
        ch.append(_in_chunk(sgu_in_w[0], E + 256 * i))
    for pr in range(8):
        ch.append(_in_chunk(sgu_in_w[0], 256 * pr))
        ch.append(_in_chunk(sgu_in_w[0], 2 * E + 256 * pr))
    for m in range(8):
        ch.append(_out_chunk(sgu_out_w[0], m))
    pool(1)
    assert len(ch) == NCH
    return np.stack(ch).astype(np.float32)


def _colvec(v, nt):
    return np.asarray(v, np.float32).reshape(nt, 128).T


_PROG = {}
DBG_TAGS = {}


def kernel(x_prompt, x_sample, state_pool, state_conv, norm_g, pool_in_w, pool_w, pool_scale, pool_out_w,
           conv_in_w, conv_w, conv_b, conv_norm_g, conv_norm_b, conv_out_w,
           sgu_in_w, sgu_norm_g, sgu_norm_b, sgu_w, sgu_b, sgu_out_w, final_g, _depth=4, _ngroups=NG):
    f = lambda a: np.asarray(a, np.float32)
    x_prompt, x_sample, state_pool, state_conv = f(x_prompt), f(x_sample), f(state_pool), f(state_conv)
    wst = _pack_weights(f(pool_in_w), f(pool_w), f(pool_out_w), f(conv_in_w), f(conv_out_w), f(sgu_in_w), f(sgu_out_w))
    par = np.zeros((128, NPAR), np.float32)
    for l in range(4):
        par[:, P_NORMG + 8 * l:P_NORMG + 8 * l + 8] = _colvec(f(norm_g)[l], 8)
    par[:, P_FINALG:P_FINALG + 8] = _colvec(f(final_g), 8)
    for j in range(2):
        par[:, P_PSCALE + 16 * j:P_PSCALE + 16 * j + 16] = _colvec(f(pool_scale)[j], 16)
    cw = f(conv_w)[0]
    par[:, P_CONVW:P_CONVW + 496] = cw.T.reshape(16, 128, 31).transpose(1, 0, 2).reshape(128, 496)
    par[:, P_CONVB:P_CONVB + 16] = _colvec(f(conv_b)[0], 16)
    par[:, P_CONVNG:P_CONVNG + 16] = _colvec(f(conv_norm_g)[0], 16)
    par[:, P_CONVNB:P_CONVNB + 16] = _colvec(f(conv_norm_b)[0], 16)
    bcin = np.zeros((128, NBC), np.float32)
    bcin[:, B_SGG:B_SGG + 2048] = f(sgu_norm_g)[0][None, :]
    bcin[:, B_SGB:B_SGB + 2048] = f(sgu_norm_b)[0][None, :]
    bcin[:, B_BSB:B_BSB + 512] = f(sgu_b)[0].reshape(1, 512)
    jj = np.arange(128)
    bcin[:, B_MASK:B_MASK + 128] = (jj[:, None] <= jj[None, :]).astype(np.float32)
    for gq, w in enumerate(POOL_WINDOWS):
        t = np.arange(16)
        bcin[:, B_RATIO + 16 * gq:B_RATIO + 16 * gq + 16] = (w / np.minimum(t + 1, w)).astype(np.float32)[None, :]
    bcin[:, B_IDENT:B_IDENT + 128] = np.eye(128, dtype=np.float32)
    sgwT = np.ascontiguousarray(f(sgu_w)[0].transpose(2, 0, 1))

    in_maps = []
    for c in range(NCORE):
        xt = np.empty((8, 128, SEQ + NSAMP * SLEN), np.float32)
        xt[:, :, :SEQ] = x_prompt[c].T.reshape(8, 128, SEQ)
        xs = x_sample[NSAMP * c:NSAMP * (c + 1)].reshape(NSAMP * SLEN, D)
        xt[:, :, SEQ:] = xs.T.reshape(8, 128, NSAMP * SLEN)
        sp = state_pool[:, NSAMP * c:NSAMP * (c + 1)]
        sp = sp.transpose(0, 1, 3, 2).reshape(2, NSAMP, 16, 128, 15).transpose(0, 1, 3, 2, 4).reshape(2, NSAMP, 128, 240)
        sc = state_conv[0, NSAMP * c:NSAMP * (c + 1)]
        sc = sc.transpose(0, 2, 1).reshape(NSAMP, 16, 128, 30).transpose(0, 2, 1, 3).reshape(NSAMP, 128, 480)
        in_maps.append({"xT": xt, "wst": wst, "params": par, "bcin": bcin, "sgwT": sgwT,
                        "stpool": np.ascontiguousarray(sp), "stconv": np.ascontiguousarray(sc)})
    key = (_depth, _ngroups)
    if key not in _PROG:
        _PROG[key] = build_program(_depth, _ngroups)
    res = run_bass_kernel_spmd(_PROG[key], in_maps, core_ids=list(range(NCORE)))
    R = res.results

    y_prompt = np.empty((8, SEQ, D), np.float32)
    y_sample = np.empty((32, SLEN, D), np.float32)
    npp = np.empty((2, 8, 15, E), np.float32)
    nps = np.empty((2, 32, 15, E), np.float32)
    ncp = np.empty((1, 8, 30, E), np.float32)
    ncs = np.empty((1, 32, 30, E), np.float32)
    nsp = np.empty((1, 8, 128, E), np.float32)
    nss = np.empty((1, 32, SLEN, E), np.float32)
    for c in range(NCORE):
        r = R[c]
        yt = np.asarray(r["yT"]).reshape(D, SEQ + NSAMP * SLEN)
        y_prompt[c] = yt[:, :SEQ].T
        y_sample[NSAMP * c:NSAMP * (c + 1)] = yt[:, SEQ:].T.reshape(NSAMP, SLEN, D)
        pp = np.asarray(r["o_poolp"]).reshape(2, 128, 16, 15)
        npp[:, c] = pp.transpose(0, 3, 2, 1).reshape(2, 15, E)
        pq = np.asarray(r["o_pools"]).reshape(2, NSAMP, 128, 16, 15)
        nps[:, NSAMP * c:NSAMP * (c + 1)] = pq.transpose(0, 1, 4, 3, 2).reshape(2, NSAMP, 15, E)
        cp = np.asarray(r["o_convp"]).reshape(128, 16, 30)
        ncp[0, c] = cp.transpose(2, 1, 0).reshape(30, E)
        cq = np.asarray(r["o_convs"]).reshape(NSAMP, 128, 16, 30)
        ncs[0, NSAMP * c:NSAMP * (c + 1)] = cq.transpose(0, 3, 2, 1).reshape(NSAMP, 30, E)
        nsp[0, c] = np.asarray(r["o_sgup"]).reshape(128, E)
        nss[0, NSAMP * c:NSAMP * (c + 1)] = np.asarray(r["o_sgus"]).reshape(NSAMP, SLEN, E)
    return (y_prompt, y_sample, npp, nps, ncp, ncs, nsp, nss)
```

```python
import numpy as np
from contextlib import ExitStack
import concourse.bass as bass
import concourse.mybir as mybir
from concourse.bass_utils import run_bass_kernel_spmd

F32 = mybir.dt.float32
BF16 = mybir.dt.bfloat16
ALU = mybir.AluOpType
AF = mybir.ActivationFunctionType

D = 1024
E = 2048
SEQ = 4096
NCORE = 8
NSAMP = 4
SLEN = 32
TP = 512
NG = SEQ // TP
NCH = 120
NSLOT = 9
PF = 6
RMS_EPS = 1e-6
LN_EPS = 1e-5
POOL_WINDOWS = (2, 4, 8, 16)
BIGW = 512 + 30 + 32
SEQW = 512 + 60 + 32

P_NORMG = 0
P_FINALG = P_NORMG + 32
P_PSCALE = P_FINALG + 8
P_CONVW = P_PSCALE + 32
P_CONVB = P_CONVW + 496
P_CONVNG = P_CONVB + 16
P_CONVNB = P_CONVNG + 16
NPAR = P_CONVNB + 16
B_SGG = 0
B_SGB = B_SGG + 2048
B_BSB = B_SGB + 2048
B_MASK = B_BSB + 512
B_RATIO = B_MASK + 128
B_IDENT = B_RATIO + 64
NBC = B_IDENT + 128
PINGPONG = True
NPE = 19


class Op:
    __slots__ = ("eng", "fn", "deps", "mark", "markno", "dkey", "dval", "idx", "tag")


class Sched:
    def __init__(self):
        self.ops = []
        self.last_w = {}
        self.readers = {}
        self.dcount = {}
        self.tag = None

    def add(self, eng, fn, reads=(), writes=(), dkey=None, ndma=1):
        op = Op()
        op.eng, op.fn, op.mark, op.markno, op.dkey, op.dval = eng, fn, False, 0, dkey, 0
        op.idx = len(self.ops)
        op.tag = self.tag
        deps = set()
        for k in reads:
            w = self.last_w.get(k)
            if w is not None:
                deps.add(w)
        for k in writes:
            w = self.last_w.get(k)
            if w is not None:
                deps.add(w)
            for r in self.readers.get(k, ()):
                deps.add(r)
        wset = set(writes)
        for k in writes:
            self.last_w[k] = op
            self.readers[k] = []
        for k in reads:
            if k not in wset:
                self.readers.setdefault(k, []).append(op)
        deps.discard(op)
        op.deps = deps
        if dkey is not None:
            self.dcount[dkey] = self.dcount.get(dkey, 0) + ndma
            op.dval = 16 * self.dcount[dkey]
        self.ops.append(op)
        return op


def build_program(depth=4, ngroups=NG):
    nc = bass.Bass("TRN2", target_bir_lowering=False)
    S = Sched()

    def dram(name, shape, dt, kind):
        return nc.dram_tensor(name, shape, dt, kind=kind).ap()

    xT = dram("xT", [8, 128, SEQ + NSAMP * SLEN], F32, "ExternalInput")
    wst = dram("wst", [NCH, 128, 2048], F32, "ExternalInput")
    params = dram("params", [128, NPAR], F32, "ExternalInput")
    bcin = dram("bcin", [128, NBC], F32, "ExternalInput")
    sgwT = dram("sgwT", [128, 4, 128], F32, "ExternalInput")
    stpool = dram("stpool", [2, NSAMP, 128, 240], F32, "ExternalInput")
    stconv = dram("stconv", [NSAMP, 128, 480], F32, "ExternalInput")
    yT = dram("yT", [8, 128, SEQ + NSAMP * SLEN], F32, "ExternalOutput")
    o_poolp = dram("o_poolp", [2, 128, 240], F32, "ExternalOutput")
    o_pools = dram("o_pools", [2, NSAMP, 128, 240], F32, "ExternalOutput")
    o_convp = dram("o_convp", [128, 480], F32, "ExternalOutput")
    o_convs = dram("o_convs", [NSAMP, 128, 480], F32, "ExternalOutput")
    o_sgup = dram("o_sgup", [128, 2048], F32, "ExternalOutput")
    o_sgus = dram("o_sgus", [NSAMP, SLEN, 2048], F32, "ExternalOutput")
    wq = dram("wq", [NCH, 128, 2048], BF16, "Internal")
    dgq = dram("dgq", [16, 128, NPE * 128], BF16, "Internal")

    es = ExitStack()

    def sb(name, shape, dt):
        return es.enter_context(nc.sbuf_tensor(name, shape, dt))

    ps = es.enter_context(nc.psum_tensor("ps", [128, 4096], F32))
    x = sb("x", [128, 8, 544], F32)
    h = sb("h", [128, 8, 544], BF16)
    ym = sb("ym", [128, 16, 544], BF16)
    big = sb("big", [128, 16, BIGW], F32)
    slots = sb("slots", [128, NSLOT, 2048], BF16)
    stg = sb("stg", [128, 2, 2048], F32)
    par = sb("par", [128, NPAR], F32)
    bc = sb("bc", [128, NBC], F32)
    sgw32 = sb("sgw32", [128, 4, 128], F32)
    wmT = sb("wmT", [128, 4, 128], BF16)
    ones_d = sb("ones_d", [128, 128], BF16)
    ones_e = sb("ones_e", [128, 128], BF16)
    epsb = sb("epsb", [128, 2], F32)
    pool_hp = sb("pool_hp", [128, 2, 16, 15], F32)
    conv_hp = sb("conv_hp", [128, 16, 30], F32)
    st_pool = sb("st_pool", [128, 2, 240], F32)
    st_conv = sb("st_conv", [128, 480], F32)
    pso = sb("pso", [128, 2, 240], F32)
    cso = sb("cso", [128, 480], F32)
    seqb = sb("seqb", [128, 4, SEQW], F32)
    sA = sb("sA", [128, SEQW], F32)
    sB = sb("sB", [128, SEQW], F32)
    dbuf = sb("dbuf", [128, 4, 544], BF16)
    dbuf2 = dbuf[:, :, :].rearrange("p a b -> p (a b)")
    t32 = sb("t32", [128, 2, 544], F32)
    t16 = sb("t16", [128, 2, 544], BF16)
    rs = sb("rs", [128, BIGW], F32)
    vraw = sb("vraw", [128, 2048], F32)
    bst = sb("bst", [128, 2, 24], F32)
    agb = sb("agb", [128, 2, SEQW], BF16)
    ident_bf = sb("ident_bf", [128, 128], BF16)
    ym2d = ym[:, :, :].rearrange("p a b -> p (a b)")
    DGOFF = [0, 5 * 544]
    dg = ym2d
    stg2d = stg[:, :, :].rearrange("p a b -> p (a b)")
    DGK = [[("ym", c) for c in range(0, 5)], [("ym", c) for c in range(5, 10)]]
    assert NPE * 128 <= 5 * 544
    dgs = vraw[:, :].bitcast(BF16)
    vrawB = ym2d[:, 0:4096].bitcast(F32)
    vnbB = ym2d[:, 4096:6144]
    assert tuple(vrawB.shape) == (128, 2048), vrawB.shape
    mv = sb("mv", [128, 2, 4], F32)

    state = {"tile": 0, "loaded": 0, "par": 0, "fine": 0}

    def next_tile(G):
        n = len(G["tiles"])
        t = state["tile"] % n
        state["tile"] = (t + 1) % n
        return G["tiles"][t]

    def psap(tile, bi, n, p0=0, p1=128, off=0):
        b = tile[bi]
        return ps[p0:p1, b * 512 + off: b * 512 + off + n]

    def PK(tile):
        return [("psb", b) for b in tile]

    def gstruct(g):
        even = (g % 2 == 0)
        if even:
            return dict(T=544, segs=[(0, 512), (512, 32)], blocks=[(0, 272), (272, 272)], s=g // 2,
                        tiles=[(0, 1), (2, 3), (4, 5)], aux=(6, 7), xi=0, g=g)
        return dict(T=512, segs=[(0, 512)], blocks=[(0, 512)], s=None,
                    tiles=[(b,) for b in range(7)], aux=(7,), xi=(1 if PINGPONG else 0), g=g)

    def XAP(G, k, c0, n):
        if G["xi"] == 0:
            return x[:, k, c0:c0 + n]
        return stg2d[:, k * 512 + c0:k * 512 + c0 + n]

    def XK(G, k):
        return ("x", k) if G["xi"] == 0 else ("xo", k)

    def pieces(G, c0, n, H):
        out = []
        for si, (s0, sn) in enumerate(G["segs"]):
            lo, hi = max(c0, s0), min(c0 + n, s0 + sn)
            if lo < hi:
                seqoff = H + (0 if si == 0 else 512 + H) + (lo - s0)
                cloff = (0 if si == 0 else 512 + H) + (lo - s0)
                out.append((lo, hi - lo, seqoff, cloff))
        return out

    def issue_load(gi):
        ci = gi % NCH
        grp = gi // NCH
        sl = gi % NSLOT
        if grp == 0:
            st = gi % 2
            S.add("sp", lambda e, ci=ci, st=st: [e.dma_start(out=stg[:, st, :], in_=wst[ci])],
                  reads=[], writes=[("stg", st)], dkey=("stg", st))
            S.add("act", lambda e, st=st, sl=sl: e.activation(out=slots[:, sl, :], in_=stg[:, st, :], func=AF.Copy),
                  reads=[("stg", st)], writes=[("slot", sl)])
            S.add("act", lambda e, ci=ci, sl=sl: [e.dma_start(out=wq[ci], in_=slots[:, sl, :])],
                  reads=[("slot", sl)], writes=[("wq", ci)], dkey=("slot", sl))
        else:
            S.add("sp", lambda e, ci=ci, sl=sl: [e.dma_start(out=slots[:, sl, :], in_=wq[ci])],
                  reads=[("wq", ci)], writes=[("slot", sl)], dkey=("slot", sl))

    total_chunks = ngroups * NCH

    def use_chunk(gi, hold_from=None):
        upto = min(gi + PF, total_chunks - 1)
        if hold_from is not None:
            upto = min(upto, hold_from + NSLOT - 1)
        while state["loaded"] <= upto:
            if (state["loaded"] % NCH) < chunks_per_pass:
                issue_load(state["loaded"])
            state["loaded"] += 1
        return gi % NSLOT

    L0 = 0
    L1 = 28
    L2 = 60
    L3 = 92
    chunks_per_pass = [0, 28, 60, 92, 120][depth]

    def mm_tile(G, tile, nk, lhs_fn, rhs_fn, reads):
        def fn(e):
            inst = None
            for k in range(nk):
                for bi, (c0, n) in enumerate(G["blocks"]):
                    inst = e.matmul(psap(tile, bi, n), lhsT=lhs_fn(k), rhs=rhs_fn(k, c0, n),
                                    start=(k == 0), stop=(k == nk - 1))
            return inst
        S.add("pe", fn, reads=reads, writes=[*PK(tile)])

    def inproj(G, sl, m2, tile):
        if state["fine"] > 0:
            state["fine"] -= 1
            for k in range(8):
                def fn(e, k=k):
                    inst = None
                    for bi, (c0, n) in enumerate(G["blocks"]):
                        inst = e.matmul(psap(tile, bi, n), lhsT=slots[:, sl, k * 256 + m2 * 128: k * 256 + m2 * 128 + 128],
                                        rhs=h[:, k, c0:c0 + n], start=(k == 0), stop=(k == 7))
                    return inst
                S.add("pe", fn, reads=[("slot", sl), ("h", k)], writes=[*PK(tile)])
            return
        mm_tile(G, tile, 8,
                lambda k: slots[:, sl, k * 256 + m2 * 128: k * 256 + m2 * 128 + 128],
                lambda k, c0, n: h[:, k, c0:c0 + n],
                reads=[("slot", sl)] + [("h", k) for k in range(8)])

    def blockwise(eng, G, fn_block, reads, writes):
        def fn(e):
            inst = None
            for bi, (c0, n) in enumerate(G["blocks"]):
                inst = fn_block(e, bi, c0, n)
            return inst
        S.add(eng, fn, reads=reads, writes=writes)

    def rmsnorm(G, gcol0, out_fn, out_keys, sq=None, sqk="h"):
        T = G["T"]
        if sq is None:
            sq = h
        for k in range(8):
            S.add("act", lambda e, k=k: e.activation(out=sq[:, k, 0:T], in_=XAP(G, k, 0, T), func=AF.Square),
                  reads=[XK(G, k)], writes=[(sqk, k)])
        mm_tile(G, G["aux"], 8, lambda k: ones_d[:, :], lambda k, c0, n: sq[:, k, c0:c0 + n],
                reads=[(sqk, k) for k in range(8)])
        blockwise("act", G, lambda e, bi, c0, n: e.activation(out=rs[:, c0:c0 + n], in_=psap(G["aux"], bi, n),
                                                              func=AF.Sqrt, bias=epsb[:, 0:1], scale=1.0),
                  reads=[*PK(G["aux"])], writes=[("rs",)])
        S.add("dve", lambda e: e.reciprocal(out=rs[:, 0:T], in_=rs[:, 0:T]), reads=[("rs",)], writes=[("rs",)])
        for k in range(8):
            S.add("dve", lambda e, k=k: e.scalar_tensor_tensor(out=out_fn(k), in0=XAP(G, k, 0, T),
                                                               scalar=par[:, gcol0 + k: gcol0 + k + 1],
                                                               in1=rs[:, 0:T], op0=ALU.mult, op1=ALU.mult),
                  reads=[XK(G, k), ("rs",)], writes=[out_keys(k)])

    def outproj(G, gbase, c_first):
        for m in range(8):
            sl = use_chunk(gbase + c_first + m)
            tile = next_tile(G)
            mm_tile(G, tile, 16, lambda k, sl=sl: slots[:, sl, k * 128:(k + 1) * 128],
                    lambda k, c0, n: ym[:, k, c0:c0 + n],
                    reads=[("slot", sl)] + [("ym", k) for k in range(16)])
            blockwise("dve", G, lambda e, bi, c0, n, m=m, tile=tile: e.tensor_tensor(
                out=XAP(G, m, c0, n), in0=psap(tile, bi, n), in1=XAP(G, m, c0, n), op=ALU.add),
                reads=[*PK(tile), XK(G, m)], writes=[XK(G, m)])

    def gen_dg(ct):
        for k in range(NPE):
            wcol = par[:, P_CONVW + ct * 31 + k:P_CONVW + ct * 31 + k + 1]
            if k % 2 == 0:
                S.add("act", lambda e, k=k, wcol=wcol: e.activation(
                    out=dgs[:, k * 128:(k + 1) * 128], in_=ident_bf[:, :], func=AF.Copy, scale=wcol),
                    reads=[("ident",)], writes=[("dgs", k)])
            else:
                S.add("pool", lambda e, k=k, wcol=wcol: e.tensor_scalar(
                    out=dgs[:, k * 128:(k + 1) * 128], in0=ident_bf[:, :], scalar1=wcol, scalar2=1.0,
                    op0=ALU.mult, op1=ALU.mult),
                    reads=[("ident",)], writes=[("dgs", k)])
        S.add("act", lambda e, ct=ct: [e.dma_start(out=dgq[ct], in_=dgs[:, 0:NPE * 128])],
              reads=[("dgs", k) for k in range(NPE)], writes=[("dgq", ct)], dkey=("dgs",))

    def pool_layer(G, g, gbase, cbase, j, l, skip_norm=False, pre_out=None):
        H = 15
        T = G["T"]
        W = H + 512 + ((H + 32) if G["s"] is not None else 0)
        offS = H + 512 + H
        if not skip_norm:
            rmsnorm(G, P_NORMG + 8 * l, lambda k: h[:, k, 0:T], lambda k: ("h", k))
            state["fine"] = 2
        for gq in range(4):
            w = POOL_WINDOWS[gq]
            cu = [cbase + gq * 5 + 0, cbase + gq * 5 + 1]
            cz = [cbase + gq * 5 + 2, cbase + gq * 5 + 3]
            cg = cbase + gq * 5 + 4
            for m in range(4):
                ct = 4 * gq + m
                sl = use_chunk(gbase + cu[m // 2])
                tile = next_tile(G)
                inproj(G, sl, m % 2, tile)
                p = state["par"]
                state["par"] ^= 1
                ub = seqb[:, p, :]
                S.add("pool", lambda e, p=p, ct=ct: e.tensor_copy(out=seqb[:, p, 0:H], in_=pool_hp[:, j, ct, :]),
                      reads=[("pool_hp", j, ct)], writes=[("seqb", p)])
                if G["s"] is not None:
                    S.add("pool", lambda e, p=p, ct=ct: e.tensor_copy(out=seqb[:, p, H + 512:H + 512 + H],
                                                                      in_=st_pool[:, j, ct * 15:(ct + 1) * 15]),
                          reads=[("st_pool",)], writes=[("seqb", p)])

                def evac(e, p=p, tile=tile):
                    inst = None
                    for bi, (c0, n) in enumerate(G["blocks"]):
                        for (lo, ln, so, co) in pieces(G, c0, n, H):
                            inst = e.activation(out=seqb[:, p, so:so + ln], in_=psap(tile, bi, ln, off=lo - c0),
                                                func=AF.Copy)
                    return inst
                S.add("act", evac, reads=[*PK(tile)], writes=[("seqb", p)])
                S.add("pool", lambda e, p=p, ct=ct: e.tensor_copy(out=pool_hp[:, j, ct, :], in_=seqb[:, p, 512:512 + H]),
                      reads=[("seqb", p)], writes=[("pool_hp", j, ct)])
                if G["s"] is not None:
                    S.add("pool", lambda e, p=p, ct=ct: e.tensor_copy(out=pso[:, j, ct * 15:(ct + 1) * 15],
                                                                      in_=seqb[:, p, offS + 17:offS + 32]),
                          reads=[("seqb", p)], writes=[("pso", j)])
                S.add("dve", lambda e, p=p: e.tensor_tensor(out=sA[:, 1:W], in0=seqb[:, p, 1:W], in1=seqb[:, p, 0:W - 1],
                                                            op=ALU.add),
                      reads=[("seqb", p)], writes=[("sA",)])
                res, rk = sA, ("sA",)
                if w >= 4:
                    S.add("dve", lambda e: e.tensor_tensor(out=sB[:, 3:W], in0=sA[:, 3:W], in1=sA[:, 1:W - 2], op=ALU.add),
                          reads=[("sA",)], writes=[("sB",)])
                    res, rk = sB, ("sB",)
                if w >= 8:
                    S.add("dve", lambda e: e.tensor_tensor(out=sA[:, 7:W], in0=sB[:, 7:W], in1=sB[:, 3:W - 4], op=ALU.add),
                          reads=[("sB",)], writes=[("sA",)])
                    res, rk = sA, ("sA",)
                if w >= 16:
                    S.add("dve", lambda e: e.tensor_tensor(out=sB[:, 15:W], in0=sA[:, 15:W], in1=sA[:, 7:W - 8], op=ALU.add),
                          reads=[("sA",)], writes=[("sB",)])
                    res, rk = sB, ("sB",)
                if g == 0:
                    S.add("dve", lambda e, res=res, gq=gq: e.tensor_tensor(
                        out=res[:, H:H + 15], in0=res[:, H:H + 15],
                        in1=bc[:, B_RATIO + gq * 16:B_RATIO + gq * 16 + 15], op=ALU.mult),
                        reads=[rk], writes=[rk])

                def dfn(e, p=p, res=res, m=m, w=w):
                    inst = None
                    for (lo, ln, so, co) in pieces(G, 0, T, H):
                        inst = e.scalar_tensor_tensor(out=dbuf[:, m, lo:lo + ln], in0=res[:, so:so + ln],
                                                      scalar=1.0 / w, in1=seqb[:, p, so:so + ln],
                                                      op0=ALU.mult, op1=ALU.subtract)
                    return inst
                S.add("dve", dfn, reads=[rk, ("seqb", p)], writes=[("d", m)])
            if g == 0 and l == 0:
                for m in range(4):
                    gen_dg(4 * gq + m)
            for m in range(4):
                ct = 4 * gq + m
                sl = use_chunk(gbase + cz[m // 2])
                tile = next_tile(G)
                inproj(G, sl, m % 2, tile)
                blockwise("act", G, lambda e, bi, c0, n, ct=ct, tile=tile: e.activation(
                    out=big[:, ct, c0:c0 + n], in_=psap(tile, bi, n), func=AF.Silu),
                    reads=[*PK(tile)], writes=[("big", ct)])
            slg = use_chunk(gbase + cg)
            for m in range(4):
                ct = 4 * gq + m
                tile = next_tile(G)
                mm_tile(G, tile, 4, lambda k, m=m, slg=slg: slots[:, slg, k * 512 + m * 128:k * 512 + m * 128 + 128],
                        lambda k, c0, n: dbuf[:, k, c0:c0 + n],
                        reads=[("slot", slg)] + [("d", k) for k in range(4)])
                blockwise("dve", G, lambda e, bi, c0, n, ct=ct, tile=tile: e.scalar_tensor_tensor(
                    out=ym[:, ct, c0:c0 + n], in0=psap(tile, bi, n),
                    scalar=par[:, P_PSCALE + j * 16 + ct:P_PSCALE + j * 16 + ct + 1],
                    in1=big[:, ct, c0:c0 + n], op0=ALU.mult, op1=ALU.mult),
                    reads=[*PK(tile), ("big", ct)], writes=[("ym", ct)])
        if G["s"] is not None:
            s = G["s"]
            S.add("sp", lambda e, s=s: [e.dma_start(out=o_pools[j, s], in_=pso[:, j, :])],
                  reads=[("pso", j)], writes=[("o_pools", j, s)], dkey=("pso", j))
        S.tag = (g, S.tag[1][:2] + 'o')
        if pre_out is not None:
            pre_out()
        outproj(G, gbase, cbase + 20)

    def conv_layer(G, g, gbase, cbase, l):
        H = 30
        T = G["T"]
        W = H + 512 + ((H + 32) if G["s"] is not None else 0)
        Wc = W - H
        offS = H + 512 + H
        rmsnorm(G, P_NORMG + 8 * l, lambda k: h[:, k, 0:T], lambda k: ("h", k))
        state["fine"] = 2
        onepass = (len(G["tiles"][0]) == 1)
        Gc = dict(G)
        aux2 = None
        if onepass:
            Gc["tiles"] = G["tiles"][:6]
            aux2 = G["tiles"][6]
        cblocks = [(0, 287), (287, 287)] if G["s"] is not None else [(0, 512)]
        halves = [(0, Wc // 2), (Wc // 2, Wc - Wc // 2)]
        fr = {}
        tile_a = {}

        def front1(ct):
            pr, m2 = ct // 2, ct % 2
            sla = use_chunk(gbase + cbase + 2 * pr)
            slg = use_chunk(gbase + cbase + 2 * pr + 1)
            if len(G["tiles"]) == 3:
                tA = G["tiles"][0 if ct % 2 == 0 else 2]
                tG = G["tiles"][1]
            else:
                tA = next_tile(Gc)
                tG = next_tile(Gc)
            inproj(G, sla, m2, tA)
            inproj(G, slg, m2, tG)
            tile_a[ct] = tA
            p = ct % 2
            q = ct % 4
            S.add("sp", lambda e, p=p, ct=ct: [e.dma_start(out=dg[:, DGOFF[p]:DGOFF[p] + NPE * 128], in_=dgq[ct])],
                  reads=[("dgq", ct)], writes=DGK[p], dkey=("dg", p))
            blockwise("act", G, lambda e, bi, c0, n, p=p, tG=tG: e.activation(
                out=t32[:, p, c0:c0 + n], in_=psap(tG, bi, n), func=AF.Sigmoid),
                reads=[*PK(tG)], writes=[("t32", p)])
            S.add("pool", lambda e, q=q, ct=ct: e.tensor_copy(out=seqb[:, q, 0:H], in_=conv_hp[:, ct, :]),
                  reads=[("conv_hp", ct)], writes=[("seqb", q)])
            if G["s"] is not None:
                S.add("pool", lambda e, q=q, ct=ct: e.tensor_copy(out=seqb[:, q, H + 512:H + 512 + H],
                                                                  in_=st_conv[:, ct * 30:(ct + 1) * 30]),
                      reads=[("st_conv",)], writes=[("seqb", q)])

            def glu(e, p=p, q=q, tA=tA):
                inst = None
                for bi, (c0, n) in enumerate(G["blocks"]):
                    for (lo, ln, so, co) in pieces(G, c0, n, H):
                        inst = e.tensor_tensor(out=seqb[:, q, so:so + ln], in0=psap(tA, bi, ln, off=lo - c0),
                                               in1=t32[:, p, lo:lo + ln], op=ALU.mult)
                return inst
            S.add("dve", glu, reads=[*PK(tA), ("t32", p)], writes=[("seqb", q)])

        def front2(ct):
            p = ct % 2
            q = ct % 4
            S.add("act", lambda e, p=p, q=q: e.activation(out=agb[:, p, 0:W], in_=seqb[:, q, 0:W], func=AF.Copy),
                  reads=[("seqb", q)], writes=[("agb", p)])
            S.add("pool", lambda e, q=q, ct=ct: e.tensor_copy(out=conv_hp[:, ct, :], in_=seqb[:, q, 512:512 + H]),
                  reads=[("seqb", q)], writes=[("conv_hp", ct)])
            if G["s"] is not None:
                S.add("pool", lambda e, q=q, ct=ct: e.tensor_copy(out=cso[:, ct * 30:(ct + 1) * 30],
                                                                  in_=seqb[:, q, offS + 2:offS + 32]),
                      reads=[("seqb", q)], writes=[("cso",)])

        def front(ct):
            front1(ct)
            front2(ct)

        acc_tile = {}

        def pe_taps(ct):
            p = ct % 2
            tAcc = tile_a[ct] if len(G["tiles"]) == 3 else next_tile(Gc)
            acc_tile[ct] = tAcc

            def pfn(e, p=p, tAcc=tAcc):
                inst = None
                for k in range(NPE):
                    for bi, (c0, n) in enumerate(cblocks):
                        inst = e.matmul(psap(tAcc, bi, n), lhsT=dg[:, DGOFF[p] + k * 128:DGOFF[p] + (k + 1) * 128],
                                        rhs=agb[:, p, k + c0:k + c0 + n], start=(k == 0), stop=(k == NPE - 1))
                return inst
            S.add("pe", pfn, reads=[("agb", p)] + DGK[p], writes=[*PK(tAcc)])

        def f0(ct):
            q = ct % 4
            tAcc = acc_tile[ct]
            k = NPE
            wcol = par[:, P_CONVW + ct * 31 + k:P_CONVW + ct * 31 + k + 1]

            def fn(e, q=q, ct=ct, wcol=wcol, k=k, tAcc=tAcc):
                inst = None
                for bi, (c0, n) in enumerate(cblocks):
                    inst = e.scalar_tensor_tensor(out=big[:, ct, c0:c0 + n], in0=seqb[:, q, k + c0:k + c0 + n],
                                                  scalar=wcol, in1=psap(tAcc, bi, n), op0=ALU.mult, op1=ALU.add)
                return inst
            S.add("dve", fn, reads=[("seqb", q), *PK(tAcc)], writes=[("big", ct)])

        def rest(cts, mid):
            ks = list(range(NPE + 1, 31))
            for i, k in enumerate(ks):
                if i == len(ks) // 2 and mid is not None:
                    mid()
                for ct in cts:
                    q = ct % 4
                    wcol = par[:, P_CONVW + ct * 31 + k:P_CONVW + ct * 31 + k + 1]
                    S.add("dve", lambda e, q=q, ct=ct, wcol=wcol, k=k: e.scalar_tensor_tensor(
                        out=big[:, ct, 0:Wc], in0=seqb[:, q, k:k + Wc], scalar=wcol,
                        in1=big[:, ct, 0:Wc], op0=ALU.mult, op1=ALU.add),
                        reads=[("seqb", q), ("big", ct)], writes=[("big", ct)])

        def tail(ct):
            p = ct % 2

            def cbf(e, p=p, ct=ct):
                inst = None
                for (lo, ln, so, co) in pieces(G, 0, T, H):
                    inst = e.activation(out=t16[:, p, lo:lo + ln], in_=big[:, ct, co:co + ln], func=AF.Identity,
                                        bias=par[:, P_CONVB + ct:P_CONVB + ct + 1], scale=1.0)
                return inst
            S.add("act", cbf, reads=[("big", ct)], writes=[("t16", p)])

            def mfn(e, p=p, ct=ct):
                inst = None
                for bi, (c0, n) in enumerate(G["blocks"]):
                    inst = e.matmul(psap(G["aux"], bi, n), lhsT=ones_e[:, :], rhs=t16[:, p, c0:c0 + n],
                                    start=(ct == 0), stop=(ct == 15))
                return inst
            S.add("pe", mfn, reads=[("t16", p)], writes=[*PK(G["aux"])])
            if onepass:
                S.add("act", lambda e, p=p, ct=ct: e.activation(out=dbuf[:, p, 0:T], in_=big[:, ct, 0:T], func=AF.Square,
                                                                bias=par[:, P_CONVB + ct:P_CONVB + ct + 1], scale=1.0),
                      reads=[("big", ct)], writes=[("d", p)])
                S.add("pe", lambda e, p=p, ct=ct: e.matmul(psap(aux2, 0, T), lhsT=ones_e[:, :], rhs=dbuf[:, p, 0:T],
                                                           start=(ct == 0), stop=(ct == 15)),
                      reads=[("d", p)], writes=[*PK(aux2)])

        front1(0)
        front1(1)
        front2(0)
        front2(1)
        pe_taps(0)
        pe_taps(1)
        f0(0)
        f0(1)
        for pr in range(8):
            c0, c1 = 2 * pr, 2 * pr + 1
            mid = None
            if pr < 7:
                front1(c0 + 2)
                front1(c1 + 2)
                front2(c0 + 2)
                front2(c1 + 2)
                pe_taps(c0 + 2)
                pe_taps(c1 + 2)
                mid = (lambda a=c0 + 2, b=c1 + 2: (f0(a), f0(b)))
            if pr > 0:
                tail(c0 - 2)
                tail(c1 - 2)
            rest([c0, c1], mid)
        tail(14)
        tail(15)
        S.tag = (g, 'L1b')

        def mevac(e):
            inst = None
            for bi, (c0, n) in enumerate(G["blocks"]):
                for (lo, ln, so, co) in pieces(G, c0, n, H):
                    inst = e.activation(out=sA[:, co:co + ln], in_=psap(G["aux"], bi, ln, off=lo - c0), func=AF.Copy)
            return inst
        S.add("act", mevac, reads=[*PK(G["aux"])], writes=[("sA",)])
        if onepass:
            S.add("act", lambda e: e.activation(out=sB[:, 0:T], in_=psap(aux2, 0, T), func=AF.Copy),
                  reads=[*PK(aux2)], writes=[("sB",)])
            S.add("dve", lambda e: e.tensor_tensor(out=rs[:, 0:T], in0=sA[:, 0:T], in1=sA[:, 0:T], op=ALU.mult),
                  reads=[("sA",)], writes=[("rs",)])
            S.add("dve", lambda e: e.tensor_tensor(out=rs[:, 0:T], in0=sB[:, 0:T], in1=rs[:, 0:T], op=ALU.subtract),
                  reads=[("sB",), ("rs",)], writes=[("rs",)])
            S.add("dve", lambda e: e.tensor_scalar(out=rs[:, 0:T], in0=rs[:, 0:T], scalar1=0.0, scalar2=None, op0=ALU.max),
                  reads=[("rs",)], writes=[("rs",)])
            S.add("act", lambda e: e.activation(out=rs[:, 0:T], in_=rs[:, 0:T], func=AF.Sqrt, bias=epsb[:, 1:2], scale=1.0),
                  reads=[("rs",)], writes=[("rs",)])
        for ct in (range(16) if not onepass else []):
            p = ct % 2
            S.add("dve", lambda e, ct=ct: e.scalar_tensor_tensor(out=big[:, ct, 0:Wc], in0=big[:, ct, 0:Wc],
                                                                 scalar=par[:, P_CONVB + ct:P_CONVB + ct + 1],
                                                                 in1=sA[:, 0:Wc], op0=ALU.add, op1=ALU.subtract),
                  reads=[("big", ct), ("sA",)], writes=[("big", ct)])

            def sqf(e, p=p, ct=ct):
                inst = None
                for (lo, ln, so, co) in pieces(G, 0, T, H):
                    inst = e.activation(out=t16[:, p, lo:lo + ln], in_=big[:, ct, co:co + ln], func=AF.Square)
                return inst
            S.add("act", sqf, reads=[("big", ct)], writes=[("t16", p)])

            def vfn(e, p=p, ct=ct):
                inst = None
                for bi, (c0, n) in enumerate(G["blocks"]):
                    inst = e.matmul(psap(G["aux"], bi, n), lhsT=ones_e[:, :], rhs=t16[:, p, c0:c0 + n],
                                    start=(ct == 0), stop=(ct == 15))
                return inst
            S.add("pe", vfn, reads=[("t16", p)], writes=[*PK(G["aux"])])

        def revac(e):
            inst = None
            for bi, (c0, n) in enumerate(G["blocks"]):
                for (lo, ln, so, co) in pieces(G, c0, n, H):
                    inst = e.activation(out=rs[:, co:co + ln], in_=psap(G["aux"], bi, ln, off=lo - c0), func=AF.Sqrt,
                                        bias=epsb[:, 1:2], scale=1.0)
            return inst
        if not onepass:
            S.add("act", revac, reads=[*PK(G["aux"])], writes=[("rs",)])
        S.add("dve", lambda e: e.reciprocal(out=rs[:, 0:Wc], in_=rs[:, 0:Wc]), reads=[("rs",)], writes=[("rs",)])
        S.tag = (g, 'L1c')
        for ct in range(16):
            if onepass:
                S.add("dve", lambda e, ct=ct: e.scalar_tensor_tensor(out=big[:, ct, 0:Wc], in0=big[:, ct, 0:Wc],
                                                                     scalar=par[:, P_CONVB + ct:P_CONVB + ct + 1],
                                                                     in1=sA[:, 0:Wc], op0=ALU.add, op1=ALU.subtract),
                      reads=[("big", ct), ("sA",)], writes=[("big", ct)])
            S.add("dve", lambda e, ct=ct: e.tensor_tensor(out=big[:, ct, 0:Wc], in0=big[:, ct, 0:Wc],
                                                          in1=rs[:, 0:Wc], op=ALU.mult),
                  reads=[("big", ct), ("rs",)], writes=[("big", ct)])
        for pr in range(8):
            slz = use_chunk(gbase + cbase + 16 + pr)
            for m2 in range(2):
                ct = 2 * pr + m2
                tZ = next_tile(Gc)
                inproj(G, slz, m2, tZ)
                p = state["par"]
                state["par"] ^= 1
                blockwise("act", G, lambda e, bi, c0, n, p=p, tZ=tZ: e.activation(
                    out=t32[:, p, c0:c0 + n], in_=psap(tZ, bi, n), func=AF.Silu),
                    reads=[*PK(tZ)], writes=[("t32", p)])
                S.add("act", lambda e, ct=ct: e.activation(out=big[:, ct, 0:Wc], in_=big[:, ct, 0:Wc], func=AF.Silu,
                                                           bias=par[:, P_CONVNB + ct:P_CONVNB + ct + 1],
                                                           scale=par[:, P_CONVNG + ct:P_CONVNG + ct + 1]),
                      reads=[("big", ct)], writes=[("big", ct)])

                def ymf(e, p=p, ct=ct):
                    inst = None
                    for (lo, ln, so, co) in pieces(G, 0, T, H):
                        inst = e.tensor_tensor(out=ym[:, ct, lo:lo + ln], in0=big[:, ct, co:co + ln],
                                               in1=t32[:, p, lo:lo + ln], op=ALU.mult)
                    return inst
                S.add("pool", ymf, reads=[("big", ct), ("t32", p)], writes=[("ym", ct)])
        if G["s"] is not None:
            s = G["s"]
            S.add("sp", lambda e, s=s: [e.dma_start(out=o_convs[s], in_=cso[:, :])],
                  reads=[("cso",)], writes=[("o_convs", s)], dkey=("cso",))
        S.tag = (g, S.tag[1][:2] + 'o')
        outproj(G, gbase, cbase + 24)

    def sgu_layer(G, g, gbase, cbase, l):
        T = G["T"]
        rmsnorm(G, P_NORMG + 8 * l, lambda k: h[:, k, 0:T], lambda k: ("h", k))
        state["fine"] = 2
        cv = [gbase + cbase + i for i in range(8)]
        slv = [use_chunk(c, hold_from=cv[0]) for c in cv]
        tchunks = [(128 * i, 128) for i in range(4)] + ([(512, 32)] if G["s"] is not None else [])
        stg_info = {}

        Gv = dict(G)
        Gv["tiles"] = [(b,) for b in range(8)]

        def vstage(tci):
            tok0, nt = tchunks[tci]
            nb = 1
            tl = [next_tile(Gv) for _ in range(4)]
            bs = tci % 2
            if bs == 0:
                VR, VNB = vraw, dbuf2
                VQ = [[("vraw", q)] for q in range(4)]
                DK = [("d", k) for k in range(4)]
            else:
                VR, VNB = vrawB, vnbB
                VQ = [[("ym", c) for c in range(8)] for q in range(4)]
                DK = [("ym", c) for c in range(7, 12)]
            VK = sorted(set(k for q in range(4) for k in VQ[q]))
            for ti, tile in enumerate(tl):
                qs = [ti * nb + b for b in range(nb)]

                def vfn(e, qs=qs, tok0=tok0, nt=nt, tile=tile):
                    inst = None
                    for bi, q in enumerate(qs):
                        for c2 in range(2):
                            cc = 2 * q + c2
                            for k in range(8):
                                inst = e.matmul(psap(tile, bi, 256, p0=0, p1=nt, off=c2 * 256),
                                                lhsT=h[:, k, tok0:tok0 + nt],
                                                rhs=slots[:, slv[cc], k * 256:(k + 1) * 256],
                                                start=(k == 0), stop=(k == 7))
                    return inst
                S.add("pe", vfn, reads=[("slot", slv[2 * q + c2]) for q in qs for c2 in range(2)] + [("h", k) for k in range(8)],
                      writes=[*PK(tile)])

                def vev(e, qs=qs, nt=nt, tile=tile, VR=VR):
                    inst = None
                    for bi, q in enumerate(qs):
                        inst = e.activation(out=VR[0:nt, q * 512:(q + 1) * 512], in_=psap(tile, bi, 512, p0=0, p1=nt),
                                            func=AF.Copy)
                    return inst
                S.add("act", vev, reads=[*PK(tile)], writes=sorted(set(k for q in qs for k in VQ[q])))
            for q in range(4):
                S.add("dve", lambda e, q=q, nt=nt, VR=VR, bs=bs: e.bn_stats(out=bst[0:nt, bs, q * 6:(q + 1) * 6],
                                                                         in_=VR[0:nt, q * 512:(q + 1) * 512]),
                      reads=VQ[q], writes=[("bst", bs, q)])
            S.add("dve", lambda e, nt=nt, bs=bs: e.bn_aggr(out=mv[0:nt, bs, 0:2], in_=bst[0:nt, bs, 0:24]),
                  reads=[("bst", bs, q) for q in range(4)], writes=[("mv", bs)])
            S.add("act", lambda e, nt=nt, bs=bs: e.activation(out=mv[0:nt, bs, 2:3], in_=mv[0:nt, bs, 1:2], func=AF.Sqrt,
                                                              bias=epsb[0:nt, 1:2], scale=1.0),
                  reads=[("mv", bs)], writes=[("mv2", bs)])
            S.add("dve", lambda e, nt=nt, bs=bs: e.reciprocal(out=mv[0:nt, bs, 3:4], in_=mv[0:nt, bs, 2:3]),
                  reads=[("mv2", bs)], writes=[("mv3", bs)])
            S.add("dve", lambda e, nt=nt, VR=VR, bs=bs: e.scalar_tensor_tensor(
                out=VR[0:nt, :], in0=VR[0:nt, :], scalar=mv[0:nt, bs, 0:1], in1=bc[0:nt, B_SGG:B_SGG + 2048],
                op0=ALU.subtract, op1=ALU.mult),
                reads=VK + [("mv", bs)], writes=VK)
            need_f32 = (nt == 32) or (g == ngroups - 1 and tci == 3)
            if need_f32:
                S.add("dve", lambda e, nt=nt, VR=VR, bs=bs: e.scalar_tensor_tensor(
                    out=VR[0:nt, :], in0=VR[0:nt, :], scalar=mv[0:nt, bs, 3:4], in1=bc[0:nt, B_SGB:B_SGB + 2048],
                    op0=ALU.mult, op1=ALU.add),
                    reads=VK + [("mv3", bs)], writes=VK)
                S.add("act", lambda e, nt=nt, VR=VR, VNB=VNB: e.activation(out=VNB[0:nt, 0:2048], in_=VR[0:nt, :], func=AF.Copy),
                      reads=VK, writes=DK)
            else:
                S.add("dve", lambda e, nt=nt, VR=VR, VNB=VNB, bs=bs: e.scalar_tensor_tensor(
                    out=VNB[0:nt, 0:2048], in0=VR[0:nt, :], scalar=mv[0:nt, bs, 3:4], in1=bc[0:nt, B_SGB:B_SGB + 2048],
                    op0=ALU.mult, op1=ALU.add),
                    reads=VK + [("mv3", bs)], writes=DK)
            if nt == 32:
                s = G["s"]
                S.add("sp", lambda e, s=s, VR=VR: [e.dma_start(out=o_sgus[s], in_=VR[0:32, :])],
                      reads=VK, writes=[("o_sgus", s)], dkey=("vraw", bs))
            elif g == ngroups - 1 and tci == 3:
                S.add("sp", lambda e, VR=VR: [e.dma_start(out=o_sgup[:, :], in_=VR[:, :])],
                      reads=VK, writes=[("o_sgup",)], dkey=("vraw", bs))
            stg_info[tci] = (tok0, nt, VNB, DK)

        def sstage(tci):
            tok0, nt, VNB, DK = stg_info[tci]
            for hd in range(4):
                tile = next_tile(Gv)

                def sfn(e, hd=hd, nt=nt, tile=tile, VNB=VNB):
                    inst = None
                    for m in range(4):
                        inst = e.matmul(psap(tile, 0, nt, off=m * 128),
                                        lhsT=VNB[0:nt, (4 * hd + m) * 128:(4 * hd + m + 1) * 128],
                                        rhs=wmT[0:nt, hd, 0:nt], start=True, stop=True)
                    return inst
                S.add("pe", sfn, reads=DK + [("wmT",)], writes=[*PK(tile)])

                def sev(e, hd=hd, nt=nt, tok0=tok0, tile=tile):
                    inst = None
                    for m in range(4):
                        inst = e.tensor_tensor(out=big[:, 4 * hd + m, tok0:tok0 + nt], in0=psap(tile, 0, nt, off=m * 128),
                                               in1=bc[:, B_BSB + hd * 128:B_BSB + hd * 128 + nt], op=ALU.add)
                    return inst
                S.add("dve", sev, reads=[*PK(tile)], writes=[("big", 4 * hd + m) for m in range(4)])
        vstage(0)
        for tci in range(len(tchunks)):
            if tci + 1 < len(tchunks):
                vstage(tci + 1)
            sstage(tci)
        S.tag = (g, 'L2u')
        for pr in range(8):
            slu = use_chunk(gbase + cbase + 8 + 2 * pr)
            slz = use_chunk(gbase + cbase + 8 + 2 * pr + 1)
            for m2 in range(2):
                ct = 2 * pr + m2
                tU = next_tile(G)
                inproj(G, slu, m2, tU)
                tZ = next_tile(G)
                inproj(G, slz, m2, tZ)
                p = state["par"]
                state["par"] ^= 1
                blockwise("act", G, lambda e, bi, c0, n, p=p, tZ=tZ: e.activation(
                    out=t32[:, p, c0:c0 + n], in_=psap(tZ, bi, n), func=AF.Silu),
                    reads=[*PK(tZ)], writes=[("t32", p)])
                blockwise("dve", G, lambda e, bi, c0, n, p=p, tU=tU: e.tensor_tensor(
                    out=t32[:, p, c0:c0 + n], in0=psap(tU, bi, n), in1=t32[:, p, c0:c0 + n], op=ALU.mult),
                    reads=[*PK(tU), ("t32", p)], writes=[("t32", p)])
                S.add("pool", lambda e, p=p, ct=ct: e.tensor_tensor(out=ym[:, ct, 0:T], in0=t32[:, p, 0:T],
                                                                    in1=big[:, ct, 0:T], op=ALU.mult),
                      reads=[("t32", p), ("big", ct)], writes=[("ym", ct)])
        S.tag = (g, S.tag[1][:2] + 'o')
        outproj(G, gbase, cbase + 24)

    S.add("sp", lambda e: [e.dma_start(out=par[:, :], in_=params[:, :])], writes=[("par",)], dkey=("par",))
    S.add("sp", lambda e: [e.dma_start(out=bc[:, :], in_=bcin[:, :])], writes=[("bc",)], dkey=("bc",))
    S.add("sp", lambda e: [e.dma_start(out=sgw32[:, :, :], in_=sgwT[:, :, :])], writes=[("sgw32",)], dkey=("sgw32",))
    S.add("pool", lambda e: e.memset(ones_d[:, :], 1.0 / D), writes=[("ones",)])
    S.add("pool", lambda e: e.memset(ones_e[:, :], 1.0 / E), writes=[("ones",)])
    S.add("pool", lambda e: e.memset(epsb[:, 0:1], RMS_EPS), writes=[("eps",)])
    S.add("pool", lambda e: e.memset(epsb[:, 1:2], LN_EPS), writes=[("eps",)])
    S.add("pool", lambda e: e.memset(pool_hp[:, :, :, :], 0.0), writes=[("pool_hp", j, ct) for j in range(2) for ct in range(16)])
    S.add("pool", lambda e: e.memset(conv_hp[:, :, :], 0.0), writes=[("conv_hp", ct) for ct in range(16)])
    S.add("pool", lambda e: e.memset(seqb[:, :, :], 0.0), writes=[("seqb", q) for q in range(4)])
    S.add("pool", lambda e: e.memset(sA[:, :], 0.0), writes=[("sA",)])
    S.add("pool", lambda e: e.memset(sB[:, :], 0.0), writes=[("sB",)])
    S.add("pool", lambda e: e.memset(rs[:, :], 1.0), writes=[("rs",)])
    S.add("pool", lambda e: e.memset(big[:, :, :], 0.0), writes=[("big", ct) for ct in range(16)])
    for hd in range(4):
        S.add("dve", lambda e, hd=hd: e.tensor_tensor(out=wmT[:, hd, :], in0=sgw32[:, hd, :],
                                                      in1=bc[:, B_MASK:B_MASK + 128], op=ALU.mult),
              reads=[("sgw32",), ("bc",)], writes=[("wmT",)])
    S.add("dve", lambda e: e.tensor_copy(out=ident_bf[:, :], in_=bc[:, B_IDENT:B_IDENT + 128]),
          reads=[("bc",)], writes=[("ident",)])
    S.add("pool", lambda e: e.memset(agb[:, :, :], 0.0), writes=[("agb", 0), ("agb", 1)])
    CONST_KEYS = [("par",), ("bc",), ("ones",), ("eps",), ("ident",)]
    for eng in ("pe", "act", "dve", "pool"):
        S.add(eng, None, reads=CONST_KEYS, writes=[])

    def load_x(g):
        G = gstruct(g)
        for k in range(8):
            if G["xi"] == 0:
                S.add("sp", lambda e, g=g, k=k: [e.dma_start(out=x[:, k, 0:512], in_=xT[k, :, g * 512:(g + 1) * 512])],
                      writes=[("x", k)], dkey=("x", k))
            else:
                S.add("sp", lambda e, g=g, k=k: [e.dma_start(out=stg2d[:, k * 512:(k + 1) * 512], in_=xT[k, :, g * 512:(g + 1) * 512])],
                      reads=[("stg", 0), ("stg", 1)], writes=[("xo", k), ("stg", 0), ("stg", 1)], dkey=("x", k))
        if G["s"] is not None:
            s = G["s"]
            S.add("sp", lambda e, s=s: [e.dma_start(out=x[:, :, 512:544],
                                                      in_=xT[:, :, SEQ + s * 32:SEQ + (s + 1) * 32].rearrange("k p t -> p k t"))],
                  writes=[("x", k) for k in range(8)], dkey=("xs",))
            S.add("sp", lambda e, s=s: [e.dma_start(out=st_pool[:, :, :], in_=stpool[:, s, :, :].rearrange("j p f -> p j f"))],
                  writes=[("st_pool",)], dkey=("st_pool",))
            S.add("sp", lambda e, s=s: [e.dma_start(out=st_conv[:, :], in_=stconv[s])],
                  writes=[("st_conv",)], dkey=("st_conv",))

    loaded_x = set()
    early_norm = set()
    for g in range(ngroups):
        G = gstruct(g)
        T = G["T"]
        gbase = g * NCH
        if g not in loaded_x:
            load_x(g)
            loaded_x.add(g)
        S.tag = (g, 'L0')
        if depth >= 1:
            pool_layer(G, g, gbase, L0, 0, 0, skip_norm=(g in early_norm))
        S.tag = (g, 'L1')
        if depth >= 2:
            conv_layer(G, g, gbase, L1, 1)
        S.tag = (g, 'L2')
        if depth >= 3:
            sgu_layer(G, g, gbase, L2, 2)
        S.tag = (g, 'L3')
        if depth >= 4:
            hook = None
            if PINGPONG and g >= 1 and g + 1 < ngroups:
                load_x(g + 1)
                loaded_x.add(g + 1)

                def hook(gn=g + 1):
                    Gn = gstruct(gn)
                    Tn = Gn["T"]
                    rmsnorm(Gn, P_NORMG, lambda k, Tn=Tn: h[:, k, 0:Tn], lambda k: ("h", k))
                    state["fine"] = 2
                    early_norm.add(gn)
            pool_layer(G, g, gbase, L3, 1, 3, pre_out=hook)
        S.tag = (g, 'FIN')
        rmsnorm(G, P_FINALG, lambda k, T=T: big[:, k, 0:T], lambda k: ("big", k), sq=ym, sqk="ym")
        for k in range(8):
            S.add("sp", lambda e, g=g, k=k: [e.dma_start(out=yT[k, :, g * 512:(g + 1) * 512], in_=big[:, k, 0:512])],
                  reads=[("big", k)], writes=[("yT", g, k)], dkey=("yout", k))
        if G["s"] is not None:
            s = G["s"]
            S.add("sp", lambda e, s=s: [e.dma_start(out=yT[:, :, SEQ + s * 32:SEQ + (s + 1) * 32].rearrange("k p t -> p k t"),
                                                      in_=big[:, 0:8, 512:544])],
                  reads=[("big", k) for k in range(8)], writes=[("yTs", s)], dkey=("youts",))
    S.add("sp", lambda e: [e.dma_start(out=o_poolp[:, :, :].rearrange("j p f -> p j f"),
                                       in_=pool_hp[:, :, :, :].rearrange("p j c r -> p j (c r)"))],
          reads=[("pool_hp", j, ct) for j in range(2) for ct in range(16)], writes=[("o_poolp",)], dkey=("ohp",))
    S.add("sp", lambda e: [e.dma_start(out=o_convp[:, :], in_=conv_hp[:, :, :].rearrange("p c r -> p (c r)"))],
          reads=[("conv_hp", ct) for ct in range(16)], writes=[("o_convp",)], dkey=("ohp",))

    ops = S.ops
    for op in ops:
        for d in op.deps:
            if d.dkey is None and not (d.eng == "pe" and op.eng == "pe"):
                d.mark = True
    cnt = {}
    for op in ops:
        if op.mark:
            cnt[op.eng] = cnt.get(op.eng, 0) + 1
            op.markno = cnt[op.eng]

    engs = ["pe", "act", "dve", "pool", "sp"]
    esem = {en: es.enter_context(nc.semaphore("sem_" + en)) for en in engs}
    dsem = {}
    for k in S.dcount:
        dsem[k] = es.enter_context(nc.semaphore("dsem%d" % len(dsem)))

    blk = es.enter_context(nc.Block())

    def emit(engname, handle):
        waited = {}
        for op in ops:
            if op.eng != engname:
                continue
            for d in sorted(op.deps, key=lambda o: o.idx):
                if d.dkey is not None:
                    sem, val = dsem[d.dkey], d.dval
                else:
                    if d.eng == "pe" and engname == "pe":
                        continue
                    if d.fn is None:
                        continue
                    sem, val = esem[d.eng], d.markno
                key = id(sem)
                if waited.get(key, 0) >= val:
                    continue
                waited[key] = val
                handle.wait_ge(sem, val)
            if op.fn is None:
                continue
            n0 = nc.n_instructions()
            r = op.fn(handle)
            DBG_TAGS.setdefault(engname, []).append((op.tag, nc.n_instructions() - n0))
            if op.dkey is not None:
                for inst in r:
                    inst.then_inc(dsem[op.dkey], 16)
            elif op.mark:
                r.then_inc(esem[engname], 1)
        if engname == "sp":
            for k, n in S.dcount.items():
                handle.wait_ge(dsem[k], 16 * n)

    @blk.tensor
    def _(e):
        emit("pe", e)

    @blk.scalar
    def _(e):
        emit("act", e)

    @blk.vector
    def _(e):
        emit("dve", e)

    @blk.gpsimd
    def _(e):
        emit("pool", e)

    @blk.sync
    def _(e):
        emit("sp", e)

    es.close()
    return nc


def _in_chunk(Wm, col0):
    blk = Wm[:, col0:col0 + 256].reshape(8, 128, 256).transpose(1, 0, 2)
    return np.ascontiguousarray(blk).reshape(128, 2048)


def _out_chunk(Wm, m):
    blk = Wm[:, m * 128:(m + 1) * 128].reshape(16, 128, 128).transpose(1, 0, 2)
    return np.ascontiguousarray(blk).reshape(128, 2048)


def _grp_chunk(Wg):
    blk = Wg.reshape(4, 128, 512).transpose(1, 0, 2)
    return np.ascontiguousarray(blk).reshape(128, 2048)


def _pack_weights(pool_in_w, pool_w, pool_out_w, conv_in_w, conv_out_w, sgu_in_w, sgu_out_w):
    ch = []

    def pool(j):
        for gq in range(4):
            ch.append(_in_chunk(pool_in_w[j], 512 * gq))
            ch.append(_in_chunk(pool_in_w[j], 512 * gq + 256))
            ch.append(_in_chunk(pool_in_w[j], E + 512 * gq))
            ch.append(_in_chunk(pool_in_w[j], E + 512 * gq + 256))
            ch.append(_grp_chunk(pool_w[j, gq]))
        for m in range(8):
            ch.append(_out_chunk(pool_out_w[j], m))
    pool(0)
    for pr in range(8):
        ch.append(_in_chunk(conv_in_w[0], 256 * pr))
        ch.append(_in_chunk(conv_in_w[0], E + 256 * pr))
    for pr in range(8):
        ch.append(_in_chunk(conv_in_w[0], 2 * E + 256 * pr))
    for m in range(8):
        ch.append(_out_chunk(conv_out_w[0], m))
    for i in range(8):
        ch.append(_in_chunk(sgu_in_w[0], E + 256 * i))
    for pr in range(8):
        ch.append(_in_chunk(sgu_in_w[0], 256 * pr))
        ch.append(_in_chunk(sgu_in_w[0], 2 * E + 256 * pr))
    for m in range(8):
        ch.append(_out_chunk(sgu_out_w[0], m))
    pool(1)
    assert len(ch) == NCH
    return np.stack(ch).astype(np.float32)


def _colvec(v, nt):
    return np.asarray(v, np.float32).reshape(nt, 128).T


_PROG = {}
DBG_TAGS = {}


def kernel(x_prompt, x_sample, state_pool, state_conv, norm_g, pool_in_w, pool_w, pool_scale, pool_out_w,
           conv_in_w, conv_w, conv_b, conv_norm_g, conv_norm_b, conv_out_w,
           sgu_in_w, sgu_norm_g, sgu_norm_b, sgu_w, sgu_b, sgu_out_w, final_g, _depth=4, _ngroups=NG):
    f = lambda a: np.asarray(a, np.float32)
    x_prompt, x_sample, state_pool, state_conv = f(x_prompt), f(x_sample), f(state_pool), f(state_conv)
    wst = _pack_weights(f(pool_in_w), f(pool_w), f(pool_out_w), f(conv_in_w), f(conv_out_w), f(sgu_in_w), f(sgu_out_w))
    par = np.zeros((128, NPAR), np.float32)
    for l in range(4):
        par[:, P_NORMG + 8 * l:P_NORMG + 8 * l + 8] = _colvec(f(norm_g)[l], 8)
    par[:, P_FINALG:P_FINALG + 8] = _colvec(f(final_g), 8)
    for j in range(2):
        par[:, P_PSCALE + 16 * j:P_PSCALE + 16 * j + 16] = _colvec(f(pool_scale)[j], 16)
    cw = f(conv_w)[0]
    par[:, P_CONVW:P_CONVW + 496] = cw.T.reshape(16, 128, 31).transpose(1, 0, 2).reshape(128, 496)
    par[:, P_CONVB:P_CONVB + 16] = _colvec(f(conv_b)[0], 16)
    par[:, P_CONVNG:P_CONVNG + 16] = _colvec(f(conv_norm_g)[0], 16)
    par[:, P_CONVNB:P_CONVNB + 16] = _colvec(f(conv_norm_b)[0], 16)
    bcin = np.zeros((128, NBC), np.float32)
    bcin[:, B_SGG:B_SGG + 2048] = f(sgu_norm_g)[0][None, :]
    bcin[:, B_SGB:B_SGB + 2048] = f(sgu_norm_b)[0][None, :]
    bcin[:, B_BSB:B_BSB + 512] = f(sgu_b)[0].reshape(1, 512)
    jj = np.arange(128)
    bcin[:, B_MASK:B_MASK + 128] = (jj[:, None] <= jj[None, :]).astype(np.float32)
    for gq, w in enumerate(POOL_WINDOWS):
        t = np.arange(16)
        bcin[:, B_RATIO + 16 * gq:B_RATIO + 16 * gq + 16] = (w / np.minimum(t + 1, w)).astype(np.float32)[None, :]
    bcin[:, B_IDENT:B_IDENT + 128] = np.eye(128, dtype=np.float32)
    sgwT = np.ascontiguousarray(f(sgu_w)[0].transpose(2, 0, 1))

    in_maps = []
    for c in range(NCORE):
        xt = np.empty((8, 128, SEQ + NSAMP * SLEN), np.float32)
        xt[:, :, :SEQ] = x_prompt[c].T.reshape(8, 128, SEQ)
        xs = x_sample[NSAMP * c:NSAMP * (c + 1)].reshape(NSAMP * SLEN, D)
        xt[:, :, SEQ:] = xs.T.reshape(8, 128, NSAMP * SLEN)
        sp = state_pool[:, NSAMP * c:NSAMP * (c + 1)]
        sp = sp.transpose(0, 1, 3, 2).reshape(2, NSAMP, 16, 128, 15).transpose(0, 1, 3, 2, 4).reshape(2, NSAMP, 128, 240)
        sc = state_conv[0, NSAMP * c:NSAMP * (c + 1)]
        sc = sc.transpose(0, 2, 1).reshape(NSAMP, 16, 128, 30).transpose(0, 2, 1, 3).reshape(NSAMP, 128, 480)
        in_maps.append({"xT": xt, "wst": wst, "params": par, "bcin": bcin, "sgwT": sgwT,
                        "stpool": np.ascontiguousarray(sp), "stconv": np.ascontiguousarray(sc)})
    key = (_depth, _ngroups)
    if key not in _PROG:
        _PROG[key] = build_program(_depth, _ngroups)
    res = run_bass_kernel_spmd(_PROG[key], in_maps, core_ids=list(range(NCORE)))
    R = res.results

    y_prompt = np.empty((8, SEQ, D), np.float32)
    y_sample = np.empty((32, SLEN, D), np.float32)
    npp = np.empty((2, 8, 15, E), np.float32)
    nps = np.empty((2, 32, 15, E), np.float32)
    ncp = np.empty((1, 8, 30, E), np.float32)
    ncs = np.empty((1, 32, 30, E), np.float32)
    nsp = np.empty((1, 8, 128, E), np.float32)
    nss = np.empty((1, 32, SLEN, E), np.float32)
    for c in range(NCORE):
        r = R[c]
        yt = np.asarray(r["yT"]).reshape(D, SEQ + NSAMP * SLEN)
        y_prompt[c] = yt[:, :SEQ].T
        y_sample[NSAMP * c:NSAMP * (c + 1)] = yt[:, SEQ:].T.reshape(NSAMP, SLEN, D)
        pp = np.asarray(r["o_poolp"]).reshape(2, 128, 16, 15)
        npp[:, c] = pp.transpose(0, 3, 2, 1).reshape(2, 15, E)
        pq = np.asarray(r["o_pools"]).reshape(2, NSAMP, 128, 16, 15)
        nps[:, NSAMP * c:NSAMP * (c + 1)] = pq.transpose(0, 1, 4, 3, 2).reshape(2, NSAMP, 15, E)
        cp = np.asarray(r["o_convp"]).reshape(128, 16, 30)
        ncp[0, c] = cp.transpose(2, 1, 0).reshape(30, E)
        cq = np.asarray(r["o_convs"]).reshape(NSAMP, 128, 16, 30)
        ncs[0, NSAMP * c:NSAMP * (c + 1)] = cq.transpose(0, 3, 2, 1).reshape(NSAMP, 30, E)
        nsp[0, c] = np.asarray(r["o_sgup"]).reshape(128, E)
        nss[0, NSAMP * c:NSAMP * (c + 1)] = np.asarray(r["o_sgus"]).reshape(NSAMP, SLEN, E)
    return (y_prompt, y_sample, npp, nps, ncp, ncs, nsp, nss)
```

```python
import numpy as np
from contextlib import ExitStack
import concourse.bass as bass
import concourse.mybir as mybir
from concourse.bass_utils import run_bass_kernel_spmd

F32 = mybir.dt.float32
BF16 = mybir.dt.bfloat16
ALU = mybir.AluOpType
AF = mybir.ActivationFunctionType

D = 1024
E = 2048
SEQ = 4096
NCORE = 8
NSAMP = 4
SLEN = 32
TP = 512
NG = SEQ // TP
NCH = 120
NSLOT = 9
PF = 6
RMS_EPS = 1e-6
LN_EPS = 1e-5
POOL_WINDOWS = (2, 4, 8, 16)
BIGW = 512 + 30 + 32
SEQW = 512 + 60 + 32

P_NORMG = 0
P_FINALG = P_NORMG + 32
P_PSCALE = P_FINALG + 8
P_CONVW = P_PSCALE + 32
P_CONVB = P_CONVW + 496
P_CONVNG = P_CONVB + 16
P_CONVNB = P_CONVNG + 16
NPAR = P_CONVNB + 16
B_SGG = 0
B_SGB = B_SGG + 2048
B_BSB = B_SGB + 2048
B_MASK = B_BSB + 512
B_RATIO = B_MASK + 128
B_IDENT = B_RATIO + 64
NBC = B_IDENT + 128
PINGPONG = True
NPE = 19


class Op:
    __slots__ = ("eng", "fn", "deps", "mark", "markno", "dkey", "dval", "idx", "tag")


class Sched:
    def __init__(self):
        self.ops = []
        self.last_w = {}
        self.readers = {}
        self.dcount = {}
        self.tag = None

    def add(self, eng, fn, reads=(), writes=(), dkey=None, ndma=1):
        op = Op()
        op.eng, op.fn, op.mark, op.markno, op.dkey, op.dval = eng, fn, False, 0, dkey, 0
        op.idx = len(self.ops)
        op.tag = self.tag
        deps = set()
        for k in reads:
            w = self.last_w.get(k)
            if w is not None:
                deps.add(w)
        for k in writes:
            w = self.last_w.get(k)
            if w is not None:
                deps.add(w)
            for r in self.readers.get(k, ()):
                deps.add(r)
        wset = set(writes)
        for k in writes:
            self.last_w[k] = op
            self.readers[k] = []
        for k in reads:
            if k not in wset:
                self.readers.setdefault(k, []).append(op)
        deps.discard(op)
        op.deps = deps
        if dkey is not None:
            self.dcount[dkey] = self.dcount.get(dkey, 0) + ndma
            op.dval = 16 * self.dcount[dkey]
        self.ops.append(op)
        return op


def build_program(depth=4, ngroups=NG):
    nc = bass.Bass("TRN2", target_bir_lowering=False)
    S = Sched()

    def dram(name, shape, dt, kind):
        return nc.dram_tensor(name, shape, dt, kind=kind).ap()

    xT = dram("xT", [8, 128, SEQ + NSAMP * SLEN], F32, "ExternalInput")
    wst = dram("wst", [NCH, 128, 2048], F32, "ExternalInput")
    params = dram("params", [128, NPAR], F32, "ExternalInput")
    bcin = dram("bcin", [128, NBC], F32, "ExternalInput")
    sgwT = dram("sgwT", [128, 4, 128], F32, "ExternalInput")
    stpool = dram("stpool", [2, NSAMP, 128, 240], F32, "ExternalInput")
    stconv = dram("stconv", [NSAMP, 128, 480], F32, "ExternalInput")
    yT = dram("yT", [8, 128, SEQ + NSAMP * SLEN], F32, "ExternalOutput")
    o_poolp = dram("o_poolp", [2, 128, 240], F32, "ExternalOutput")
    o_pools = dram("o_pools", [2, NSAMP, 128, 240], F32, "ExternalOutput")
    o_convp = dram("o_convp", [128, 480], F32, "ExternalOutput")
    o_convs = dram("o_convs", [NSAMP, 128, 480], F32, "ExternalOutput")
    o_sgup = dram("o_sgup", [128, 2048], F32, "ExternalOutput")
    o_sgus = dram("o_sgus", [NSAMP, SLEN, 2048], F32, "ExternalOutput")
    wq = dram("wq", [NCH, 128, 2048], BF16, "Internal")
    dgq = dram("dgq", [16, 128, NPE * 128], BF16, "Internal")

    es = ExitStack()

    def sb(name, shape, dt):
        return es.enter_context(nc.sbuf_tensor(name, shape, dt))

    ps = es.enter_context(nc.psum_tensor("ps", [128, 4096], F32))
    x = sb("x", [128, 8, 544], F32)
    h = sb("h", [128, 8, 544], BF16)
    ym = sb("ym", [128, 16, 544], BF16)
    big = sb("big", [128, 16, BIGW], F32)
    slots = sb("slots", [128, NSLOT, 2048], BF16)
    stg = sb("stg", [128, 2, 2048], F32)
    par = sb("par", [128, NPAR], F32)
    bc = sb("bc", [128, NBC], F32)
    sgw32 = sb("sgw32", [128, 4, 128], F32)
    wmT = sb("wmT", [128, 4, 128], BF16)
    ones_d = sb("ones_d", [128, 128], BF16)
    ones_e = sb("ones_e", [128, 128], BF16)
    epsb = sb("epsb", [128, 2], F32)
    pool_hp = sb("pool_hp", [128, 2, 16, 15], F32)
    conv_hp = sb("conv_hp", [128, 16, 30], F32)
    st_pool = sb("st_pool", [128, 2, 240], F32)
    st_conv = sb("st_conv", [128, 480], F32)
    pso = sb("pso", [128, 2, 240], F32)
    cso = sb("cso", [128, 480], F32)
    seqb = sb("seqb", [128, 4, SEQW], F32)
    sA = sb("sA", [128, SEQW], F32)
    sB = sb("sB", [128, SEQW], F32)
    dbuf = sb("dbuf", [128, 4, 544], BF16)
    dbuf2 = dbuf[:, :, :].rearrange("p a b -> p (a b)")
    t32 = sb("t32", [128, 2, 544], F32)
    t16 = sb("t16", [128, 2, 544], BF16)
    rs = sb("rs", [128, BIGW], F32)
    vraw = sb("vraw", [128, 2048], F32)
    bst = sb("bst", [128, 2, 24], F32)
    agb = sb("agb", [128, 2, SEQW], BF16)
    ident_bf = sb("ident_bf", [128, 128], BF16)
    ym2d = ym[:, :, :].rearrange("p a b -> p (a b)")
    DGOFF = [0, 5 * 544]
    dg = ym2d
    stg2d = stg[:, :, :].rearrange("p a b -> p (a b)")
    DGK = [[("ym", c) for c in range(0, 5)], [("ym", c) for c in range(5, 10)]]
    assert NPE * 128 <= 5 * 544
    dgs = vraw[:, :].bitcast(BF16)
    vrawB = ym2d[:, 0:4096].bitcast(F32)
    vnbB = ym2d[:, 4096:6144]
    assert tuple(vrawB.shape) == (128, 2048), vrawB.shape
    mv = sb("mv", [128, 2, 4], F32)

    state = {"tile": 0, "loaded": 0, "par": 0, "fine": 0}

    def next_tile(G):
        n = len(G["tiles"])
        t = state["tile"] % n
        state["tile"] = (t + 1) % n
        return G["tiles"][t]

    def psap(tile, bi, n, p0=0, p1=128, off=0):
        b = tile[bi]
        return ps[p0:p1, b * 512 + off: b * 512 + off + n]

    def PK(tile):
        return [("psb", b) for b in tile]

    def gstruct(g):
        even = (g % 2 == 0)
        if even:
            return dict(T=544, segs=[(0, 512), (512, 32)], blocks=[(0, 272), (272, 272)], s=g // 2,
                        tiles=[(0, 1), (2, 3), (4, 5)], aux=(6, 7), xi=0, g=g)
        return dict(T=512, segs=[(0, 512)], blocks=[(0, 512)], s=None,
                    tiles=[(b,) for b in range(7)], aux=(7,), xi=(1 if PINGPONG else 0), g=g)

    def XAP(G, k, c0, n):
        if G["xi"] == 0:
            return x[:, k, c0:c0 + n]
        return stg2d[:, k * 512 + c0:k * 512 + c0 + n]

    def XK(G, k):
        return ("x", k) if G["xi"] == 0 else ("xo", k)

    def pieces(G, c0, n, H):
        out = []
        for si, (s0, sn) in enumerate(G["segs"]):
            lo, hi = max(c0, s0), min(c0 + n, s0 + sn)
            if lo < hi:
                seqoff = H + (0 if si == 0 else 512 + H) + (lo - s0)
                cloff = (0 if si == 0 else 512 + H) + (lo - s0)
                out.append((lo, hi - lo, seqoff, cloff))
        return out

    def issue_load(gi):
        ci = gi % NCH
        grp = gi // NCH
        sl = gi % NSLOT
        if grp == 0:
            st = gi % 2
            S.add("sp", lambda e, ci=ci, st=st: [e.dma_start(out=stg[:, st, :], in_=wst[ci])],
                  reads=[], writes=[("stg", st)], dkey=("stg", st))
            S.add("act", lambda e, st=st, sl=sl: e.activation(out=slots[:, sl, :], in_=stg[:, st, :], func=AF.Copy),
                  reads=[("stg", st)], writes=[("slot", sl)])
            S.add("act", lambda e, ci=ci, sl=sl: [e.dma_start(out=wq[ci], in_=slots[:, sl, :])],
                  reads=[("slot", sl)], writes=[("wq", ci)], dkey=("slot", sl))
        else:
            S.add("sp", lambda e, ci=ci, sl=sl: [e.dma_start(out=slots[:, sl, :], in_=wq[ci])],
                  reads=[("wq", ci)], writes=[("slot", sl)], dkey=("slot", sl))

    total_chunks = ngroups * NCH

    def use_chunk(gi, hold_from=None):
        upto = min(gi + PF, total_chunks - 1)
        if hold_from is not None:
            upto = min(upto, hold_from + NSLOT - 1)
        while state["loaded"] <= upto:
            if (state["loaded"] % NCH) < chunks_per_pass:
                issue_load(state["loaded"])
            state["loaded"] += 1
        return gi % NSLOT

    L0 = 0
    L1 = 28
    L2 = 60
    L3 = 92
    chunks_per_pass = [0, 28, 60, 92, 120][depth]

    def mm_tile(G, tile, nk, lhs_fn, rhs_fn, reads):
        def fn(e):
            inst = None
            for k in range(nk):
                for bi, (c0, n) in enumerate(G["blocks"]):
                    inst = e.matmul(psap(tile, bi, n), lhsT=lhs_fn(k), rhs=rhs_fn(k, c0, n),
                                    start=(k == 0), stop=(k == nk - 1))
            return inst
        S.add("pe", fn, reads=reads, writes=[*PK(tile)])

    def inproj(G, sl, m2, tile):
        if state["fine"] > 0:
            state["fine"] -= 1
            for k in range(8):
                def fn(e, k=k):
                    inst = None
                    for bi, (c0, n) in enumerate(G["blocks"]):
                        inst = e.matmul(psap(tile, bi, n), lhsT=slots[:, sl, k * 256 + m2 * 128: k * 256 + m2 * 128 + 128],
                                        rhs=h[:, k, c0:c0 + n], start=(k == 0), stop=(k == 7))
                    return inst
                S.add("pe", fn, reads=[("slot", sl), ("h", k)], writes=[*PK(tile)])
            return
        mm_tile(G, tile, 8,
                lambda k: slots[:, sl, k * 256 + m2 * 128: k * 256 + m2 * 128 + 128],
                lambda k, c0, n: h[:, k, c0:c0 + n],
                reads=[("slot", sl)] + [("h", k) for k in range(8)])

    def blockwise(eng, G, fn_block, reads, writes):
        def fn(e):
            inst = None
            for bi, (c0, n) in enumerate(G["blocks"]):
                inst = fn_block(e, bi, c0, n)
            return inst
        S.add(eng, fn, reads=reads, writes=writes)

    def rmsnorm(G, gcol0, out_fn, out_keys, sq=None, sqk="h"):
        T = G["T"]
        if sq is None:
            sq = h
        for k in range(8):
            S.add("act", lambda e, k=k: e.activation(out=sq[:, k, 0:T], in_=XAP(G, k, 0, T), func=AF.Square),
                  reads=[XK(G, k)], writes=[(sqk, k)])
        for k in range(8):
            def sfn(e, k=k):
                inst = None
                for bi, (c0, n) in enumerate(G["blocks"]):
                    inst = e.matmul(psap(G["aux"], bi, n), lhsT=ones_d[:, :], rhs=sq[:, k, c0:c0 + n],
                                    start=(k == 0), stop=(k == 7))
                return inst
            S.add("pe", sfn, reads=[(sqk, k)], writes=[*PK(G["aux"])])
        blockwise("act", G, lambda e, bi, c0, n: e.activation(out=rs[:, c0:c0 + n], in_=psap(G["aux"], bi, n),
                                                              func=AF.Sqrt, bias=epsb[:, 0:1], scale=1.0),
                  reads=[*PK(G["aux"])], writes=[("rs",)])
        S.add("dve", lambda e: e.reciprocal(out=rs[:, 0:T], in_=rs[:, 0:T]), reads=[("rs",)], writes=[("rs",)])
        for k in range(8):
            S.add("dve", lambda e, k=k: e.scalar_tensor_tensor(out=out_fn(k), in0=XAP(G, k, 0, T),
                                                               scalar=par[:, gcol0 + k: gcol0 + k + 1],
                                                               in1=rs[:, 0:T], op0=ALU.mult, op1=ALU.mult),
                  reads=[XK(G, k), ("rs",)], writes=[out_keys(k)])

    def outproj(G, gbase, c_first):
        for m in range(8):
            sl = use_chunk(gbase + c_first + m)
            tile = next_tile(G)
            mm_tile(G, tile, 16, lambda k, sl=sl: slots[:, sl, k * 128:(k + 1) * 128],
                    lambda k, c0, n: ym[:, k, c0:c0 + n],
                    reads=[("slot", sl)] + [("ym", k) for k in range(16)])
            blockwise("dve", G, lambda e, bi, c0, n, m=m, tile=tile: e.tensor_tensor(
                out=XAP(G, m, c0, n), in0=psap(tile, bi, n), in1=XAP(G, m, c0, n), op=ALU.add),
                reads=[*PK(tile), XK(G, m)], writes=[XK(G, m)])

    def gen_dg(ct):
        for k in range(NPE):
            wcol = par[:, P_CONVW + ct * 31 + k:P_CONVW + ct * 31 + k + 1]
            if k % 2 == 0:
                S.add("act", lambda e, k=k, wcol=wcol: e.activation(
                    out=dgs[:, k * 128:(k + 1) * 128], in_=ident_bf[:, :], func=AF.Copy, scale=wcol),
                    reads=[("ident",)], writes=[("dgs", k)])
            else:
                S.add("pool", lambda e, k=k, wcol=wcol: e.tensor_scalar(
                    out=dgs[:, k * 128:(k + 1) * 128], in0=ident_bf[:, :], scalar1=wcol, scalar2=1.0,
                    op0=ALU.mult, op1=ALU.mult),
                    reads=[("ident",)], writes=[("dgs", k)])
        S.add("act", lambda e, ct=ct: [e.dma_start(out=dgq[ct], in_=dgs[:, 0:NPE * 128])],
              reads=[("dgs", k) for k in range(NPE)], writes=[("dgq", ct)], dkey=("dgs",))

    def pool_layer(G, g, gbase, cbase, j, l, skip_norm=False, pre_out=None):
        H = 15
        T = G["T"]
        W = H + 512 + ((H + 32) if G["s"] is not None else 0)
        offS = H + 512 + H
        if not skip_norm:
            rmsnorm(G, P_NORMG + 8 * l, lambda k: h[:, k, 0:T], lambda k: ("h", k))
            state["fine"] = 2
        for gq in range(4):
            w = POOL_WINDOWS[gq]
            cu = [cbase + gq * 5 + 0, cbase + gq * 5 + 1]
            cz = [cbase + gq * 5 + 2, cbase + gq * 5 + 3]
            cg = cbase + gq * 5 + 4
            for m in range(4):
                ct = 4 * gq + m
                sl = use_chunk(gbase + cu[m // 2])
                tile = next_tile(G)
                inproj(G, sl, m % 2, tile)
                p = state["par"]
                state["par"] ^= 1
                ub = seqb[:, p, :]
                S.add("pool", lambda e, p=p, ct=ct: e.tensor_copy(out=seqb[:, p, 0:H], in_=pool_hp[:, j, ct, :]),
                      reads=[("pool_hp", j, ct)], writes=[("seqb", p)])
                if G["s"] is not None:
                    S.add("pool", lambda e, p=p, ct=ct: e.tensor_copy(out=seqb[:, p, H + 512:H + 512 + H],
                                                                      in_=st_pool[:, j, ct * 15:(ct + 1) * 15]),
                          reads=[("st_pool",)], writes=[("seqb", p)])

                def evac(e, p=p, tile=tile):
                    inst = None
                    for bi, (c0, n) in enumerate(G["blocks"]):
                        for (lo, ln, so, co) in pieces(G, c0, n, H):
                            inst = e.activation(out=seqb[:, p, so:so + ln], in_=psap(tile, bi, ln, off=lo - c0),
                                                func=AF.Copy)
                    return inst
                S.add("act", evac, reads=[*PK(tile)], writes=[("seqb", p)])
                S.add("pool", lambda e, p=p, ct=ct: e.tensor_copy(out=pool_hp[:, j, ct, :], in_=seqb[:, p, 512:512 + H]),
                      reads=[("seqb", p)], writes=[("pool_hp", j, ct)])
                if G["s"] is not None:
                    S.add("pool", lambda e, p=p, ct=ct: e.tensor_copy(out=pso[:, j, ct * 15:(ct + 1) * 15],
                                                                      in_=seqb[:, p, offS + 17:offS + 32]),
                          reads=[("seqb", p)], writes=[("pso", j)])
                S.add("dve", lambda e, p=p: e.tensor_tensor(out=sA[:, 1:W], in0=seqb[:, p, 1:W], in1=seqb[:, p, 0:W - 1],
                                                            op=ALU.add),
                      reads=[("seqb", p)], writes=[("sA",)])
                res, rk = sA, ("sA",)
                if w >= 4:
                    S.add("dve", lambda e: e.tensor_tensor(out=sB[:, 3:W], in0=sA[:, 3:W], in1=sA[:, 1:W - 2], op=ALU.add),
                          reads=[("sA",)], writes=[("sB",)])
                    res, rk = sB, ("sB",)
                if w >= 8:
                    S.add("dve", lambda e: e.tensor_tensor(out=sA[:, 7:W], in0=sB[:, 7:W], in1=sB[:, 3:W - 4], op=ALU.add),
                          reads=[("sB",)], writes=[("sA",)])
                    res, rk = sA, ("sA",)
                if w >= 16:
                    S.add("dve", lambda e: e.tensor_tensor(out=sB[:, 15:W], in0=sA[:, 15:W], in1=sA[:, 7:W - 8], op=ALU.add),
                          reads=[("sA",)], writes=[("sB",)])
                    res, rk = sB, ("sB",)
                if g == 0:
                    S.add("dve", lambda e, res=res, gq=gq: e.tensor_tensor(
                        out=res[:, H:H + 15], in0=res[:, H:H + 15],
                        in1=bc[:, B_RATIO + gq * 16:B_RATIO + gq * 16 + 15], op=ALU.mult),
                        reads=[rk], writes=[rk])

                def dfn(e, p=p, res=res, m=m, w=w):
                    inst = None
                    for (lo, ln, so, co) in pieces(G, 0, T, H):
                        inst = e.scalar_tensor_tensor(out=dbuf[:, m, lo:lo + ln], in0=res[:, so:so + ln],
                                                      scalar=1.0 / w, in1=seqb[:, p, so:so + ln],
                                                      op0=ALU.mult, op1=ALU.subtract)
                    return inst
                S.add("dve", dfn, reads=[rk, ("seqb", p)], writes=[("d", m)])
            if g == 0 and l == 0:
                for m in range(4):
                    gen_dg(4 * gq + m)
            for m in range(4):
                ct = 4 * gq + m
                sl = use_chunk(gbase + cz[m // 2])
                tile = next_tile(G)
                inproj(G, sl, m % 2, tile)
                blockwise("act", G, lambda e, bi, c0, n, ct=ct, tile=tile: e.activation(
                    out=big[:, ct, c0:c0 + n], in_=psap(tile, bi, n), func=AF.Silu),
                    reads=[*PK(tile)], writes=[("big", ct)])
            slg = use_chunk(gbase + cg)
            for m in range(4):
                ct = 4 * gq + m
                tile = next_tile(G)
                mm_tile(G, tile, 4, lambda k, m=m, slg=slg: slots[:, slg, k * 512 + m * 128:k * 512 + m * 128 + 128],
                        lambda k, c0, n: dbuf[:, k, c0:c0 + n],
                        reads=[("slot", slg)] + [("d", k) for k in range(4)])
                blockwise("dve", G, lambda e, bi, c0, n, ct=ct, tile=tile: e.scalar_tensor_tensor(
                    out=ym[:, ct, c0:c0 + n], in0=psap(tile, bi, n),
                    scalar=par[:, P_PSCALE + j * 16 + ct:P_PSCALE + j * 16 + ct + 1],
                    in1=big[:, ct, c0:c0 + n], op0=ALU.mult, op1=ALU.mult),
                    reads=[*PK(tile), ("big", ct)], writes=[("ym", ct)])
        if G["s"] is not None:
            s = G["s"]
            S.add("sp", lambda e, s=s: [e.dma_start(out=o_pools[j, s], in_=pso[:, j, :])],
                  reads=[("pso", j)], writes=[("o_pools", j, s)], dkey=("pso", j))
        S.tag = (g, S.tag[1][:2] + 'o')
        if pre_out is not None:
            pre_out()
        outproj(G, gbase, cbase + 20)

    def conv_layer(G, g, gbase, cbase, l):
        H = 30
        T = G["T"]
        W = H + 512 + ((H + 32) if G["s"] is not None else 0)
        Wc = W - H
        offS = H + 512 + H
        rmsnorm(G, P_NORMG + 8 * l, lambda k: h[:, k, 0:T], lambda k: ("h", k))
        state["fine"] = 2
        cblocks = [(0, 287), (287, 287)] if G["s"] is not None else [(0, 512)]
        halves = [(0, Wc // 2), (Wc // 2, Wc - Wc // 2)]
        fr = {}
        tile_a = {}

        def front1(ct):
            pr, m2 = ct // 2, ct % 2
            sla = use_chunk(gbase + cbase + 2 * pr)
            slg = use_chunk(gbase + cbase + 2 * pr + 1)
            if len(G["tiles"]) == 3:
                tA = G["tiles"][0 if ct % 2 == 0 else 2]
                tG = G["tiles"][1]
            else:
                tA = next_tile(G)
                tG = next_tile(G)
            inproj(G, sla, m2, tA)
            inproj(G, slg, m2, tG)
            tile_a[ct] = tA
            p = ct % 2
            q = ct % 4
            S.add("sp", lambda e, p=p, ct=ct: [e.dma_start(out=dg[:, DGOFF[p]:DGOFF[p] + NPE * 128], in_=dgq[ct])],
                  reads=[("dgq", ct)], writes=DGK[p], dkey=("dg", p))
            blockwise("act", G, lambda e, bi, c0, n, p=p, tG=tG: e.activation(
                out=t32[:, p, c0:c0 + n], in_=psap(tG, bi, n), func=AF.Sigmoid),
                reads=[*PK(tG)], writes=[("t32", p)])
            S.add("pool", lambda e, q=q, ct=ct: e.tensor_copy(out=seqb[:, q, 0:H], in_=conv_hp[:, ct, :]),
                  reads=[("conv_hp", ct)], writes=[("seqb", q)])
            if G["s"] is not None:
                S.add("pool", lambda e, q=q, ct=ct: e.tensor_copy(out=seqb[:, q, H + 512:H + 512 + H],
                                                                  in_=st_conv[:, ct * 30:(ct + 1) * 30]),
                      reads=[("st_conv",)], writes=[("seqb", q)])

            def glu(e, p=p, q=q, tA=tA):
                inst = None
                for bi, (c0, n) in enumerate(G["blocks"]):
                    for (lo, ln, so, co) in pieces(G, c0, n, H):
                        inst = e.tensor_tensor(out=seqb[:, q, so:so + ln], in0=psap(tA, bi, ln, off=lo - c0),
                                               in1=t32[:, p, lo:lo + ln], op=ALU.mult)
                return inst
            S.add("dve", glu, reads=[*PK(tA), ("t32", p)], writes=[("seqb", q)])

        def front2(ct):
            p = ct % 2
            q = ct % 4
            S.add("act", lambda e, p=p, q=q: e.activation(out=agb[:, p, 0:W], in_=seqb[:, q, 0:W], func=AF.Copy),
                  reads=[("seqb", q)], writes=[("agb", p)])
            S.add("pool", lambda e, q=q, ct=ct: e.tensor_copy(out=conv_hp[:, ct, :], in_=seqb[:, q, 512:512 + H]),
                  reads=[("seqb", q)], writes=[("conv_hp", ct)])
            if G["s"] is not None:
                S.add("pool", lambda e, q=q, ct=ct: e.tensor_copy(out=cso[:, ct * 30:(ct + 1) * 30],
                                                                  in_=seqb[:, q, offS + 2:offS + 32]),
                      reads=[("seqb", q)], writes=[("cso",)])

        def front(ct):
            front1(ct)
            front2(ct)

        acc_tile = {}

        def pe_taps(ct):
            p = ct % 2
            tAcc = tile_a[ct] if len(G["tiles"]) == 3 else next_tile(G)
            acc_tile[ct] = tAcc

            def pfn(e, p=p, tAcc=tAcc):
                inst = None
                for k in range(NPE):
                    for bi, (c0, n) in enumerate(cblocks):
                        inst = e.matmul(psap(tAcc, bi, n), lhsT=dg[:, DGOFF[p] + k * 128:DGOFF[p] + (k + 1) * 128],
                                        rhs=agb[:, p, k + c0:k + c0 + n], start=(k == 0), stop=(k == NPE - 1))
                return inst
            S.add("pe", pfn, reads=[("agb", p)] + DGK[p], writes=[*PK(tAcc)])

        def f0(ct):
            q = ct % 4
            tAcc = acc_tile[ct]
            k = NPE
            wcol = par[:, P_CONVW + ct * 31 + k:P_CONVW + ct * 31 + k + 1]

            def fn(e, q=q, ct=ct, wcol=wcol, k=k, tAcc=tAcc):
                inst = None
                for bi, (c0, n) in enumerate(cblocks):
                    inst = e.scalar_tensor_tensor(out=big[:, ct, c0:c0 + n], in0=seqb[:, q, k + c0:k + c0 + n],
                                                  scalar=wcol, in1=psap(tAcc, bi, n), op0=ALU.mult, op1=ALU.add)
                return inst
            S.add("dve", fn, reads=[("seqb", q), *PK(tAcc)], writes=[("big", ct)])

        def rest(cts, mid):
            ks = list(range(NPE + 1, 31))
            for i, k in enumerate(ks):
                if i == len(ks) // 2 and mid is not None:
                    mid()
                for ct in cts:
                    q = ct % 4
                    wcol = par[:, P_CONVW + ct * 31 + k:P_CONVW + ct * 31 + k + 1]
                    S.add("dve", lambda e, q=q, ct=ct, wcol=wcol, k=k: e.scalar_tensor_tensor(
                        out=big[:, ct, 0:Wc], in0=seqb[:, q, k:k + Wc], scalar=wcol,
                        in1=big[:, ct, 0:Wc], op0=ALU.mult, op1=ALU.add),
                        reads=[("seqb", q), ("big", ct)], writes=[("big", ct)])

        def tail(ct):
            p = ct % 2

            def cbf(e, p=p, ct=ct):
                inst = None
                for (lo, ln, so, co) in pieces(G, 0, T, H):
                    inst = e.activation(out=t16[:, p, lo:lo + ln], in_=big[:, ct, co:co + ln], func=AF.Identity,
                                        bias=par[:, P_CONVB + ct:P_CONVB + ct + 1], scale=1.0)
                return inst
            S.add("act", cbf, reads=[("big", ct)], writes=[("t16", p)])

            def mfn(e, p=p, ct=ct):
                inst = None
                for bi, (c0, n) in enumerate(G["blocks"]):
                    inst = e.matmul(psap(G["aux"], bi, n), lhsT=ones_e[:, :], rhs=t16[:, p, c0:c0 + n],
                                    start=(ct == 0), stop=(ct == 15))
                return inst
            S.add("pe", mfn, reads=[("t16", p)], writes=[*PK(G["aux"])])

        front1(0)
        front1(1)
        front2(0)
        front2(1)
        pe_taps(0)
        pe_taps(1)
        f0(0)
        f0(1)
        for pr in range(8):
            c0, c1 = 2 * pr, 2 * pr + 1
            mid = None
            if pr < 7:
                front1(c0 + 2)
                front1(c1 + 2)
                front2(c0 + 2)
                front2(c1 + 2)
                pe_taps(c0 + 2)
                pe_taps(c1 + 2)
                mid = (lambda a=c0 + 2, b=c1 + 2: (f0(a), f0(b)))
            if pr > 0:
                tail(c0 - 2)
                tail(c1 - 2)
            rest([c0, c1], mid)
        tail(14)
        tail(15)
        S.tag = (g, 'L1b')

        def mevac(e):
            inst = None
            for bi, (c0, n) in enumerate(G["blocks"]):
                for (lo, ln, so, co) in pieces(G, c0, n, H):
                    inst = e.activation(out=sA[:, co:co + ln], in_=psap(G["aux"], bi, ln, off=lo - c0), func=AF.Copy)
            return inst
        S.add("act", mevac, reads=[*PK(G["aux"])], writes=[("sA",)])
        for ct in range(16):
            p = ct % 2
            S.add("dve", lambda e, ct=ct: e.scalar_tensor_tensor(out=big[:, ct, 0:Wc], in0=big[:, ct, 0:Wc],
                                                                 scalar=par[:, P_CONVB + ct:P_CONVB + ct + 1],
                                                                 in1=sA[:, 0:Wc], op0=ALU.add, op1=ALU.subtract),
                  reads=[("big", ct), ("sA",)], writes=[("big", ct)])

            def sqf(e, p=p, ct=ct):
                inst = None
                for (lo, ln, so, co) in pieces(G, 0, T, H):
                    inst = e.activation(out=t16[:, p, lo:lo + ln], in_=big[:, ct, co:co + ln], func=AF.Square)
                return inst
            S.add("act", sqf, reads=[("big", ct)], writes=[("t16", p)])

            def vfn(e, p=p, ct=ct):
                inst = None
                for bi, (c0, n) in enumerate(G["blocks"]):
                    inst = e.matmul(psap(G["aux"], bi, n), lhsT=ones_e[:, :], rhs=t16[:, p, c0:c0 + n],
                                    start=(ct == 0), stop=(ct == 15))
                return inst
            S.add("pe", vfn, reads=[("t16", p)], writes=[*PK(G["aux"])])

        def revac(e):
            inst = None
            for bi, (c0, n) in enumerate(G["blocks"]):
                for (lo, ln, so, co) in pieces(G, c0, n, H):
                    inst = e.activation(out=rs[:, co:co + ln], in_=psap(G["aux"], bi, ln, off=lo - c0), func=AF.Sqrt,
                                        bias=epsb[:, 1:2], scale=1.0)
            return inst
        S.add("act", revac, reads=[*PK(G["aux"])], writes=[("rs",)])
        S.add("dve", lambda e: e.reciprocal(out=rs[:, 0:Wc], in_=rs[:, 0:Wc]), reads=[("rs",)], writes=[("rs",)])
        S.tag = (g, 'L1c')
        for pr in range(8):
            slz = use_chunk(gbase + cbase + 16 + pr)
            for m2 in range(2):
                ct = 2 * pr + m2
                tZ = next_tile(G)
                inproj(G, slz, m2, tZ)
                p = state["par"]
                state["par"] ^= 1
                blockwise("act", G, lambda e, bi, c0, n, p=p, tZ=tZ: e.activation(
                    out=t32[:, p, c0:c0 + n], in_=psap(tZ, bi, n), func=AF.Silu),
                    reads=[*PK(tZ)], writes=[("t32", p)])
                S.add("dve", lambda e, ct=ct: e.tensor_tensor(out=big[:, ct, 0:Wc], in0=big[:, ct, 0:Wc],
                                                              in1=rs[:, 0:Wc], op=ALU.mult),
                      reads=[("big", ct), ("rs",)], writes=[("big", ct)])
                S.add("act", lambda e, ct=ct: e.activation(out=big[:, ct, 0:Wc], in_=big[:, ct, 0:Wc], func=AF.Silu,
                                                           bias=par[:, P_CONVNB + ct:P_CONVNB + ct + 1],
                                                           scale=par[:, P_CONVNG + ct:P_CONVNG + ct + 1]),
                      reads=[("big", ct)], writes=[("big", ct)])

                def ymf(e, p=p, ct=ct):
                    inst = None
                    for (lo, ln, so, co) in pieces(G, 0, T, H):
                        inst = e.tensor_tensor(out=ym[:, ct, lo:lo + ln], in0=big[:, ct, co:co + ln],
                                               in1=t32[:, p, lo:lo + ln], op=ALU.mult)
                    return inst
                S.add("pool", ymf, reads=[("big", ct), ("t32", p)], writes=[("ym", ct)])
        if G["s"] is not None:
            s = G["s"]
            S.add("sp", lambda e, s=s: [e.dma_start(out=o_convs[s], in_=cso[:, :])],
                  reads=[("cso",)], writes=[("o_convs", s)], dkey=("cso",))
        S.tag = (g, S.tag[1][:2] + 'o')
        outproj(G, gbase, cbase + 24)

    def sgu_layer(G, g, gbase, cbase, l):
        T = G["T"]
        rmsnorm(G, P_NORMG + 8 * l, lambda k: h[:, k, 0:T], lambda k: ("h", k))
        state["fine"] = 2
        cv = [gbase + cbase + i for i in range(8)]
        slv = [use_chunk(c, hold_from=cv[0]) for c in cv]
        tchunks = [(128 * i, 128) for i in range(4)] + ([(512, 32)] if G["s"] is not None else [])
        stg_info = {}

        Gv = dict(G)
        Gv["tiles"] = [(b,) for b in range(8)]

        def vstage(tci):
            tok0, nt = tchunks[tci]
            nb = 1
            tl = [next_tile(Gv) for _ in range(4)]
            bs = tci % 2
            if bs == 0:
                VR, VNB = vraw, dbuf2
                VQ = [[("vraw", q)] for q in range(4)]
                DK = [("d", k) for k in range(4)]
            else:
                VR, VNB = vrawB, vnbB
                VQ = [[("ym", c) for c in range(8)] for q in range(4)]
                DK = [("ym", c) for c in range(7, 12)]
            VK = sorted(set(k for q in range(4) for k in VQ[q]))
            for ti, tile in enumerate(tl):
                qs = [ti * nb + b for b in range(nb)]

                def vfn(e, qs=qs, tok0=tok0, nt=nt, tile=tile):
                    inst = None
                    for bi, q in enumerate(qs):
                        for c2 in range(2):
                            cc = 2 * q + c2
                            for k in range(8):
                                inst = e.matmul(psap(tile, bi, 256, p0=0, p1=nt, off=c2 * 256),
                                                lhsT=h[:, k, tok0:tok0 + nt],
                                                rhs=slots[:, slv[cc], k * 256:(k + 1) * 256],
                                                start=(k == 0), stop=(k == 7))
                    return inst
                S.add("pe", vfn, reads=[("slot", slv[2 * q + c2]) for q in qs for c2 in range(2)] + [("h", k) for k in range(8)],
                      writes=[*PK(tile)])

                def vev(e, qs=qs, nt=nt, tile=tile, VR=VR):
                    inst = None
                    for bi, q in enumerate(qs):
                        inst = e.activation(out=VR[0:nt, q * 512:(q + 1) * 512], in_=psap(tile, bi, 512, p0=0, p1=nt),
                                            func=AF.Copy)
                    return inst
                S.add("act", vev, reads=[*PK(tile)], writes=sorted(set(k for q in qs for k in VQ[q])))
            for q in range(4):
                S.add("dve", lambda e, q=q, nt=nt, VR=VR, bs=bs: e.bn_stats(out=bst[0:nt, bs, q * 6:(q + 1) * 6],
                                                                         in_=VR[0:nt, q * 512:(q + 1) * 512]),
                      reads=VQ[q], writes=[("bst", bs, q)])
            S.add("dve", lambda e, nt=nt, bs=bs: e.bn_aggr(out=mv[0:nt, bs, 0:2], in_=bst[0:nt, bs, 0:24]),
                  reads=[("bst", bs, q) for q in range(4)], writes=[("mv", bs)])
            S.add("act", lambda e, nt=nt, bs=bs: e.activation(out=mv[0:nt, bs, 2:3], in_=mv[0:nt, bs, 1:2], func=AF.Sqrt,
                                                              bias=epsb[0:nt, 1:2], scale=1.0),
                  reads=[("mv", bs)], writes=[("mv2", bs)])
            S.add("dve", lambda e, nt=nt, bs=bs: e.reciprocal(out=mv[0:nt, bs, 3:4], in_=mv[0:nt, bs, 2:3]),
                  reads=[("mv2", bs)], writes=[("mv3", bs)])
            S.add("dve", lambda e, nt=nt, VR=VR, bs=bs: e.scalar_tensor_tensor(
                out=VR[0:nt, :], in0=VR[0:nt, :], scalar=mv[0:nt, bs, 0:1], in1=bc[0:nt, B_SGG:B_SGG + 2048],
                op0=ALU.subtract, op1=ALU.mult),
                reads=VK + [("mv", bs)], writes=VK)
            need_f32 = (nt == 32) or (g == ngroups - 1 and tci == 3)
            if need_f32:
                S.add("dve", lambda e, nt=nt, VR=VR, bs=bs: e.scalar_tensor_tensor(
                    out=VR[0:nt, :], in0=VR[0:nt, :], scalar=mv[0:nt, bs, 3:4], in1=bc[0:nt, B_SGB:B_SGB + 2048],
                    op0=ALU.mult, op1=ALU.add),
                    reads=VK + [("mv3", bs)], writes=VK)
                S.add("act", lambda e, nt=nt, VR=VR, VNB=VNB: e.activation(out=VNB[0:nt, 0:2048], in_=VR[0:nt, :], func=AF.Copy),
                      reads=VK, writes=DK)
            else:
                S.add("dve", lambda e, nt=nt, VR=VR, VNB=VNB, bs=bs: e.scalar_tensor_tensor(
                    out=VNB[0:nt, 0:2048], in0=VR[0:nt, :], scalar=mv[0:nt, bs, 3:4], in1=bc[0:nt, B_SGB:B_SGB + 2048],
                    op0=ALU.mult, op1=ALU.add),
                    reads=VK + [("mv3", bs)], writes=DK)
            if nt == 32:
                s = G["s"]
                S.add("sp", lambda e, s=s, VR=VR: [e.dma_start(out=o_sgus[s], in_=VR[0:32, :])],
                      reads=VK, writes=[("o_sgus", s)], dkey=("vraw", bs))
            elif g == ngroups - 1 and tci == 3:
                S.add("sp", lambda e, VR=VR: [e.dma_start(out=o_sgup[:, :], in_=VR[:, :])],
                      reads=VK, writes=[("o_sgup",)], dkey=("vraw", bs))
            stg_info[tci] = (tok0, nt, VNB, DK)

        def sstage(tci):
            tok0, nt, VNB, DK = stg_info[tci]
            for hd in range(4):
                tile = next_tile(Gv)

                def sfn(e, hd=hd, nt=nt, tile=tile, VNB=VNB):
                    inst = None
                    for m in range(4):
                        inst = e.matmul(psap(tile, 0, nt, off=m * 128),
                                        lhsT=VNB[0:nt, (4 * hd + m) * 128:(4 * hd + m + 1) * 128],
                                        rhs=wmT[0:nt, hd, 0:nt], start=True, stop=True)
                    return inst
                S.add("pe", sfn, reads=DK + [("wmT",)], writes=[*PK(tile)])

                def sev(e, hd=hd, nt=nt, tok0=tok0, tile=tile):
                    inst = None
                    for m in range(4):
                        inst = e.tensor_tensor(out=big[:, 4 * hd + m, tok0:tok0 + nt], in0=psap(tile, 0, nt, off=m * 128),
                                               in1=bc[:, B_BSB + hd * 128:B_BSB + hd * 128 + nt], op=ALU.add)
                    return inst
                S.add("dve", sev, reads=[*PK(tile)], writes=[("big", 4 * hd + m) for m in range(4)])
        vstage(0)
        for tci in range(len(tchunks)):
            if tci + 1 < len(tchunks):
                vstage(tci + 1)
            sstage(tci)
        S.tag = (g, 'L2u')
        for pr in range(8):
            slu = use_chunk(gbase + cbase + 8 + 2 * pr)
            slz = use_chunk(gbase + cbase + 8 + 2 * pr + 1)
            for m2 in range(2):
                ct = 2 * pr + m2
                tU = next_tile(G)
                inproj(G, slu, m2, tU)
                tZ = next_tile(G)
                inproj(G, slz, m2, tZ)
                p = state["par"]
                state["par"] ^= 1
                blockwise("act", G, lambda e, bi, c0, n, p=p, tZ=tZ: e.activation(
                    out=t32[:, p, c0:c0 + n], in_=psap(tZ, bi, n), func=AF.Silu),
                    reads=[*PK(tZ)], writes=[("t32", p)])
                blockwise("dve", G, lambda e, bi, c0, n, p=p, tU=tU: e.tensor_tensor(
                    out=t32[:, p, c0:c0 + n], in0=psap(tU, bi, n), in1=t32[:, p, c0:c0 + n], op=ALU.mult),
                    reads=[*PK(tU), ("t32", p)], writes=[("t32", p)])
                S.add("pool", lambda e, p=p, ct=ct: e.tensor_tensor(out=ym[:, ct, 0:T], in0=t32[:, p, 0:T],
                                                                    in1=big[:, ct, 0:T], op=ALU.mult),
                      reads=[("t32", p), ("big", ct)], writes=[("ym", ct)])
        S.tag = (g, S.tag[1][:2] + 'o')
        outproj(G, gbase, cbase + 24)

    S.add("sp", lambda e: [e.dma_start(out=par[:, :], in_=params[:, :])], writes=[("par",)], dkey=("par",))
    S.add("sp", lambda e: [e.dma_start(out=bc[:, :], in_=bcin[:, :])], writes=[("bc",)], dkey=("bc",))
    S.add("sp", lambda e: [e.dma_start(out=sgw32[:, :, :], in_=sgwT[:, :, :])], writes=[("sgw32",)], dkey=("sgw32",))
    S.add("pool", lambda e: e.memset(ones_d[:, :], 1.0 / D), writes=[("ones",)])
    S.add("pool", lambda e: e.memset(ones_e[:, :], 1.0 / E), writes=[("ones",)])
    S.add("pool", lambda e: e.memset(epsb[:, 0:1], RMS_EPS), writes=[("eps",)])
    S.add("pool", lambda e: e.memset(epsb[:, 1:2], LN_EPS), writes=[("eps",)])
    S.add("pool", lambda e: e.memset(pool_hp[:, :, :, :], 0.0), writes=[("pool_hp", j, ct) for j in range(2) for ct in range(16)])
    S.add("pool", lambda e: e.memset(conv_hp[:, :, :], 0.0), writes=[("conv_hp", ct) for ct in range(16)])
    S.add("pool", lambda e: e.memset(seqb[:, :, :], 0.0), writes=[("seqb", q) for q in range(4)])
    S.add("pool", lambda e: e.memset(sA[:, :], 0.0), writes=[("sA",)])
    S.add("pool", lambda e: e.memset(sB[:, :], 0.0), writes=[("sB",)])
    S.add("pool", lambda e: e.memset(rs[:, :], 1.0), writes=[("rs",)])
    S.add("pool", lambda e: e.memset(big[:, :, :], 0.0), writes=[("big", ct) for ct in range(16)])
    for hd in range(4):
        S.add("dve", lambda e, hd=hd: e.tensor_tensor(out=wmT[:, hd, :], in0=sgw32[:, hd, :],
                                                      in1=bc[:, B_MASK:B_MASK + 128], op=ALU.mult),
              reads=[("sgw32",), ("bc",)], writes=[("wmT",)])
    S.add("dve", lambda e: e.tensor_copy(out=ident_bf[:, :], in_=bc[:, B_IDENT:B_IDENT + 128]),
          reads=[("bc",)], writes=[("ident",)])
    S.add("pool", lambda e: e.memset(agb[:, :, :], 0.0), writes=[("agb", 0), ("agb", 1)])
    CONST_KEYS = [("par",), ("bc",), ("ones",), ("eps",), ("ident",)]
    for eng in ("pe", "act", "dve", "pool"):
        S.add(eng, None, reads=CONST_KEYS, writes=[])

    def load_x(g):
        G = gstruct(g)
        for k in range(8):
            if G["xi"] == 0:
                S.add("sp", lambda e, g=g, k=k: [e.dma_start(out=x[:, k, 0:512], in_=xT[k, :, g * 512:(g + 1) * 512])],
                      writes=[("x", k)], dkey=("x", k))
            else:
                S.add("sp", lambda e, g=g, k=k: [e.dma_start(out=stg2d[:, k * 512:(k + 1) * 512], in_=xT[k, :, g * 512:(g + 1) * 512])],
                      reads=[("stg", 0), ("stg", 1)], writes=[("xo", k), ("stg", 0), ("stg", 1)], dkey=("x", k))
        if G["s"] is not None:
            s = G["s"]
            S.add("sp", lambda e, s=s: [e.dma_start(out=x[:, :, 512:544],
                                                      in_=xT[:, :, SEQ + s * 32:SEQ + (s + 1) * 32].rearrange("k p t -> p k t"))],
                  writes=[("x", k) for k in range(8)], dkey=("xs",))
            S.add("sp", lambda e, s=s: [e.dma_start(out=st_pool[:, :, :], in_=stpool[:, s, :, :].rearrange("j p f -> p j f"))],
                  writes=[("st_pool",)], dkey=("st_pool",))
            S.add("sp", lambda e, s=s: [e.dma_start(out=st_conv[:, :], in_=stconv[s])],
                  writes=[("st_conv",)], dkey=("st_conv",))

    loaded_x = set()
    early_norm = set()
    for g in range(ngroups):
        G = gstruct(g)
        T = G["T"]
        gbase = g * NCH
        if g not in loaded_x:
            load_x(g)
            loaded_x.add(g)
        S.tag = (g, 'L0')
        if depth >= 1:
            pool_layer(G, g, gbase, L0, 0, 0, skip_norm=(g in early_norm))
        S.tag = (g, 'L1')
        if depth >= 2:
            conv_layer(G, g, gbase, L1, 1)
        S.tag = (g, 'L2')
        if depth >= 3:
            sgu_layer(G, g, gbase, L2, 2)
        S.tag = (g, 'L3')
        if depth >= 4:
            hook = None
            if PINGPONG and g >= 1 and g + 1 < ngroups:
                load_x(g + 1)
                loaded_x.add(g + 1)

                def hook(gn=g + 1):
                    Gn = gstruct(gn)
                    Tn = Gn["T"]
                    rmsnorm(Gn, P_NORMG, lambda k, Tn=Tn: h[:, k, 0:Tn], lambda k: ("h", k))
                    state["fine"] = 2
                    early_norm.add(gn)
            pool_layer(G, g, gbase, L3, 1, 3, pre_out=hook)
        S.tag = (g, 'FIN')
        rmsnorm(G, P_FINALG, lambda k, T=T: big[:, k, 0:T], lambda k: ("big", k), sq=ym, sqk="ym")
        for k in range(8):
            S.add("sp", lambda e, g=g, k=k: [e.dma_start(out=yT[k, :, g * 512:(g + 1) * 512], in_=big[:, k, 0:512])],
                  reads=[("big", k)], writes=[("yT", g, k)], dkey=("yout", k))
        if G["s"] is not None:
            s = G["s"]
            S.add("sp", lambda e, s=s: [e.dma_start(out=yT[:, :, SEQ + s * 32:SEQ + (s + 1) * 32].rearrange("k p t -> p k t"),
                                                      in_=big[:, 0:8, 512:544])],
                  reads=[("big", k) for k in range(8)], writes=[("yTs", s)], dkey=("youts",))
    S.add("sp", lambda e: [e.dma_start(out=o_poolp[:, :, :].rearrange("j p f -> p j f"),
                                       in_=pool_hp[:, :, :, :].rearrange("p j c r -> p j (c r)"))],
          reads=[("pool_hp", j, ct) for j in range(2) for ct in range(16)], writes=[("o_poolp",)], dkey=("ohp",))
    S.add("sp", lambda e: [e.dma_start(out=o_convp[:, :], in_=conv_hp[:, :, :].rearrange("p c r -> p (c r)"))],
          reads=[("conv_hp", ct) for ct in range(16)], writes=[("o_convp",)], dkey=("ohp",))

    ops = S.ops
    for op in ops:
        for d in op.deps:
            if d.dkey is None and not (d.eng == "pe" and op.eng == "pe"):
                d.mark = True
    cnt = {}
    for op in ops:
        if op.mark:
            cnt[op.eng] = cnt.get(op.eng, 0) + 1
            op.markno = cnt[op.eng]

    engs = ["pe", "act", "dve", "pool", "sp"]
    esem = {en: es.enter_context(nc.semaphore("sem_" + en)) for en in engs}
    dsem = {}
    for k in S.dcount:
        dsem[k] = es.enter_context(nc.semaphore("dsem%d" % len(dsem)))

    blk = es.enter_context(nc.Block())

    def emit(engname, handle):
        waited = {}
        for op in ops:
            if op.eng != engname:
                continue
            for d in sorted(op.deps, key=lambda o: o.idx):
                if d.dkey is not None:
                    sem, val = dsem[d.dkey], d.dval
                else:
                    if d.eng == "pe" and engname == "pe":
                        continue
                    if d.fn is None:
                        continue
                    sem, val = esem[d.eng], d.markno
                key = id(sem)
                if waited.get(key, 0) >= val:
                    continue
                waited[key] = val
                handle.wait_ge(sem, val)
            if op.fn is None:
                continue
            n0 = nc.n_instructions()
            r = op.fn(handle)
            DBG_TAGS.setdefault(engname, []).append((op.tag, nc.n_instructions() - n0))
            if op.dkey is not None:
                for inst in r:
                    inst.then_inc(dsem[op.dkey], 16)
            elif op.mark:
                r.then_inc(esem[engname], 1)
        if engname == "sp":
            for k, n in S.dcount.items():
                handle.wait_ge(dsem[k], 16 * n)

    @blk.tensor
    def _(e):
        emit("pe", e)

    @blk.scalar
    def _(e):
        emit("act", e)

    @blk.vector
    def _(e):
        emit("dve", e)

    @blk.gpsimd
    def _(e):
        emit("pool", e)

    @blk.sync
    def _(e):
        emit("sp", e)

    es.close()
    return nc


def _in_chunk(Wm, col0):
    blk = Wm[:, col0:col0 + 256].reshape(8, 128, 256).transpose(1, 0, 2)
    return np.ascontiguousarray(blk).reshape(128, 2048)


def _out_chunk(Wm, m):
    blk = Wm[:, m * 128:(m + 1) * 128].reshape(16, 128, 128).transpose(1, 0, 2)
    return np.ascontiguousarray(blk).reshape(128, 2048)


def _grp_chunk(Wg):
    blk = Wg.reshape(4, 128, 512).transpose(1, 0, 2)
    return np.ascontiguousarray(blk).reshape(128, 2048)


def _pack_weights(pool_in_w, pool_w, pool_out_w, conv_in_w, conv_out_w, sgu_in_w, sgu_out_w):
    ch = []

    def pool(j):
        for gq in range(4):
            ch.append(_in_chunk(pool_in_w[j], 512 * gq))
            ch.append(_in_chunk(pool_in_w[j], 512 * gq + 256))
            ch.append(_in_chunk(pool_in_w[j], E + 512 * gq))
            ch.append(_in_chunk(pool_in_w[j], E + 512 * gq + 256))
            ch.append(_grp_chunk(pool_w[j, gq]))
        for m in range(8):
            ch.append(_out_chunk(pool_out_w[j], m))
    pool(0)
    for pr in range(8):
        ch.append(_in_chunk(conv_in_w[0], 256 * pr))
        ch.append(_in_chunk(conv_in_w[0], E + 256 * pr))
    for pr in range(8):
        ch.append(_in_chunk(conv_in_w[0], 2 * E + 256 * pr))
    for m in range(8):
        ch.append(_out_chunk(conv_out_w[0], m))
    for i in range(8):
        ch.append(_in_chunk(sgu_in_w[0], E + 256 * i))
    for pr in range(8):
        ch.append(_in_chunk(sgu_in_w[0], 256 * pr))
        ch.append(_in_chunk(sgu_in_w[0], 2 * E + 256 * pr))
    for m in range(8):
        ch.append(_out_chunk(sgu_out_w[0], m))
    pool(1)
    assert len(ch) == NCH
    return np.stack(ch).astype(np.float32)


def _colvec(v, nt):
    return np.asarray(v, np.float32).reshape(nt, 128).T


_PROG = {}
DBG_TAGS = {}


def kernel(x_prompt, x_sample, state_pool, state_conv, norm_g, pool_in_w, pool_w, pool_scale, pool_out_w,
           conv_in_w, conv_w, conv_b, conv_norm_g, conv_norm_b, conv_out_w,
           sgu_in_w, sgu_norm_g, sgu_norm_b, sgu_w, sgu_b, sgu_out_w, final_g, _depth=4, _ngroups=NG):
    f = lambda a: np.asarray(a, np.float32)
    x_prompt, x_sample, state_pool, state_conv = f(x_prompt), f(x_sample), f(state_pool), f(state_conv)
    wst = _pack_weights(f(pool_in_w), f(pool_w), f(pool_out_w), f(conv_in_w), f(conv_out_w), f(sgu_in_w), f(sgu_out_w))
    par = np.zeros((128, NPAR), np.float32)
    for l in range(4):
        par[:, P_NORMG + 8 * l:P_NORMG + 8 * l + 8] = _colvec(f(norm_g)[l], 8)
    par[:, P_FINALG:P_FINALG + 8] = _colvec(f(final_g), 8)
    for j in range(2):
        par[:, P_PSCALE + 16 * j:P_PSCALE + 16 * j + 16] = _colvec(f(pool_scale)[j], 16)
    cw = f(conv_w)[0]
    par[:, P_CONVW:P_CONVW + 496] = cw.T.reshape(16, 128, 31).transpose(1, 0, 2).reshape(128, 496)
    par[:, P_CONVB:P_CONVB + 16] = _colvec(f(conv_b)[0], 16)
    par[:, P_CONVNG:P_CONVNG + 16] = _colvec(f(conv_norm_g)[0], 16)
    par[:, P_CONVNB:P_CONVNB + 16] = _colvec(f(conv_norm_b)[0], 16)
    bcin = np.zeros((128, NBC), np.float32)
    bcin[:, B_SGG:B_SGG + 2048] = f(sgu_norm_g)[0][None, :]
    bcin[:, B_SGB:B_SGB + 2048] = f(sgu_norm_b)[0][None, :]
    bcin[:, B_BSB:B_BSB + 512] = f(sgu_b)[0].reshape(1, 512)
    jj = np.arange(128)
    bcin[:, B_MASK:B_MASK + 128] = (jj[:, None] <= jj[None, :]).astype(np.float32)
    for gq, w in enumerate(POOL_WINDOWS):
        t = np.arange(16)
        bcin[:, B_RATIO + 16 * gq:B_RATIO + 16 * gq + 16] = (w / np.minimum(t + 1, w)).astype(np.float32)[None, :]
    bcin[:, B_IDENT:B_IDENT + 128] = np.eye(128, dtype=np.float32)
    sgwT = np.ascontiguousarray(f(sgu_w)[0].transpose(2, 0, 1))

    in_maps = []
    for c in range(NCORE):
        xt = np.empty((8, 128, SEQ + NSAMP * SLEN), np.float32)
        xt[:, :, :SEQ] = x_prompt[c].T.reshape(8, 128, SEQ)
        xs = x_sample[NSAMP * c:NSAMP * (c + 1)].reshape(NSAMP * SLEN, D)
        xt[:, :, SEQ:] = xs.T.reshape(8, 128, NSAMP * SLEN)
        sp = state_pool[:, NSAMP * c:NSAMP * (c + 1)]
        sp = sp.transpose(0, 1, 3, 2).reshape(2, NSAMP, 16, 128, 15).transpose(0, 1, 3, 2, 4).reshape(2, NSAMP, 128, 240)
        sc = state_conv[0, NSAMP * c:NSAMP * (c + 1)]
        sc = sc.transpose(0, 2, 1).reshape(NSAMP, 16, 128, 30).transpose(0, 2, 1, 3).reshape(NSAMP, 128, 480)
        in_maps.append({"xT": xt, "wst": wst, "params": par, "bcin": bcin, "sgwT": sgwT,
                        "stpool": np.ascontiguousarray(sp), "stconv": np.ascontiguousarray(sc)})
    key = (_depth, _ngroups)
    if key not in _PROG:
        _PROG[key] = build_program(_depth, _ngroups)
    res = run_bass_kernel_spmd(_PROG[key], in_maps, core_ids=list(range(NCORE)))
    R = res.results

    y_prompt = np.empty((8, SEQ, D), np.float32)
    y_sample = np.empty((32, SLEN, D), np.float32)
    npp = np.empty((2, 8, 15, E), np.float32)
    nps = np.empty((2, 32, 15, E), np.float32)
    ncp = np.empty((1, 8, 30, E), np.float32)
    ncs = np.empty((1, 32, 30, E), np.float32)
    nsp = np.empty((1, 8, 128, E), np.float32)
    nss = np.empty((1, 32, SLEN, E), np.float32)
    for c in range(NCORE):
        r = R[c]
        yt = np.asarray(r["yT"]).reshape(D, SEQ + NSAMP * SLEN)
        y_prompt[c] = yt[:, :SEQ].T
        y_sample[NSAMP * c:NSAMP * (c + 1)] = yt[:, SEQ:].T.reshape(NSAMP, SLEN, D)
        pp = np.asarray(r["o_poolp"]).reshape(2, 128, 16, 15)
        npp[:, c] = pp.transpose(0, 3, 2, 1).reshape(2, 15, E)
        pq = np.asarray(r["o_pools"]).reshape(2, NSAMP, 128, 16, 15)
        nps[:, NSAMP * c:NSAMP * (c + 1)] = pq.transpose(0, 1, 4, 3, 2).reshape(2, NSAMP, 15, E)
        cp = np.asarray(r["o_convp"]).reshape(128, 16, 30)
        ncp[0, c] = cp.transpose(2, 1, 0).reshape(30, E)
        cq = np.asarray(r["o_convs"]).reshape(NSAMP, 128, 16, 30)
        ncs[0, NSAMP * c:NSAMP * (c + 1)] = cq.transpose(0, 3, 2, 1).reshape(NSAMP, 30, E)
        nsp[0, c] = np.asarray(r["o_sgup"]).reshape(128, E)
        nss[0, NSAMP * c:NSAMP * (c + 1)] = np.asarray(r["o_sgus"]).reshape(NSAMP, SLEN, E)
    return (y_prompt, y_sample, npp, nps, ncp, ncs, nsp, nss)
```

```python
import numpy as np
from contextlib import ExitStack
import concourse.bass as bass
import concourse.mybir as mybir
from concourse.bass_utils import run_bass_kernel_spmd

F32 = mybir.dt.float32
BF16 = mybir.dt.bfloat16
ALU = mybir.AluOpType
AF = mybir.ActivationFunctionType

D = 1024
E = 2048
SEQ = 4096
NCORE = 8
NSAMP = 4
SLEN = 32
TP = 512
NG = SEQ // TP
NCH = 120
NSLOT = 9
PF = 6
RMS_EPS = 1e-6
LN_EPS = 1e-5
POOL_WINDOWS = (2, 4, 8, 16)
BIGW = 512 + 30 + 32
SEQW = 512 + 60 + 32

P_NORMG = 0
P_FINALG = P_NORMG + 32
P_PSCALE = P_FINALG + 8
P_CONVW = P_PSCALE + 32
P_CONVB = P_CONVW + 496
P_CONVNG = P_CONVB + 16
P_CONVNB = P_CONVNG + 16
NPAR = P_CONVNB + 16
B_SGG = 0
B_SGB = B_SGG + 2048
B_BSB = B_SGB + 2048
B_MASK = B_BSB + 512
B_RATIO = B_MASK + 128
B_IDENT = B_RATIO + 64
NBC = B_IDENT + 128
PINGPONG = True
NPE = 19


class Op:
    __slots__ = ("eng", "fn", "deps", "mark", "markno", "dkey", "dval", "idx", "tag")


class Sched:
    def __init__(self):
        self.ops = []
        self.last_w = {}
        self.readers = {}
        self.dcount = {}
        self.tag = None

    def add(self, eng, fn, reads=(), writes=(), dkey=None, ndma=1):
        op = Op()
        op.eng, op.fn, op.mark, op.markno, op.dkey, op.dval = eng, fn, False, 0, dkey, 0
        op.idx = len(self.ops)
        op.tag = self.tag
        deps = set()
        for k in reads:
            w = self.last_w.get(k)
            if w is not None:
                deps.add(w)
        for k in writes:
            w = self.last_w.get(k)
            if w is not None:
                deps.add(w)
            for r in self.readers.get(k, ()):
                deps.add(r)
        wset = set(writes)
        for k in writes:
            self.last_w[k] = op
            self.readers[k] = []
        for k in reads:
            if k not in wset:
                self.readers.setdefault(k, []).append(op)
        deps.discard(op)
        op.deps = deps
        if dkey is not None:
            self.dcount[dkey] = self.dcount.get(dkey, 0) + ndma
            op.dval = 16 * self.dcount[dkey]
        self.ops.append(op)
        return op


def build_program(depth=4, ngroups=NG):
    nc = bass.Bass("TRN2", target_bir_lowering=False)
    S = Sched()

    def dram(name, shape, dt, kind):
        return nc.dram_tensor(name, shape, dt, kind=kind).ap()

    xT = dram("xT", [8, 128, SEQ + NSAMP * SLEN], F32, "ExternalInput")
    wst = dram("wst", [NCH, 128, 2048], F32, "ExternalInput")
    params = dram("params", [128, NPAR], F32, "ExternalInput")
    bcin = dram("bcin", [128, NBC], F32, "ExternalInput")
    sgwT = dram("sgwT", [128, 4, 128], F32, "ExternalInput")
    stpool = dram("stpool", [2, NSAMP, 128, 240], F32, "ExternalInput")
    stconv = dram("stconv", [NSAMP, 128, 480], F32, "ExternalInput")
    yT = dram("yT", [8, 128, SEQ + NSAMP * SLEN], F32, "ExternalOutput")
    o_poolp = dram("o_poolp", [2, 128, 240], F32, "ExternalOutput")
    o_pools = dram("o_pools", [2, NSAMP, 128, 240], F32, "ExternalOutput")
    o_convp = dram("o_convp", [128, 480], F32, "ExternalOutput")
    o_convs = dram("o_convs", [NSAMP, 128, 480], F32, "ExternalOutput")
    o_sgup = dram("o_sgup", [128, 2048], F32, "ExternalOutput")
    o_sgus = dram("o_sgus", [NSAMP, SLEN, 2048], F32, "ExternalOutput")
    wq = dram("wq", [NCH, 128, 2048], BF16, "Internal")
    dgq = dram("dgq", [16, 128, NPE * 128], BF16, "Internal")

    es = ExitStack()

    def sb(name, shape, dt):
        return es.enter_context(nc.sbuf_tensor(name, shape, dt))

    ps = es.enter_context(nc.psum_tensor("ps", [128, 4096], F32))
    x = sb("x", [128, 8, 544], F32)
    h = sb("h", [128, 8, 544], BF16)
    ym = sb("ym", [128, 16, 544], BF16)
    big = sb("big", [128, 16, BIGW], F32)
    slots = sb("slots", [128, NSLOT, 2048], BF16)
    stg = sb("stg", [128, 2, 2048], F32)
    par = sb("par", [128, NPAR], F32)
    bc = sb("bc", [128, NBC], F32)
    sgw32 = sb("sgw32", [128, 4, 128], F32)
    wmT = sb("wmT", [128, 4, 128], BF16)
    ones_d = sb("ones_d", [128, 128], BF16)
    ones_e = sb("ones_e", [128, 128], BF16)
    epsb = sb("epsb", [128, 2], F32)
    pool_hp = sb("pool_hp", [128, 2, 16, 15], F32)
    conv_hp = sb("conv_hp", [128, 16, 30], F32)
    st_pool = sb("st_pool", [128, 2, 240], F32)
    st_conv = sb("st_conv", [128, 480], F32)
    pso = sb("pso", [128, 2, 240], F32)
    cso = sb("cso", [128, 480], F32)
    seqb = sb("seqb", [128, 4, SEQW], F32)
    sA = sb("sA", [128, SEQW], F32)
    sB = sb("sB", [128, SEQW], F32)
    dbuf = sb("dbuf", [128, 4, 544], BF16)
    dbuf2 = dbuf[:, :, :].rearrange("p a b -> p (a b)")
    t32 = sb("t32", [128, 2, 544], F32)
    t16 = sb("t16", [128, 2, 544], BF16)
    rs = sb("rs", [128, BIGW], F32)
    vraw = sb("vraw", [128, 2048], F32)
    bst = sb("bst", [128, 2, 24], F32)
    agb = sb("agb", [128, 2, SEQW], BF16)
    ident_bf = sb("ident_bf", [128, 128], BF16)
    ym2d = ym[:, :, :].rearrange("p a b -> p (a b)")
    DGOFF = [0, 5 * 544]
    dg = ym2d
    stg2d = stg[:, :, :].rearrange("p a b -> p (a b)")
    DGK = [[("ym", c) for c in range(0, 5)], [("ym", c) for c in range(5, 10)]]
    assert NPE * 128 <= 5 * 544
    dgs = vraw[:, :].bitcast(BF16)
    vrawB = ym2d[:, 0:4096].bitcast(F32)
    vnbB = ym2d[:, 4096:6144]
    assert tuple(vrawB.shape) == (128, 2048), vrawB.shape
    mv = sb("mv", [128, 2, 4], F32)

    state = {"tile": 0, "loaded": 0, "par": 0, "fine": 0}

    def next_tile(G):
        n = len(G["tiles"])
        t = state["tile"] % n
        state["tile"] = (t + 1) % n
        return G["tiles"][t]

    def psap(tile, bi, n, p0=0, p1=128, off=0):
        b = tile[bi]
        return ps[p0:p1, b * 512 + off: b * 512 + off + n]

    def PK(tile):
        return [("psb", b) for b in tile]

    def gstruct(g):
        even = (g % 2 == 0)
        if even:
            return dict(T=544, segs=[(0, 512), (512, 32)], blocks=[(0, 272), (272, 272)], s=g // 2,
                        tiles=[(0, 1), (2, 3), (4, 5)], aux=(6, 7), xi=0, g=g)
        return dict(T=512, segs=[(0, 512)], blocks=[(0, 512)], s=None,
                    tiles=[(b,) for b in range(7)], aux=(7,), xi=(1 if PINGPONG else 0), g=g)

    def XAP(G, k, c0, n):
        if G["xi"] == 0:
            return x[:, k, c0:c0 + n]
        return stg2d[:, k * 512 + c0:k * 512 + c0 + n]

    def XK(G, k):
        return ("x", k) if G["xi"] == 0 else ("xo", k)

    def pieces(G, c0, n, H):
        out = []
        for si, (s0, sn) in enumerate(G["segs"]):
            lo, hi = max(c0, s0), min(c0 + n, s0 + sn)
            if lo < hi:
                seqoff = H + (0 if si == 0 else 512 + H) + (lo - s0)
                cloff = (0 if si == 0 else 512 + H) + (lo - s0)
                out.append((lo, hi - lo, seqoff, cloff))
        return out

    def issue_load(gi):
        ci = gi % NCH
        grp = gi // NCH
        sl = gi % NSLOT
        if grp == 0:
            st = gi % 2
            S.add("sp", lambda e, ci=ci, st=st: [e.dma_start(out=stg[:, st, :], in_=wst[ci])],
                  reads=[], writes=[("stg", st)], dkey=("stg", st))
            S.add("act", lambda e, st=st, sl=sl: e.activation(out=slots[:, sl, :], in_=stg[:, st, :], func=AF.Copy),
                  reads=[("stg", st)], writes=[("slot", sl)])
            S.add("act", lambda e, ci=ci, sl=sl: [e.dma_start(out=wq[ci], in_=slots[:, sl, :])],
                  reads=[("slot", sl)], writes=[("wq", ci)], dkey=("slot", sl))
        else:
            S.add("sp", lambda e, ci=ci, sl=sl: [e.dma_start(out=slots[:, sl, :], in_=wq[ci])],
                  reads=[("wq", ci)], writes=[("slot", sl)], dkey=("slot", sl))

    total_chunks = ngroups * NCH

    def use_chunk(gi, hold_from=None):
        upto = min(gi + PF, total_chunks - 1)
        if hold_from is not None:
            upto = min(upto, hold_from + NSLOT - 1)
        while state["loaded"] <= upto:
            if (state["loaded"] % NCH) < chunks_per_pass:
                issue_load(state["loaded"])
            state["loaded"] += 1
        return gi % NSLOT

    L0 = 0
    L1 = 28
    L2 = 60
    L3 = 92
    chunks_per_pass = [0, 28, 60, 92, 120][depth]

    def mm_tile(G, tile, nk, lhs_fn, rhs_fn, reads):
        def fn(e):
            inst = None
            for k in range(nk):
                for bi, (c0, n) in enumerate(G["blocks"]):
                    inst = e.matmul(psap(tile, bi, n), lhsT=lhs_fn(k), rhs=rhs_fn(k, c0, n),
                                    start=(k == 0), stop=(k == nk - 1))
            return inst
        S.add("pe", fn, reads=reads, writes=[*PK(tile)])

    def inproj(G, sl, m2, tile):
        if state["fine"] > 0:
            state["fine"] -= 1
            for k in range(8):
                def fn(e, k=k):
                    inst = None
                    for bi, (c0, n) in enumerate(G["blocks"]):
                        inst = e.matmul(psap(tile, bi, n), lhsT=slots[:, sl, k * 256 + m2 * 128: k * 256 + m2 * 128 + 128],
                                        rhs=h[:, k, c0:c0 + n], start=(k == 0), stop=(k == 7))
                    return inst
                S.add("pe", fn, reads=[("slot", sl), ("h", k)], writes=[*PK(tile)])
            return
        mm_tile(G, tile, 8,
                lambda k: slots[:, sl, k * 256 + m2 * 128: k * 256 + m2 * 128 + 128],
                lambda k, c0, n: h[:, k, c0:c0 + n],
                reads=[("slot", sl)] + [("h", k) for k in range(8)])

    def blockwise(eng, G, fn_block, reads, writes):
        def fn(e):
            inst = None
            for bi, (c0, n) in enumerate(G["blocks"]):
                inst = fn_block(e, bi, c0, n)
            return inst
        S.add(eng, fn, reads=reads, writes=writes)

    def rmsnorm(G, gcol0, out_fn, out_keys, sq=None, sqk="h"):
        T = G["T"]
        if sq is None:
            sq = h
        for k in range(8):
            S.add("act", lambda e, k=k: e.activation(out=sq[:, k, 0:T], in_=XAP(G, k, 0, T), func=AF.Square),
                  reads=[XK(G, k)], writes=[(sqk, k)])
        for k in range(8):
            def sfn(e, k=k):
                inst = None
                for bi, (c0, n) in enumerate(G["blocks"]):
                    inst = e.matmul(psap(G["aux"], bi, n), lhsT=ones_d[:, :], rhs=sq[:, k, c0:c0 + n],
                                    start=(k == 0), stop=(k == 7))
                return inst
            S.add("pe", sfn, reads=[(sqk, k)], writes=[*PK(G["aux"])])
        blockwise("act", G, lambda e, bi, c0, n: e.activation(out=rs[:, c0:c0 + n], in_=psap(G["aux"], bi, n),
                                                              func=AF.Sqrt, bias=epsb[:, 0:1], scale=1.0),
                  reads=[*PK(G["aux"])], writes=[("rs",)])
        S.add("dve", lambda e: e.reciprocal(out=rs[:, 0:T], in_=rs[:, 0:T]), reads=[("rs",)], writes=[("rs",)])
        for k in range(8):
            S.add("dve", lambda e, k=k: e.scalar_tensor_tensor(out=out_fn(k), in0=XAP(G, k, 0, T),
                                                               scalar=par[:, gcol0 + k: gcol0 + k + 1],
                                                               in1=rs[:, 0:T], op0=ALU.mult, op1=ALU.mult),
                  reads=[XK(G, k), ("rs",)], writes=[out_keys(k)])

    def outproj(G, gbase, c_first):
        for m in range(8):
            sl = use_chunk(gbase + c_first + m)
            tile = next_tile(G)
            if m == 0:
                for k in range(16):
                    def ofn(e, k=k, sl=sl, tile=tile):
                        inst = None
                        for bi, (c0, n) in enumerate(G["blocks"]):
                            inst = e.matmul(psap(tile, bi, n), lhsT=slots[:, sl, k * 128:(k + 1) * 128],
                                            rhs=ym[:, k, c0:c0 + n], start=(k == 0), stop=(k == 15))
                        return inst
                    S.add("pe", ofn, reads=[("slot", sl), ("ym", k)], writes=[*PK(tile)])
            else:
                mm_tile(G, tile, 16, lambda k, sl=sl: slots[:, sl, k * 128:(k + 1) * 128],
                        lambda k, c0, n: ym[:, k, c0:c0 + n],
                        reads=[("slot", sl)] + [("ym", k) for k in range(16)])
            blockwise("dve", G, lambda e, bi, c0, n, m=m, tile=tile: e.tensor_tensor(
                out=XAP(G, m, c0, n), in0=psap(tile, bi, n), in1=XAP(G, m, c0, n), op=ALU.add),
                reads=[*PK(tile), XK(G, m)], writes=[XK(G, m)])

    def gen_dg(ct):
        for k in range(NPE):
            wcol = par[:, P_CONVW + ct * 31 + k:P_CONVW + ct * 31 + k + 1]
            if k % 2 == 0:
                S.add("act", lambda e, k=k, wcol=wcol: e.activation(
                    out=dgs[:, k * 128:(k + 1) * 128], in_=ident_bf[:, :], func=AF.Copy, scale=wcol),
                    reads=[("ident",)], writes=[("dgs", k)])
            else:
                S.add("pool", lambda e, k=k, wcol=wcol: e.tensor_scalar(
                    out=dgs[:, k * 128:(k + 1) * 128], in0=ident_bf[:, :], scalar1=wcol, scalar2=1.0,
                    op0=ALU.mult, op1=ALU.mult),
                    reads=[("ident",)], writes=[("dgs", k)])
        S.add("act", lambda e, ct=ct: [e.dma_start(out=dgq[ct], in_=dgs[:, 0:NPE * 128])],
              reads=[("dgs", k) for k in range(NPE)], writes=[("dgq", ct)], dkey=("dgs",))

    def pool_layer(G, g, gbase, cbase, j, l, skip_norm=False, pre_out=None):
        H = 15
        T = G["T"]
        W = H + 512 + ((H + 32) if G["s"] is not None else 0)
        offS = H + 512 + H
        if not skip_norm:
            rmsnorm(G, P_NORMG + 8 * l, lambda k: h[:, k, 0:T], lambda k: ("h", k))
            state["fine"] = 2
        for gq in range(4):
            w = POOL_WINDOWS[gq]
            cu = [cbase + gq * 5 + 0, cbase + gq * 5 + 1]
            cz = [cbase + gq * 5 + 2, cbase + gq * 5 + 3]
            cg = cbase + gq * 5 + 4
            for m in range(4):
                ct = 4 * gq + m
                sl = use_chunk(gbase + cu[m // 2])
                tile = next_tile(G)
                inproj(G, sl, m % 2, tile)
                p = state["par"]
                state["par"] ^= 1
                ub = seqb[:, p, :]
                S.add("pool", lambda e, p=p, ct=ct: e.tensor_copy(out=seqb[:, p, 0:H], in_=pool_hp[:, j, ct, :]),
                      reads=[("pool_hp", j, ct)], writes=[("seqb", p)])
                if G["s"] is not None:
                    S.add("pool", lambda e, p=p, ct=ct: e.tensor_copy(out=seqb[:, p, H + 512:H + 512 + H],
                                                                      in_=st_pool[:, j, ct * 15:(ct + 1) * 15]),
                          reads=[("st_pool",)], writes=[("seqb", p)])

                def evac(e, p=p, tile=tile):
                    inst = None
                    for bi, (c0, n) in enumerate(G["blocks"]):
                        for (lo, ln, so, co) in pieces(G, c0, n, H):
                            inst = e.activation(out=seqb[:, p, so:so + ln], in_=psap(tile, bi, ln, off=lo - c0),
                                                func=AF.Copy)
                    return inst
                S.add("act", evac, reads=[*PK(tile)], writes=[("seqb", p)])
                S.add("pool", lambda e, p=p, ct=ct: e.tensor_copy(out=pool_hp[:, j, ct, :], in_=seqb[:, p, 512:512 + H]),
                      reads=[("seqb", p)], writes=[("pool_hp", j, ct)])
                if G["s"] is not None:
                    S.add("pool", lambda e, p=p, ct=ct: e.tensor_copy(out=pso[:, j, ct * 15:(ct + 1) * 15],
                                                                      in_=seqb[:, p, offS + 17:offS + 32]),
                          reads=[("seqb", p)], writes=[("pso", j)])
                S.add("dve", lambda e, p=p: e.tensor_tensor(out=sA[:, 1:W], in0=seqb[:, p, 1:W], in1=seqb[:, p, 0:W - 1],
                                                            op=ALU.add),
                      reads=[("seqb", p)], writes=[("sA",)])
                res, rk = sA, ("sA",)
                if w >= 4:
                    S.add("dve", lambda e: e.tensor_tensor(out=sB[:, 3:W], in0=sA[:, 3:W], in1=sA[:, 1:W - 2], op=ALU.add),
                          reads=[("sA",)], writes=[("sB",)])
                    res, rk = sB, ("sB",)
                if w >= 8:
                    S.add("dve", lambda e: e.tensor_tensor(out=sA[:, 7:W], in0=sB[:, 7:W], in1=sB[:, 3:W - 4], op=ALU.add),
                          reads=[("sB",)], writes=[("sA",)])
                    res, rk = sA, ("sA",)
                if w >= 16:
                    S.add("dve", lambda e: e.tensor_tensor(out=sB[:, 15:W], in0=sA[:, 15:W], in1=sA[:, 7:W - 8], op=ALU.add),
                          reads=[("sA",)], writes=[("sB",)])
                    res, rk = sB, ("sB",)
                if g == 0:
                    S.add("dve", lambda e, res=res, gq=gq: e.tensor_tensor(
                        out=res[:, H:H + 15], in0=res[:, H:H + 15],
                        in1=bc[:, B_RATIO + gq * 16:B_RATIO + gq * 16 + 15], op=ALU.mult),
                        reads=[rk], writes=[rk])

                def dfn(e, p=p, res=res, m=m, w=w):
                    inst = None
                    for (lo, ln, so, co) in pieces(G, 0, T, H):
                        inst = e.scalar_tensor_tensor(out=dbuf[:, m, lo:lo + ln], in0=res[:, so:so + ln],
                                                      scalar=1.0 / w, in1=seqb[:, p, so:so + ln],
                                                      op0=ALU.mult, op1=ALU.subtract)
                    return inst
                S.add("dve", dfn, reads=[rk, ("seqb", p)], writes=[("d", m)])
            if g == 0 and l == 0:
                for m in range(4):
                    gen_dg(4 * gq + m)
            for m in range(4):
                ct = 4 * gq + m
                sl = use_chunk(gbase + cz[m // 2])
                tile = next_tile(G)
                inproj(G, sl, m % 2, tile)
                blockwise("act", G, lambda e, bi, c0, n, ct=ct, tile=tile: e.activation(
                    out=big[:, ct, c0:c0 + n], in_=psap(tile, bi, n), func=AF.Silu),
                    reads=[*PK(tile)], writes=[("big", ct)])
            slg = use_chunk(gbase + cg)
            for m in range(4):
                ct = 4 * gq + m
                tile = next_tile(G)
                mm_tile(G, tile, 4, lambda k, m=m, slg=slg: slots[:, slg, k * 512 + m * 128:k * 512 + m * 128 + 128],
                        lambda k, c0, n: dbuf[:, k, c0:c0 + n],
                        reads=[("slot", slg)] + [("d", k) for k in range(4)])
                blockwise("dve", G, lambda e, bi, c0, n, ct=ct, tile=tile: e.scalar_tensor_tensor(
                    out=ym[:, ct, c0:c0 + n], in0=psap(tile, bi, n),
                    scalar=par[:, P_PSCALE + j * 16 + ct:P_PSCALE + j * 16 + ct + 1],
                    in1=big[:, ct, c0:c0 + n], op0=ALU.mult, op1=ALU.mult),
                    reads=[*PK(tile), ("big", ct)], writes=[("ym", ct)])
        if G["s"] is not None:
            s = G["s"]
            S.add("sp", lambda e, s=s: [e.dma_start(out=o_pools[j, s], in_=pso[:, j, :])],
                  reads=[("pso", j)], writes=[("o_pools", j, s)], dkey=("pso", j))
        S.tag = (g, S.tag[1][:2] + 'o')
        if pre_out is not None:
            pre_out()
        outproj(G, gbase, cbase + 20)

    def conv_layer(G, g, gbase, cbase, l):
        H = 30
        T = G["T"]
        W = H + 512 + ((H + 32) if G["s"] is not None else 0)
        Wc = W - H
        offS = H + 512 + H
        rmsnorm(G, P_NORMG + 8 * l, lambda k: h[:, k, 0:T], lambda k: ("h", k))
        state["fine"] = 2
        cblocks = [(0, 287), (287, 287)] if G["s"] is not None else [(0, 512)]
        halves = [(0, Wc // 2), (Wc // 2, Wc - Wc // 2)]
        fr = {}
        tile_a = {}

        def front1(ct):
            pr, m2 = ct // 2, ct % 2
            sla = use_chunk(gbase + cbase + 2 * pr)
            slg = use_chunk(gbase + cbase + 2 * pr + 1)
            if len(G["tiles"]) == 3:
                tA = G["tiles"][0 if ct % 2 == 0 else 2]
                tG = G["tiles"][1]
            else:
                tA = next_tile(G)
                tG = next_tile(G)
            inproj(G, sla, m2, tA)
            inproj(G, slg, m2, tG)
            tile_a[ct] = tA
            p = ct % 2
            q = ct % 4
            S.add("sp", lambda e, p=p, ct=ct: [e.dma_start(out=dg[:, DGOFF[p]:DGOFF[p] + NPE * 128], in_=dgq[ct])],
                  reads=[("dgq", ct)], writes=DGK[p], dkey=("dg", p))
            blockwise("act", G, lambda e, bi, c0, n, p=p, tG=tG: e.activation(
                out=t32[:, p, c0:c0 + n], in_=psap(tG, bi, n), func=AF.Sigmoid),
                reads=[*PK(tG)], writes=[("t32", p)])
            S.add("pool", lambda e, q=q, ct=ct: e.tensor_copy(out=seqb[:, q, 0:H], in_=conv_hp[:, ct, :]),
                  reads=[("conv_hp", ct)], writes=[("seqb", q)])
            if G["s"] is not None:
                S.add("pool", lambda e, q=q, ct=ct: e.tensor_copy(out=seqb[:, q, H + 512:H + 512 + H],
                                                                  in_=st_conv[:, ct * 30:(ct + 1) * 30]),
                      reads=[("st_conv",)], writes=[("seqb", q)])

            def glu(e, p=p, q=q, tA=tA):
                inst = None
                for bi, (c0, n) in enumerate(G["blocks"]):
                    for (lo, ln, so, co) in pieces(G, c0, n, H):
                        inst = e.tensor_tensor(out=seqb[:, q, so:so + ln], in0=psap(tA, bi, ln, off=lo - c0),
                                               in1=t32[:, p, lo:lo + ln], op=ALU.mult)
                return inst
            S.add("dve", glu, reads=[*PK(tA), ("t32", p)], writes=[("seqb", q)])

        def front2(ct):
            p = ct % 2
            q = ct % 4
            S.add("act", lambda e, p=p, q=q: e.activation(out=agb[:, p, 0:W], in_=seqb[:, q, 0:W], func=AF.Copy),
                  reads=[("seqb", q)], writes=[("agb", p)])
            S.add("pool", lambda e, q=q, ct=ct: e.tensor_copy(out=conv_hp[:, ct, :], in_=seqb[:, q, 512:512 + H]),
                  reads=[("seqb", q)], writes=[("conv_hp", ct)])
            if G["s"] is not None:
                S.add("pool", lambda e, q=q, ct=ct: e.tensor_copy(out=cso[:, ct * 30:(ct + 1) * 30],
                                                                  in_=seqb[:, q, offS + 2:offS + 32]),
                      reads=[("seqb", q)], writes=[("cso",)])

        def front(ct):
            front1(ct)
            front2(ct)

        acc_tile = {}

        def pe_taps(ct):
            p = ct % 2
            tAcc = tile_a[ct] if len(G["tiles"]) == 3 else next_tile(G)
            acc_tile[ct] = tAcc

            def pfn(e, p=p, tAcc=tAcc):
                inst = None
                for k in range(NPE):
                    for bi, (c0, n) in enumerate(cblocks):
                        inst = e.matmul(psap(tAcc, bi, n), lhsT=dg[:, DGOFF[p] + k * 128:DGOFF[p] + (k + 1) * 128],
                                        rhs=agb[:, p, k + c0:k + c0 + n], start=(k == 0), stop=(k == NPE - 1))
                return inst
            S.add("pe", pfn, reads=[("agb", p)] + DGK[p], writes=[*PK(tAcc)])

        def f0(ct):
            q = ct % 4
            tAcc = acc_tile[ct]
            k = NPE
            wcol = par[:, P_CONVW + ct * 31 + k:P_CONVW + ct * 31 + k + 1]

            def fn(e, q=q, ct=ct, wcol=wcol, k=k, tAcc=tAcc):
                inst = None
                for bi, (c0, n) in enumerate(cblocks):
                    inst = e.scalar_tensor_tensor(out=big[:, ct, c0:c0 + n], in0=seqb[:, q, k + c0:k + c0 + n],
                                                  scalar=wcol, in1=psap(tAcc, bi, n), op0=ALU.mult, op1=ALU.add)
                return inst
            S.add("dve", fn, reads=[("seqb", q), *PK(tAcc)], writes=[("big", ct)])

        def rest(cts, mid):
            ks = list(range(NPE + 1, 31))
            for i, k in enumerate(ks):
                if i == len(ks) // 2 and mid is not None:
                    mid()
                for ct in cts:
                    q = ct % 4
                    wcol = par[:, P_CONVW + ct * 31 + k:P_CONVW + ct * 31 + k + 1]
                    S.add("dve", lambda e, q=q, ct=ct, wcol=wcol, k=k: e.scalar_tensor_tensor(
                        out=big[:, ct, 0:Wc], in0=seqb[:, q, k:k + Wc], scalar=wcol,
                        in1=big[:, ct, 0:Wc], op0=ALU.mult, op1=ALU.add),
                        reads=[("seqb", q), ("big", ct)], writes=[("big", ct)])

        def tail(ct):
            p = ct % 2

            def cbf(e, p=p, ct=ct):
                inst = None
                for (lo, ln, so, co) in pieces(G, 0, T, H):
                    inst = e.activation(out=t16[:, p, lo:lo + ln], in_=big[:, ct, co:co + ln], func=AF.Identity,
                                        bias=par[:, P_CONVB + ct:P_CONVB + ct + 1], scale=1.0)
                return inst
            S.add("act", cbf, reads=[("big", ct)], writes=[("t16", p)])

            def mfn(e, p=p, ct=ct):
                inst = None
                for bi, (c0, n) in enumerate(G["blocks"]):
                    inst = e.matmul(psap(G["aux"], bi, n), lhsT=ones_e[:, :], rhs=t16[:, p, c0:c0 + n],
                                    start=(ct == 0), stop=(ct == 15))
                return inst
            S.add("pe", mfn, reads=[("t16", p)], writes=[*PK(G["aux"])])

        front1(0)
        front1(1)
        front2(0)
        front2(1)
        pe_taps(0)
        pe_taps(1)
        f0(0)
        f0(1)
        for pr in range(8):
            c0, c1 = 2 * pr, 2 * pr + 1
            mid = None
            if pr < 7:
                front1(c0 + 2)
                front1(c1 + 2)
                front2(c0 + 2)
                front2(c1 + 2)
                pe_taps(c0 + 2)
                pe_taps(c1 + 2)
                mid = (lambda a=c0 + 2, b=c1 + 2: (f0(a), f0(b)))
            if pr > 0:
                tail(c0 - 2)
                tail(c1 - 2)
            rest([c0, c1], mid)
        tail(14)
        tail(15)
        S.tag = (g, 'L1b')

        def mevac(e):
            inst = None
            for bi, (c0, n) in enumerate(G["blocks"]):
                for (lo, ln, so, co) in pieces(G, c0, n, H):
                    inst = e.activation(out=sA[:, co:co + ln], in_=psap(G["aux"], bi, ln, off=lo - c0), func=AF.Copy)
            return inst
        S.add("act", mevac, reads=[*PK(G["aux"])], writes=[("sA",)])
        for ct in range(16):
            p = ct % 2
            S.add("dve", lambda e, ct=ct: e.scalar_tensor_tensor(out=big[:, ct, 0:Wc], in0=big[:, ct, 0:Wc],
                                                                 scalar=par[:, P_CONVB + ct:P_CONVB + ct + 1],
                                                                 in1=sA[:, 0:Wc], op0=ALU.add, op1=ALU.subtract),
                  reads=[("big", ct), ("sA",)], writes=[("big", ct)])

            def sqf(e, p=p, ct=ct):
                inst = None
                for (lo, ln, so, co) in pieces(G, 0, T, H):
                    inst = e.activation(out=t16[:, p, lo:lo + ln], in_=big[:, ct, co:co + ln], func=AF.Square)
                return inst
            S.add("act", sqf, reads=[("big", ct)], writes=[("t16", p)])

            def vfn(e, p=p, ct=ct):
                inst = None
                for bi, (c0, n) in enumerate(G["blocks"]):
                    inst = e.matmul(psap(G["aux"], bi, n), lhsT=ones_e[:, :], rhs=t16[:, p, c0:c0 + n],
                                    start=(ct == 0), stop=(ct == 15))
                return inst
            S.add("pe", vfn, reads=[("t16", p)], writes=[*PK(G["aux"])])

        def revac(e):
            inst = None
            for bi, (c0, n) in enumerate(G["blocks"]):
                for (lo, ln, so, co) in pieces(G, c0, n, H):
                    inst = e.activation(out=rs[:, co:co + ln], in_=psap(G["aux"], bi, ln, off=lo - c0), func=AF.Sqrt,
                                        bias=epsb[:, 1:2], scale=1.0)
            return inst
        S.add("act", revac, reads=[*PK(G["aux"])], writes=[("rs",)])
        S.add("dve", lambda e: e.reciprocal(out=rs[:, 0:Wc], in_=rs[:, 0:Wc]), reads=[("rs",)], writes=[("rs",)])
        S.tag = (g, 'L1c')
        for pr in range(8):
            slz = use_chunk(gbase + cbase + 16 + pr)
            for m2 in range(2):
                ct = 2 * pr + m2
                tZ = next_tile(G)
                inproj(G, slz, m2, tZ)
                p = state["par"]
                state["par"] ^= 1
                blockwise("act", G, lambda e, bi, c0, n, p=p, tZ=tZ: e.activation(
                    out=t32[:, p, c0:c0 + n], in_=psap(tZ, bi, n), func=AF.Silu),
                    reads=[*PK(tZ)], writes=[("t32", p)])
                S.add("dve", lambda e, ct=ct: e.tensor_tensor(out=big[:, ct, 0:Wc], in0=big[:, ct, 0:Wc],
                                                              in1=rs[:, 0:Wc], op=ALU.mult),
                      reads=[("big", ct), ("rs",)], writes=[("big", ct)])
                S.add("act", lambda e, ct=ct: e.activation(out=big[:, ct, 0:Wc], in_=big[:, ct, 0:Wc], func=AF.Silu,
                                                           bias=par[:, P_CONVNB + ct:P_CONVNB + ct + 1],
                                                           scale=par[:, P_CONVNG + ct:P_CONVNG + ct + 1]),
                      reads=[("big", ct)], writes=[("big", ct)])

                def ymf(e, p=p, ct=ct):
                    inst = None
                    for (lo, ln, so, co) in pieces(G, 0, T, H):
                        inst = e.tensor_tensor(out=ym[:, ct, lo:lo + ln], in0=big[:, ct, co:co + ln],
                                               in1=t32[:, p, lo:lo + ln], op=ALU.mult)
                    return inst
                S.add("pool", ymf, reads=[("big", ct), ("t32", p)], writes=[("ym", ct)])
        if G["s"] is not None:
            s = G["s"]
            S.add("sp", lambda e, s=s: [e.dma_start(out=o_convs[s], in_=cso[:, :])],
                  reads=[("cso",)], writes=[("o_convs", s)], dkey=("cso",))
        S.tag = (g, S.tag[1][:2] + 'o')
        outproj(G, gbase, cbase + 24)

    def sgu_layer(G, g, gbase, cbase, l):
        T = G["T"]
        rmsnorm(G, P_NORMG + 8 * l, lambda k: h[:, k, 0:T], lambda k: ("h", k))
        state["fine"] = 2
        cv = [gbase + cbase + i for i in range(8)]
        slv = [use_chunk(c, hold_from=cv[0]) for c in cv]
        tchunks = [(128 * i, 128) for i in range(4)] + ([(512, 32)] if G["s"] is not None else [])
        stg_info = {}

        Gv = dict(G)
        Gv["tiles"] = [(b,) for b in range(8)]

        def vstage(tci):
            tok0, nt = tchunks[tci]
            nb = 1
            tl = [next_tile(Gv) for _ in range(4)]
            bs = tci % 2
            if bs == 0:
                VR, VNB = vraw, dbuf2
                VQ = [[("vraw", q)] for q in range(4)]
                DK = [("d", k) for k in range(4)]
            else:
                VR, VNB = vrawB, vnbB
                VQ = [[("ym", c) for c in range(8)] for q in range(4)]
                DK = [("ym", c) for c in range(7, 12)]
            VK = sorted(set(k for q in range(4) for k in VQ[q]))
            for ti, tile in enumerate(tl):
                qs = [ti * nb + b for b in range(nb)]

                def vfn(e, qs=qs, tok0=tok0, nt=nt, tile=tile):
                    inst = None
                    for bi, q in enumerate(qs):
                        for c2 in range(2):
                            cc = 2 * q + c2
                            for k in range(8):
                                inst = e.matmul(psap(tile, bi, 256, p0=0, p1=nt, off=c2 * 256),
                                                lhsT=h[:, k, tok0:tok0 + nt],
                                                rhs=slots[:, slv[cc], k * 256:(k + 1) * 256],
                                                start=(k == 0), stop=(k == 7))
                    return inst
                S.add("pe", vfn, reads=[("slot", slv[2 * q + c2]) for q in qs for c2 in range(2)] + [("h", k) for k in range(8)],
                      writes=[*PK(tile)])

                def vev(e, qs=qs, nt=nt, tile=tile, VR=VR):
                    inst = None
                    for bi, q in enumerate(qs):
                        inst = e.activation(out=VR[0:nt, q * 512:(q + 1) * 512], in_=psap(tile, bi, 512, p0=0, p1=nt),
                                            func=AF.Copy)
                    return inst
                S.add("act", vev, reads=[*PK(tile)], writes=sorted(set(k for q in qs for k in VQ[q])))
            for q in range(4):
                S.add("dve", lambda e, q=q, nt=nt, VR=VR, bs=bs: e.bn_stats(out=bst[0:nt, bs, q * 6:(q + 1) * 6],
                                                                         in_=VR[0:nt, q * 512:(q + 1) * 512]),
                      reads=VQ[q], writes=[("bst", bs, q)])
            S.add("dve", lambda e, nt=nt, bs=bs: e.bn_aggr(out=mv[0:nt, bs, 0:2], in_=bst[0:nt, bs, 0:24]),
                  reads=[("bst", bs, q) for q in range(4)], writes=[("mv", bs)])
            S.add("act", lambda e, nt=nt, bs=bs: e.activation(out=mv[0:nt, bs, 2:3], in_=mv[0:nt, bs, 1:2], func=AF.Sqrt,
                                                              bias=epsb[0:nt, 1:2], scale=1.0),
                  reads=[("mv", bs)], writes=[("mv2", bs)])
            S.add("dve", lambda e, nt=nt, bs=bs: e.reciprocal(out=mv[0:nt, bs, 3:4], in_=mv[0:nt, bs, 2:3]),
                  reads=[("mv2", bs)], writes=[("mv3", bs)])
            S.add("dve", lambda e, nt=nt, VR=VR, bs=bs: e.scalar_tensor_tensor(
                out=VR[0:nt, :], in0=VR[0:nt, :], scalar=mv[0:nt, bs, 0:1], in1=bc[0:nt, B_SGG:B_SGG + 2048],
                op0=ALU.subtract, op1=ALU.mult),
                reads=VK + [("mv", bs)], writes=VK)
            need_f32 = (nt == 32) or (g == ngroups - 1 and tci == 3)
            if need_f32:
                S.add("dve", lambda e, nt=nt, VR=VR, bs=bs: e.scalar_tensor_tensor(
                    out=VR[0:nt, :], in0=VR[0:nt, :], scalar=mv[0:nt, bs, 3:4], in1=bc[0:nt, B_SGB:B_SGB + 2048],
                    op0=ALU.mult, op1=ALU.add),
                    reads=VK + [("mv3", bs)], writes=VK)
                S.add("act", lambda e, nt=nt, VR=VR, VNB=VNB: e.activation(out=VNB[0:nt, 0:2048], in_=VR[0:nt, :], func=AF.Copy),
                      reads=VK, writes=DK)
            else:
                S.add("dve", lambda e, nt=nt, VR=VR, VNB=VNB, bs=bs: e.scalar_tensor_tensor(
                    out=VNB[0:nt, 0:2048], in0=VR[0:nt, :], scalar=mv[0:nt, bs, 3:4], in1=bc[0:nt, B_SGB:B_SGB + 2048],
                    op0=ALU.mult, op1=ALU.add),
                    reads=VK + [("mv3", bs)], writes=DK)
            if nt == 32:
                s = G["s"]
                S.add("sp", lambda e, s=s, VR=VR: [e.dma_start(out=o_sgus[s], in_=VR[0:32, :])],
                      reads=VK, writes=[("o_sgus", s)], dkey=("vraw", bs))
            elif g == ngroups - 1 and tci == 3:
                S.add("sp", lambda e, VR=VR: [e.dma_start(out=o_sgup[:, :], in_=VR[:, :])],
                      reads=VK, writes=[("o_sgup",)], dkey=("vraw", bs))
            stg_info[tci] = (tok0, nt, VNB, DK)

        def sstage(tci):
            tok0, nt, VNB, DK = stg_info[tci]
            for hd in range(4):
                tile = next_tile(Gv)

                def sfn(e, hd=hd, nt=nt, tile=tile, VNB=VNB):
                    inst = None
                    for m in range(4):
                        inst = e.matmul(psap(tile, 0, nt, off=m * 128),
                                        lhsT=VNB[0:nt, (4 * hd + m) * 128:(4 * hd + m + 1) * 128],
                                        rhs=wmT[0:nt, hd, 0:nt], start=True, stop=True)
                    return inst
                S.add("pe", sfn, reads=DK + [("wmT",)], writes=[*PK(tile)])

                def sev(e, hd=hd, nt=nt, tok0=tok0, tile=tile):
                    inst = None
                    for m in range(4):
                        inst = e.tensor_tensor(out=big[:, 4 * hd + m, tok0:tok0 + nt], in0=psap(tile, 0, nt, off=m * 128),
                                               in1=bc[:, B_BSB + hd * 128:B_BSB + hd * 128 + nt], op=ALU.add)
                    return inst
                S.add("dve", sev, reads=[*PK(tile)], writes=[("big", 4 * hd + m) for m in range(4)])
        vstage(0)
        for tci in range(len(tchunks)):
            if tci + 1 < len(tchunks):
                vstage(tci + 1)
            sstage(tci)
        S.tag = (g, 'L2u')
        for pr in range(8):
            slu = use_chunk(gbase + cbase + 8 + 2 * pr)
            slz = use_chunk(gbase + cbase + 8 + 2 * pr + 1)
            for m2 in range(2):
                ct = 2 * pr + m2
                tU = next_tile(G)
                inproj(G, slu, m2, tU)
                tZ = next_tile(G)
                inproj(G, slz, m2, tZ)
                p = state["par"]
                state["par"] ^= 1
                blockwise("act", G, lambda e, bi, c0, n, p=p, tZ=tZ: e.activation(
                    out=t32[:, p, c0:c0 + n], in_=psap(tZ, bi, n), func=AF.Silu),
                    reads=[*PK(tZ)], writes=[("t32", p)])
                blockwise("dve", G, lambda e, bi, c0, n, p=p, tU=tU: e.tensor_tensor(
                    out=t32[:, p, c0:c0 + n], in0=psap(tU, bi, n), in1=t32[:, p, c0:c0 + n], op=ALU.mult),
                    reads=[*PK(tU), ("t32", p)], writes=[("t32", p)])
                S.add("pool", lambda e, p=p, ct=ct: e.tensor_tensor(out=ym[:, ct, 0:T], in0=t32[:, p, 0:T],
                                                                    in1=big[:, ct, 0:T], op=ALU.mult),
                      reads=[("t32", p), ("big", ct)], writes=[("ym", ct)])
        S.tag = (g, S.tag[1][:2] + 'o')
        outproj(G, gbase, cbase + 24)

    S.add("sp", lambda e: [e.dma_start(out=par[:, :], in_=params[:, :])], writes=[("par",)], dkey=("par",))
    S.add("sp", lambda e: [e.dma_start(out=bc[:, :], in_=bcin[:, :])], writes=[("bc",)], dkey=("bc",))
    S.add("sp", lambda e: [e.dma_start(out=sgw32[:, :, :], in_=sgwT[:, :, :])], writes=[("sgw32",)], dkey=("sgw32",))
    S.add("pool", lambda e: e.memset(ones_d[:, :], 1.0 / D), writes=[("ones",)])
    S.add("pool", lambda e: e.memset(ones_e[:, :], 1.0 / E), writes=[("ones",)])
    S.add("pool", lambda e: e.memset(epsb[:, 0:1], RMS_EPS), writes=[("eps",)])
    S.add("pool", lambda e: e.memset(epsb[:, 1:2], LN_EPS), writes=[("eps",)])
    S.add("pool", lambda e: e.memset(pool_hp[:, :, :, :], 0.0), writes=[("pool_hp", j, ct) for j in range(2) for ct in range(16)])
    S.add("pool", lambda e: e.memset(conv_hp[:, :, :], 0.0), writes=[("conv_hp", ct) for ct in range(16)])
    S.add("pool", lambda e: e.memset(seqb[:, :, :], 0.0), writes=[("seqb", q) for q in range(4)])
    S.add("pool", lambda e: e.memset(sA[:, :], 0.0), writes=[("sA",)])
    S.add("pool", lambda e: e.memset(sB[:, :], 0.0), writes=[("sB",)])
    S.add("pool", lambda e: e.memset(rs[:, :], 1.0), writes=[("rs",)])
    S.add("pool", lambda e: e.memset(big[:, :, :], 0.0), writes=[("big", ct) for ct in range(16)])
    for hd in range(4):
        S.add("dve", lambda e, hd=hd: e.tensor_tensor(out=wmT[:, hd, :], in0=sgw32[:, hd, :],
                                                      in1=bc[:, B_MASK:B_MASK + 128], op=ALU.mult),
              reads=[("sgw32",), ("bc",)], writes=[("wmT",)])
    S.add("dve", lambda e: e.tensor_copy(out=ident_bf[:, :], in_=bc[:, B_IDENT:B_IDENT + 128]),
          reads=[("bc",)], writes=[("ident",)])
    S.add("pool", lambda e: e.memset(agb[:, :, :], 0.0), writes=[("agb", 0), ("agb", 1)])
    CONST_KEYS = [("par",), ("bc",), ("ones",), ("eps",), ("ident",)]
    for eng in ("pe", "act", "dve", "pool"):
        S.add(eng, None, reads=CONST_KEYS, writes=[])

    def load_x(g):
        G = gstruct(g)
        for k in range(8):
            if G["xi"] == 0:
                S.add("sp", lambda e, g=g, k=k: [e.dma_start(out=x[:, k, 0:512], in_=xT[k, :, g * 512:(g + 1) * 512])],
                      writes=[("x", k)], dkey=("x", k))
            else:
                S.add("sp", lambda e, g=g, k=k: [e.dma_start(out=stg2d[:, k * 512:(k + 1) * 512], in_=xT[k, :, g * 512:(g + 1) * 512])],
                      reads=[("stg", 0), ("stg", 1)], writes=[("xo", k), ("stg", 0), ("stg", 1)], dkey=("x", k))
        if G["s"] is not None:
            s = G["s"]
            S.add("sp", lambda e, s=s: [e.dma_start(out=x[:, :, 512:544],
                                                      in_=xT[:, :, SEQ + s * 32:SEQ + (s + 1) * 32].rearrange("k p t -> p k t"))],
                  writes=[("x", k) for k in range(8)], dkey=("xs",))
            S.add("sp", lambda e, s=s: [e.dma_start(out=st_pool[:, :, :], in_=stpool[:, s, :, :].rearrange("j p f -> p j f"))],
                  writes=[("st_pool",)], dkey=("st_pool",))
            S.add("sp", lambda e, s=s: [e.dma_start(out=st_conv[:, :], in_=stconv[s])],
                  writes=[("st_conv",)], dkey=("st_conv",))

    loaded_x = set()
    early_norm = set()
    for g in range(ngroups):
        G = gstruct(g)
        T = G["T"]
        gbase = g * NCH
        if g not in loaded_x:
            load_x(g)
            loaded_x.add(g)
        S.tag = (g, 'L0')
        if depth >= 1:
            pool_layer(G, g, gbase, L0, 0, 0, skip_norm=(g in early_norm))
        S.tag = (g, 'L1')
        if depth >= 2:
            conv_layer(G, g, gbase, L1, 1)
        S.tag = (g, 'L2')
        if depth >= 3:
            sgu_layer(G, g, gbase, L2, 2)
        S.tag = (g, 'L3')
        if depth >= 4:
            hook = None
            if PINGPONG and g >= 1 and g + 1 < ngroups:
                load_x(g + 1)
                loaded_x.add(g + 1)

                def hook(gn=g + 1):
                    Gn = gstruct(gn)
                    Tn = Gn["T"]
                    rmsnorm(Gn, P_NORMG, lambda k, Tn=Tn: h[:, k, 0:Tn], lambda k: ("h", k))
                    state["fine"] = 2
                    early_norm.add(gn)
            pool_layer(G, g, gbase, L3, 1, 3, pre_out=hook)
        S.tag = (g, 'FIN')
        rmsnorm(G, P_FINALG, lambda k, T=T: big[:, k, 0:T], lambda k: ("big", k), sq=ym, sqk="ym")
        for k in range(8):
            S.add("sp", lambda e, g=g, k=k: [e.dma_start(out=yT[k, :, g * 512:(g + 1) * 512], in_=big[:, k, 0:512])],
                  reads=[("big", k)], writes=[("yT", g, k)], dkey=("yout", k))
        if G["s"] is not None:
            s = G["s"]
            S.add("sp", lambda e, s=s: [e.dma_start(out=yT[:, :, SEQ + s * 32:SEQ + (s + 1) * 32].rearrange("k p t -> p k t"),
                                                      in_=big[:, 0:8, 512:544])],
                  reads=[("big", k) for k in range(8)], writes=[("yTs", s)], dkey=("youts",))
    S.add("sp", lambda e: [e.dma_start(out=o_poolp[:, :, :].rearrange("j p f -> p j f"),
                                       in_=pool_hp[:, :, :, :].rearrange("p j c r -> p j (c r)"))],
          reads=[("pool_hp", j, ct) for j in range(2) for ct in range(16)], writes=[("o_poolp",)], dkey=("ohp",))
    S.add("sp", lambda e: [e.dma_start(out=o_convp[:, :], in_=conv_hp[:, :, :].rearrange("p c r -> p (c r)"))],
          reads=[("conv_hp", ct) for ct in range(16)], writes=[("o_convp",)], dkey=("ohp",))

    ops = S.ops
    for op in ops:
        for d in op.deps:
            if d.dkey is None and not (d.eng == "pe" and op.eng == "pe"):
                d.mark = True
    cnt = {}
    for op in ops:
        if op.mark:
            cnt[op.eng] = cnt.get(op.eng, 0) + 1
            op.markno = cnt[op.eng]

    engs = ["pe", "act", "dve", "pool", "sp"]
    esem = {en: es.enter_context(nc.semaphore("sem_" + en)) for en in engs}
    dsem = {}
    for k in S.dcount:
        dsem[k] = es.enter_context(nc.semaphore("dsem%d" % len(dsem)))

    blk = es.enter_context(nc.Block())

    def emit(engname, handle):
        waited = {}
        for op in ops:
            if op.eng != engname:
                continue
            for d in sorted(op.deps, key=lambda o: o.idx):
                if d.dkey is not None:
                    sem, val = dsem[d.dkey], d.dval
                else:
                    if d.eng == "pe" and engname == "pe":
                        continue
                    if d.fn is None:
                        continue
                    sem, val = esem[d.eng], d.markno
                key = id(sem)
                if waited.get(key, 0) >= val:
                    continue
                waited[key] = val
                handle.wait_ge(sem, val)
            if op.fn is None:
                continue
            n0 = nc.n_instructions()
            r = op.fn(handle)
            DBG_TAGS.setdefault(engname, []).append((op.tag, nc.n_instructions() - n0))
            if op.dkey is not None:
                for inst in r:
                    inst.then_inc(dsem[op.dkey], 16)
            elif op.mark:
                r.then_inc(esem[engname], 1)
        if engname == "sp":
            for k, n in S.dcount.items():
                handle.wait_ge(dsem[k], 16 * n)

    @blk.tensor
    def _(e):
        emit("pe", e)

    @blk.scalar
    def _(e):
        emit("act", e)

    @blk.vector
    def _(e):
        emit("dve", e)

    @blk.gpsimd
    def _(e):
        emit("pool", e)

    @blk.sync
    def _(e):
        emit("sp", e)

    es.close()
    return nc


def _in_chunk(Wm, col0):
    blk = Wm[:, col0:col0 + 256].reshape(8, 128, 256).transpose(1, 0, 2)
    return np.ascontiguousarray(blk).reshape(128, 2048)


def _out_chunk(Wm, m):
    blk = Wm[:, m * 128:(m + 1) * 128].reshape(16, 128, 128).transpose(1, 0, 2)
    return np.ascontiguousarray(blk).reshape(128, 2048)


def _grp_chunk(Wg):
    blk = Wg.reshape(4, 128, 512).transpose(1, 0, 2)
    return np.ascontiguousarray(blk).reshape(128, 2048)


def _pack_weights(pool_in_w, pool_w, pool_out_w, conv_in_w, conv_out_w, sgu_in_w, sgu_out_w):
    ch = []

    def pool(j):
        for gq in range(4):
            ch.append(_in_chunk(pool_in_w[j], 512 * gq))
            ch.append(_in_chunk(pool_in_w[j], 512 * gq + 256))
            ch.append(_in_chunk(pool_in_w[j], E + 512 * gq))
            ch.append(_in_chunk(pool_in_w[j], E + 512 * gq + 256))
            ch.append(_grp_chunk(pool_w[j, gq]))
        for m in range(8):
            ch.append(_out_chunk(pool_out_w[j], m))
    pool(0)
    for pr in range(8):
        ch.append(_in_chunk(conv_in_w[0], 256 * pr))
        ch.append(_in_chunk(conv_in_w[0], E + 256 * pr))
    for pr in range(8):
        ch.append(_in_chunk(conv_in_w[0], 2 * E + 256 * pr))
    for m in range(8):
        ch.append(_out_chunk(conv_out_w[0], m))
    for i in range(8):
        ch.append(_in_chunk(sgu_in_w[0], E + 256 * i))
    for pr in range(8):
        ch.append(_in_chunk(sgu_in_w[0], 256 * pr))
        ch.append(_in_chunk(sgu_in_w[0], 2 * E + 256 * pr))
    for m in range(8):
        ch.append(_out_chunk(sgu_out_w[0], m))
    pool(1)
    assert len(ch) == NCH
    return np.stack(ch).astype(np.float32)


def _colvec(v, nt):
    return np.asarray(v, np.float32).reshape(nt, 128).T


_PROG = {}
DBG_TAGS = {}


def kernel(x_prompt, x_sample, state_pool, state_conv, norm_g, pool_in_w, pool_w, pool_scale, pool_out_w,
           conv_in_w, conv_w, conv_b, conv_norm_g, conv_norm_b, conv_out_w,
           sgu_in_w, sgu_norm_g, sgu_norm_b, sgu_w, sgu_b, sgu_out_w, final_g, _depth=4, _ngroups=NG):
    f = lambda a: np.asarray(a, np.float32)
    x_prompt, x_sample, state_pool, state_conv = f(x_prompt), f(x_sample), f(state_pool), f(state_conv)
    wst = _pack_weights(f(pool_in_w), f(pool_w), f(pool_out_w), f(conv_in_w), f(conv_out_w), f(sgu_in_w), f(sgu_out_w))
    par = np.zeros((128, NPAR), np.float32)
    for l in range(4):
        par[:, P_NORMG + 8 * l:P_NORMG + 8 * l + 8] = _colvec(f(norm_g)[l], 8)
    par[:, P_FINALG:P_FINALG + 8] = _colvec(f(final_g), 8)
    for j in range(2):
        par[:, P_PSCALE + 16 * j:P_PSCALE + 16 * j + 16] = _colvec(f(pool_scale)[j], 16)
    cw = f(conv_w)[0]
    par[:, P_CONVW:P_CONVW + 496] = cw.T.reshape(16, 128, 31).transpose(1, 0, 2).reshape(128, 496)
    par[:, P_CONVB:P_CONVB + 16] = _colvec(f(conv_b)[0], 16)
    par[:, P_CONVNG:P_CONVNG + 16] = _colvec(f(conv_norm_g)[0], 16)
    par[:, P_CONVNB:P_CONVNB + 16] = _colvec(f(conv_norm_b)[0], 16)
    bcin = np.zeros((128, NBC), np.float32)
    bcin[:, B_SGG:B_SGG + 2048] = f(sgu_norm_g)[0][None, :]
    bcin[:, B_SGB:B_SGB + 2048] = f(sgu_norm_b)[0][None, :]
    bcin[:, B_BSB:B_BSB + 512] = f(sgu_b)[0].reshape(1, 512)
    jj = np.arange(128)
    bcin[:, B_MASK:B_MASK + 128] = (jj[:, None] <= jj[None, :]).astype(np.float32)
    for gq, w in enumerate(POOL_WINDOWS):
        t = np.arange(16)
        bcin[:, B_RATIO + 16 * gq:B_RATIO + 16 * gq + 16] = (w / np.minimum(t + 1, w)).astype(np.float32)[None, :]
    bcin[:, B_IDENT:B_IDENT + 128] = np.eye(128, dtype=np.float32)
    sgwT = np.ascontiguousarray(f(sgu_w)[0].transpose(2, 0, 1))

    in_maps = []
    for c in range(NCORE):
        xt = np.empty((8, 128, SEQ + NSAMP * SLEN), np.float32)
        xt[:, :, :SEQ] = x_prompt[c].T.reshape(8, 128, SEQ)
        xs = x_sample[NSAMP * c:NSAMP * (c + 1)].reshape(NSAMP * SLEN, D)
        xt[:, :, SEQ:] = xs.T.reshape(8, 128, NSAMP * SLEN)
        sp = state_pool[:, NSAMP * c:NSAMP * (c + 1)]
        sp = sp.transpose(0, 1, 3, 2).reshape(2, NSAMP, 16, 128, 15).transpose(0, 1, 3, 2, 4).reshape(2, NSAMP, 128, 240)
        sc = state_conv[0, NSAMP * c:NSAMP * (c + 1)]
        sc = sc.transpose(0, 2, 1).reshape(NSAMP, 16, 128, 30).transpose(0, 2, 1, 3).reshape(NSAMP, 128, 480)
        in_maps.append({"xT": xt, "wst": wst, "params": par, "bcin": bcin, "sgwT": sgwT,
                        "stpool": np.ascontiguousarray(sp), "stconv": np.ascontiguousarray(sc)})
    key = (_depth, _ngroups)
    if key not in _PROG:
        _PROG[key] = build_program(_depth, _ngroups)
    res = run_bass_kernel_spmd(_PROG[key], in_maps, core_ids=list(range(NCORE)))
    R = res.results

    y_prompt = np.empty((8, SEQ, D), np.float32)
    y_sample = np.empty((32, SLEN, D), np.float32)
    npp = np.empty((2, 8, 15, E), np.float32)
    nps = np.empty((2, 32, 15, E), np.float32)
    ncp = np.empty((1, 8, 30, E), np.float32)
    ncs = np.empty((1, 32, 30, E), np.float32)
    nsp = np.empty((1, 8, 128, E), np.float32)
    nss = np.empty((1, 32, SLEN, E), np.float32)
    for c in range(NCORE):
        r = R[c]
        yt = np.asarray(r["yT"]).reshape(D, SEQ + NSAMP * SLEN)
        y_prompt[c] = yt[:, :SEQ].T
        y_sample[NSAMP * c:NSAMP * (c + 1)] = yt[:, SEQ:].T.reshape(NSAMP, SLEN, D)
        pp = np.asarray(r["o_poolp"]).reshape(2, 128, 16, 15)
        npp[:, c] = pp.transpose(0, 3, 2, 1).reshape(2, 15, E)
        pq = np.asarray(r["o_pools"]).reshape(2, NSAMP, 128, 16, 15)
        nps[:, NSAMP * c:NSAMP * (c + 1)] = pq.transpose(0, 1, 4, 3, 2).reshape(2, NSAMP, 15, E)
        cp = np.asarray(r["o_convp"]).reshape(128, 16, 30)
        ncp[0, c] = cp.transpose(2, 1, 0).reshape(30, E)
        cq = np.asarray(r["o_convs"]).reshape(NSAMP, 128, 16, 30)
        ncs[0, NSAMP * c:NSAMP * (c + 1)] = cq.transpose(0, 3, 2, 1).reshape(NSAMP, 30, E)
        nsp[0, c] = np.asarray(r["o_sgup"]).reshape(128, E)
        nss[0, NSAMP * c:NSAMP * (c + 1)] = np.asarray(r["o_sgus"]).reshape(NSAMP, SLEN, E)
    return (y_prompt, y_sample, npp, nps, ncp, ncs, nsp, nss)
```
